# Optimizing a Trainium2 kernel written in Bass

```python
import math
import jax, jax.numpy as jnp
from jax import lax
import numpy as np

D_MODEL = 1024
BATCH = 8
SEQ = 8192
DEPTH = 4

GRID_W = 64
CTX_LEN = 256
N_MIXERS = 3
N_A = (DEPTH + 2) // 3
N_B = (DEPTH + 1) // 3
N_C = DEPTH // 3
N_MOD = 6

SSD_EXPAND = 2
D_INNER = SSD_EXPAND * D_MODEL
SSD_HEADDIM = 64
SSD_HEADS = D_INNER // SSD_HEADDIM
SSD_GROUPS = 4
SSD_HPG = SSD_HEADS // SSD_GROUPS
SSD_STATE = 128
SSD_CONV_W = 5
SSD_CHUNK = 128
SSD_CONV_DIM = D_INNER + 2 * SSD_GROUPS * SSD_STATE
SSD_IN_DIM = D_INNER + SSD_CONV_DIM + 2 * SSD_HEADS

ATTN_HEAD_DIM = 64
ATTN_Q_HEADS = D_MODEL // ATTN_HEAD_DIM
ATTN_KV_HEADS = 4
ATTN_GROUP = ATTN_Q_HEADS // ATTN_KV_HEADS
ATTN_WINDOW = 128
ATTN_BLOCK = 128
ATTN_BAND = ATTN_BLOCK + 2 * ATTN_WINDOW
ATTN_QKV_DIM = (ATTN_Q_HEADS + 2 * ATTN_KV_HEADS) * ATTN_HEAD_DIM
ROPE_BASE = 10000.0
ROPE_FREQS = ATTN_HEAD_DIM // 4

GMLP_WIDTH = 2 * D_MODEL
GMLP_GROUPS = 8
GMLP_GROUP_DIM = GMLP_WIDTH // GMLP_GROUPS
GMLP_CHUNK = 128

FFN_HIDDEN = -(-8 * D_MODEL // (3 * 256)) * 256

kernel_name = "hybrid_ssd_swa_gmlp_diffusion_block"


def rms_norm(x, g, eps=1e-6):
    xf = x.astype(jnp.float32)
    y = xf * lax.rsqrt(jnp.mean(xf * xf, axis=-1, keepdims=True) + eps)
    return (y * g.astype(jnp.float32)).astype(x.dtype)


def layer_norm(x, g, b, eps=1e-5):
    xf = x.astype(jnp.float32)
    mu = jnp.mean(xf, axis=-1, keepdims=True)
    xc = xf - mu
    var = jnp.mean(xc * xc, axis=-1, keepdims=True)
    return (xc * lax.rsqrt(var + eps) * g.astype(jnp.float32) + b.astype(jnp.float32)).astype(x.dtype)


def modulate(h, shift, scale):
    return h * (1 + scale) + shift


def swiglu(h, w_in, w_out):
    gu = h @ w_in
    return (jax.nn.silu(gu[..., :FFN_HIDDEN]) * gu[..., FFN_HIDDEN:]) @ w_out


def axial_rope_tables(row, col, dtype):
    inv = ROPE_BASE ** (-jnp.arange(ROPE_FREQS, dtype=jnp.float32) / ROPE_FREQS)
    ang = jnp.stack([row.astype(jnp.float32)[:, None] * inv, col.astype(jnp.float32)[:, None] * inv], axis=1)
    ang = jnp.repeat(ang[:, :, None, :], 2, axis=2).reshape(row.shape[0], ATTN_HEAD_DIM)
    return jnp.cos(ang).astype(dtype), jnp.sin(ang).astype(dtype)


def apply_axial_rope(x, cos, sin):
    xs = x.reshape(x.shape[:-1] + (2, 2, ROPE_FREQS))
    rot = jnp.stack([-xs[..., 1, :], xs[..., 0, :]], axis=-2).reshape(x.shape)
    return x * cos[:, None, :] + rot * sin[:, None, :]


def depthwise_conv(u, w, b):
    ch = u.shape[-1]
    y = lax.conv_general_dilated(u, w[:, None, :].astype(u.dtype), window_strides=(1,),
                                 padding=[(SSD_CONV_W // 2, SSD_CONV_W // 2)],
                                 dimension_numbers=('NWC', 'WIO', 'NWC'), feature_group_count=ch)
    return y + b


def ssd_scan(xs, dt, A, Bm, Cm, state0):
    f32 = jnp.float32
    xs, dt, Bm, Cm = xs.astype(f32), dt.astype(f32), Bm.astype(f32), Cm.astype(f32)
    b, l = xs.shape[:2]
    nc = l // SSD_CHUNK

    def chunks(t):
        return jnp.moveaxis(t.reshape((b, nc, SSD_CHUNK) + t.shape[2:]), 1, 0)

    mask = jnp.tril(jnp.ones((SSD_CHUNK, SSD_CHUNK), dtype=bool))[None, :, :, None, None]

    def step(state, inp):
        xc, dtc, bc, cc = inp
        a_cum = jnp.cumsum(dtc * A, axis=1)
        seg = a_cum[:, :, None] - a_cum[:, None]
        decay = jnp.exp(jnp.where(mask, seg, -jnp.inf))
        xdt = xc * dtc[..., None]
        cb = jnp.einsum('blgn,bsgn->blsg', cc, bc)
        y = jnp.einsum('blsg,blsgr,bsgrp->blgrp', cb, decay, xdt)
        y = y + jnp.einsum('blgn,bgrpn->blgrp', cc, state) * jnp.exp(a_cum)[..., None]
        w_end = jnp.exp(a_cum[:, -1:] - a_cum)
        state = state * jnp.exp(a_cum[:, -1])[..., None, None] + jnp.einsum('bsgn,bsgr,bsgrp->bgrpn', bc, w_end, xdt)
        return state, y

    final, ys = lax.scan(step, state0.astype(f32), (chunks(xs), chunks(dt), chunks(Bm), chunks(Cm)))
    return jnp.moveaxis(ys, 0, 1).reshape(xs.shape), final


def ssd_mixer(h_ctx, h_lat, w_in, conv_w, conv_b, a_log, dt_bias, d_skip, norm_w, w_out, need_ctx_out):
    A = -jnp.exp(a_log.astype(jnp.float32)).reshape(2, SSD_GROUPS, SSD_HPG)
    d = d_skip.astype(jnp.float32).reshape(SSD_GROUPS, SSD_HPG)[..., None]
    flip = lambda t: jnp.flip(t, axis=1)

    def branch(h):
        b, l, _ = h.shape
        zxbcdt = h @ w_in
        z = zxbcdt[..., :D_INNER]
        xbc = jax.nn.silu(depthwise_conv(zxbcdt[..., D_INNER:D_INNER + SSD_CONV_DIM], conv_w, conv_b))
        dt_raw = zxbcdt[..., D_INNER + SSD_CONV_DIM:].astype(jnp.float32).reshape(b, l, 2, SSD_HEADS)
        dt = jax.nn.softplus(dt_raw + dt_bias.astype(jnp.float32)).reshape(b, l, 2, SSD_GROUPS, SSD_HPG)
        xs = xbc[..., :D_INNER].reshape(b, l, SSD_GROUPS, SSD_HPG, SSD_HEADDIM)
        gn = SSD_GROUPS * SSD_STATE
        Bm = xbc[..., D_INNER:D_INNER + gn].reshape(b, l, SSD_GROUPS, SSD_STATE)
        Cm = xbc[..., D_INNER + gn:].reshape(b, l, SSD_GROUPS, SSD_STATE)
        return z, xs, Bm, Cm, dt

    def run(parts, s_f, s_b):
        z, xs, Bm, Cm, dt = parts
        y_f, fin_f = ssd_scan(xs, dt[:, :, 0], A[0], Bm, Cm, s_f)
        y_b, fin_b = ssd_scan(flip(xs), flip(dt[:, :, 1]), A[1], flip(Bm), flip(Cm), s_b)
        y = y_f + flip(y_b) + xs.astype(jnp.float32) * d
        return y, fin_f, fin_b

    def finish(y, z, dtype):
        b, l = z.shape[:2]
        g = y.reshape(b, l, D_INNER) * jax.nn.silu(z.astype(jnp.float32))
        return rms_norm(g, norm_w).astype(dtype) @ w_out

    ctx_parts = branch(h_ctx)
    zero = jnp.zeros((h_ctx.shape[0], SSD_GROUPS, SSD_HPG, SSD_HEADDIM, SSD_STATE), jnp.float32)
    y_c, s_f, s_b = run(ctx_parts, zero, zero)
    lat_parts = branch(h_lat)
    y_l, _, _ = run(lat_parts, s_f, s_b)
    o_lat = finish(y_l, lat_parts[0], h_lat.dtype)
    o_ctx = finish(y_c, ctx_parts[0], h_ctx.dtype) if need_ctx_out else None
    return o_ctx, o_lat


def window_attn_mixer(h_ctx, h_lat, w_qkv, sink, w_o, cos, sin, need_ctx_out):
    scale = ATTN_HEAD_DIM ** -0.5
    qd, kd = ATTN_Q_HEADS * ATTN_HEAD_DIM, ATTN_KV_HEADS * ATTN_HEAD_DIM
    sink_g = sink.astype(jnp.float32).reshape(1, ATTN_KV_HEADS, ATTN_GROUP, 1, 1)

    def proj(h):
        b, l, _ = h.shape
        qkv = h @ w_qkv
        q = qkv[..., :qd].reshape(b, l, ATTN_Q_HEADS, ATTN_HEAD_DIM)
        k = qkv[..., qd:qd + kd].reshape(b, l, ATTN_KV_HEADS, ATTN_HEAD_DIM)
        v = qkv[..., qd + kd:].reshape(b, l, ATTN_KV_HEADS, ATTN_HEAD_DIM)
        return q, k, v

    def group(q):
        return q.reshape(q.shape[:2] + (ATTN_KV_HEADS, ATTN_GROUP, ATTN_HEAD_DIM))

    q_c, k_c, v_c = proj(h_ctx)
    n_ctx = k_c.shape[1]

    q_l, k_l, v_l = proj(h_lat)
    q_l = group(apply_axial_rope(q_l, cos, sin))
    k_l = apply_axial_rope(k_l, cos, sin)
    b, n = h_lat.shape[:2]
    nb = n // ATTN_BLOCK
    pad = ((0, 0), (ATTN_WINDOW, ATTN_WINDOW), (0, 0), (0, 0))
    k_pad, v_pad = jnp.pad(k_l, pad), jnp.pad(v_l, pad)
    q_blocks = jnp.moveaxis(q_l.reshape((b, nb, ATTN_BLOCK) + q_l.shape[2:]), 1, 0)
    t_idx = jnp.arange(ATTN_BLOCK)[:, None]
    u_idx = jnp.arange(ATTN_BAND)[None, :]

    def attend_block(args):
        j, qb = args
        start = j * ATTN_BLOCK
        kb = lax.dynamic_slice_in_dim(k_pad, start, ATTN_BAND, axis=1)
        vb = lax.dynamic_slice_in_dim(v_pad, start, ATTN_BAND, axis=1)
        pos_k = start - ATTN_WINDOW + u_idx
        valid = (u_idx >= t_idx) & (u_idx <= t_idx + 2 * ATTN_WINDOW) & (pos_k >= 0) & (pos_k < n)
        s_band = jnp.einsum('bqhgd,bkhd->bhgqk', qb, kb).astype(jnp.float32) * scale
        s_band = jnp.where(valid, s_band, -jnp.inf)
        s_ctx = jnp.einsum('bqhgd,bkhd->bhgqk', qb, k_c).astype(jnp.float32) * scale
        sinks = jnp.broadcast_to(sink_g, s_ctx.shape[:-1] + (1,))
        p = jax.nn.softmax(jnp.concatenate([sinks, s_ctx, s_band], axis=-1), axis=-1).astype(v_l.dtype)
        o = jnp.einsum('bhgqk,bkhd->bqhgd', p[..., 1:1 + n_ctx], v_c)
        return o + jnp.einsum('bhgqk,bkhd->bqhgd', p[..., 1 + n_ctx:], vb)

    o_blocks = lax.map(attend_block, (jnp.arange(nb), q_blocks))
    o_lat = jnp.moveaxis(o_blocks, 0, 1).reshape(b, n, qd) @ w_o

    o_ctx = None
    if need_ctx_out:
        qg = group(q_c)
        s = jnp.einsum('bqhgd,bkhd->bhgqk', qg, k_c).astype(jnp.float32) * scale
        sinks = jnp.broadcast_to(sink_g, s.shape[:-1] + (1,))
        p = jax.nn.softmax(jnp.concatenate([sinks, s], axis=-1), axis=-1)[..., 1:].astype(v_c.dtype)
        o_ctx = jnp.einsum('bhgqk,bkhd->bqhgd', p, v_c).reshape(h_ctx.shape[0], n_ctx, qd) @ w_o
    return o_ctx, o_lat


def gmlp_mixer(h_ctx, h_lat, w_in, ln_g, ln_b, w_s, b_s, w_out, need_ctx_out):
    def mix(h):
        b, l, _ = h.shape
        zz = jax.nn.gelu(h @ w_in, approximate=False)
        u, v = zz[..., :GMLP_WIDTH], zz[..., GMLP_WIDTH:]
        v = layer_norm(v, ln_g, ln_b)
        vb = v.reshape(b, l // GMLP_CHUNK, GMLP_CHUNK, GMLP_GROUPS, GMLP_GROUP_DIM)
        sv = jnp.einsum('gts,bcsgd->bctgd', w_s, vb) + b_s.T[None, None, :, :, None]
        return (u * sv.reshape(b, l, GMLP_WIDTH)) @ w_out
    o_lat = mix(h_lat)
    o_ctx = mix(h_ctx) if need_ctx_out else None
    return o_ctx, o_lat


def setup_inputs(seed: int = 0) -> dict:
    key = jax.random.key(seed)
    ks = jax.random.split(key, 32)
    nrm = lambda k, s, sc: jax.random.normal(k, s, jnp.float32) * sc
    D = D_MODEL
    dt0 = jnp.exp(jax.random.uniform(ks[10], (N_A, 2, SSD_HEADS), jnp.float32)
                  * (math.log(0.1) - math.log(0.001)) + math.log(0.001))
    return {
        "x": nrm(ks[0], (BATCH, SEQ, D), 1.0),
        "c": nrm(ks[1], (BATCH, D), 1.0),
        "ctx": nrm(ks[2], (BATCH, CTX_LEN, D), 1.0),
        "c_ctx": nrm(ks[3], (D,), 1.0),
        "w_mod": nrm(ks[4], (DEPTH, D, N_MOD * D), 0.5 * D ** -0.5),
        "b_mod": nrm(ks[5], (DEPTH, N_MOD * D), 0.02),
        "norm_g": 1.0 + nrm(ks[6], (DEPTH, 2, D), 0.02),
        "final_g": 1.0 + nrm(ks[7], (D,), 0.02),
        "ssd_w_in": nrm(ks[8], (N_A, D, SSD_IN_DIM), D ** -0.5),
        "ssd_conv_w": nrm(ks[9], (N_A, SSD_CONV_W, SSD_CONV_DIM), SSD_CONV_W ** -0.5),
        "ssd_conv_b": nrm(ks[11], (N_A, SSD_CONV_DIM), 0.02),
        "ssd_a_log": jnp.log(jax.random.uniform(ks[12], (N_A, 2, SSD_HEADS), jnp.float32, 1.0, 16.0)),
        "ssd_dt_bias": dt0 + jnp.log(-jnp.expm1(-dt0)),
        "ssd_d": 1.0 + nrm(ks[13], (N_A, SSD_HEADS), 0.1),
        "ssd_norm_w": 1.0 + nrm(ks[14], (N_A, D_INNER), 0.02),
        "ssd_w_out": nrm(ks[15], (N_A, D_INNER, D), D_INNER ** -0.5),
        "attn_w_qkv": nrm(ks[16], (N_B, D, ATTN_QKV_DIM), D ** -0.5),
        "attn_sink": nrm(ks[17], (N_B, ATTN_Q_HEADS), 1.0),
        "attn_w_o": nrm(ks[18], (N_B, ATTN_Q_HEADS * ATTN_HEAD_DIM, D), (ATTN_Q_HEADS * ATTN_HEAD_DIM) ** -0.5),
        "gmlp_w_in": nrm(ks[19], (N_C, D, 2 * GMLP_WIDTH), D ** -0.5),
        "gmlp_ln_g": 1.0 + nrm(ks[20], (N_C, GMLP_WIDTH), 0.02),
        "gmlp_ln_b": nrm(ks[21], (N_C, GMLP_WIDTH), 0.02),
        "gmlp_w_s": nrm(ks[22], (N_C, GMLP_GROUPS, GMLP_CHUNK, GMLP_CHUNK), GMLP_CHUNK ** -0.5),
        "gmlp_b_s": 1.0 + nrm(ks[23], (N_C, GMLP_GROUPS, GMLP_CHUNK), 0.02),
        "gmlp_w_out": nrm(ks[24], (N_C, GMLP_WIDTH, D), GMLP_WIDTH ** -0.5),
        "ffn_w_in": nrm(ks[25], (DEPTH, D, 2 * FFN_HIDDEN), D ** -0.5),
        "ffn_w_out": nrm(ks[26], (DEPTH, FFN_HIDDEN, D), FFN_HIDDEN ** -0.5),
    }


def reference(x, c, ctx, c_ctx, w_mod, b_mod, norm_g, final_g,
              ssd_w_in, ssd_conv_w, ssd_conv_b, ssd_a_log, ssd_dt_bias, ssd_d, ssd_norm_w, ssd_w_out,
              attn_w_qkv, attn_sink, attn_w_o,
              gmlp_w_in, gmlp_ln_g, gmlp_ln_b, gmlp_w_s, gmlp_b_s, gmlp_w_out,
              ffn_w_in, ffn_w_out):
    n = x.shape[1]
    rows = n // GRID_W
    row = jnp.repeat(jnp.arange(rows, dtype=jnp.int32), GRID_W, total_repeat_length=n)
    col = jnp.tile(jnp.arange(GRID_W, dtype=jnp.int32), rows)
    cos, sin = axial_rope_tables(row, col, x.dtype)

    h_lat, h_ctx = x, ctx
    for i in range(DEPTH):
        last = i == DEPTH - 1
        kind, j = i % N_MIXERS, i // N_MIXERS
        m_lat = jax.nn.silu(c) @ w_mod[i] + b_mod[i]
        sh1, sc1, g1, sh2, sc2, g2 = jnp.split(m_lat[:, None, :], N_MOD, axis=-1)
        m_ctx = jax.nn.silu(c_ctx) @ w_mod[i] + b_mod[i]
        csh1, csc1, cg1, csh2, csc2, cg2 = jnp.split(m_ctx[None, None, :], N_MOD, axis=-1)

        a_lat = modulate(rms_norm(h_lat, norm_g[i, 0]), sh1, sc1)
        need_ctx_out = not last
        if kind == 0:
            a_ctx = modulate(rms_norm(h_ctx, norm_g[i, 0]), csh1, csc1)
            o_ctx, o_lat = ssd_mixer(a_ctx, a_lat, ssd_w_in[j], ssd_conv_w[j], ssd_conv_b[j], ssd_a_log[j],
                                     ssd_dt_bias[j], ssd_d[j], ssd_norm_w[j], ssd_w_out[j], need_ctx_out)
        elif kind == 1:
            a_ctx = modulate(rms_norm(h_ctx, norm_g[i, 0]), csh1, csc1)
            o_ctx, o_lat = window_attn_mixer(a_ctx, a_lat, attn_w_qkv[j], attn_sink[j], attn_w_o[j],
                                             cos, sin, need_ctx_out)
        else:
            a_ctx = modulate(rms_norm(h_ctx, norm_g[i, 0]), csh1, csc1) if need_ctx_out else None
            o_ctx, o_lat = gmlp_mixer(a_ctx, a_lat, gmlp_w_in[j], gmlp_ln_g[j], gmlp_ln_b[j], gmlp_w_s[j],
                                      gmlp_b_s[j], gmlp_w_out[j], need_ctx_out)

        h_lat = h_lat + g1 * o_lat
        h_lat = h_lat + g2 * swiglu(modulate(rms_norm(h_lat, norm_g[i, 1]), sh2, sc2), ffn_w_in[i], ffn_w_out[i])
        if need_ctx_out:
            h_ctx = h_ctx + cg1 * o_ctx
            h_ctx = h_ctx + cg2 * swiglu(modulate(rms_norm(h_ctx, norm_g[i, 1]), csh2, csc2),
                                         ffn_w_in[i], ffn_w_out[i])
    return rms_norm(h_lat, final_g)
```

```python
import numpy as np
from contextlib import ExitStack
import concourse.bass as bass
import concourse.mybir as mybir
from concourse.bass_utils import run_bass_kernel_spmd

F32 = mybir.dt.float32
BF16 = mybir.dt.bfloat16
AF = mybir.ActivationFunctionType
ALU = mybir.AluOpType

D = 1024
KC = 8
TCTX = 256
TLAT = 8192
T = TCTX + TLAT
DEPTH = 4
FFN_H = 2816
FJ = FFN_H // 128
SBUF_BASE = 16512
SBUF_LIMIT = 229344

SELF_WAIT = True


class Prog:
    CE = ["pe", "act", "dve", "pool"]

    def __init__(self, nc, es):
        self.nc = nc
        self.es = es
        self.ops = {e: [] for e in self.CE + ["sp"]}
        self.cnt = {e: 0 for e in self.CE}
        self.sems = {}
        for e in self.CE:
            self.sems["c_" + e] = es.enter_context(nc.semaphore("c_" + e))
        self.dcnt = {}
        self.known = {e: {} for e in self.CE + ["sp"]}
        self.last_w = {}
        self.readers = {}

    def _deps(self, eng, reads, writes):
        ev = {}

        def need(s, v):
            if ev.get(s, 0) < v:
                ev[s] = v

        for k in reads:
            e = self.last_w.get(k)
            if e is not None:
                need(*e)
        for k in writes:
            e = self.last_w.get(k)
            if e is not None:
                need(*e)
            for s, v in self.readers.get(k, {}).items():
                need(s, v)
        waits = []
        for s, v in ev.items():
            if (not SELF_WAIT) and s == "c_" + eng:
                continue
            if self.known[eng].get(s, 0) < v:
                self.known[eng][s] = v
                waits.append((s, v))
        return waits

    def _commit(self, event, reads, writes):
        s, v = event
        for k in reads:
            r = self.readers.setdefault(k, {})
            if r.get(s, 0) < v:
                r[s] = v
        for k in writes:
            self.last_w[k] = event
            self.readers[k] = {}

    def op(self, eng, calls, reads=(), writes=()):
        if isinstance(calls, tuple):
            calls = [calls]
        waits = self._deps(eng, reads, writes)
        self.cnt[eng] += 1
        event = ("c_" + eng, self.cnt[eng])
        self.ops[eng].append((waits, calls, "c", event[0]))
        self._commit(event, reads, writes)

    def dma(self, q, semkey, calls, reads=(), writes=()):
        if isinstance(calls, tuple):
            calls = [calls]
        name = "d_" + semkey
        if name not in self.sems:
            self.sems[name] = self.es.enter_context(self.nc.semaphore(name))
            self.dcnt[name] = 0
        waits = self._deps(q, reads, writes)
        self.dcnt[name] += 16 * len(calls)
        event = (name, self.dcnt[name])
        self.ops[q].append((waits, calls, "d", name))
        self._commit(event, reads, writes)

    def barrier(self):
        for e in self.CE + ["sp"]:
            waits = []
            allv = [("c_" + x, self.cnt[x]) for x in self.CE] + list(self.dcnt.items())
            for s, v in allv:
                if v > 0 and self.known[e].get(s, 0) < v:
                    if s == "c_" + e and not SELF_WAIT:
                        continue
                    self.known[e][s] = v
                    waits.append((s, v))
            if waits:
                self.ops[e].append((waits, [], "n", None))

    def emit(self):
        nc = self.nc
        sems = self.sems

        def run(eo, name):
            for waits, calls, kind, sname in self.ops[name]:
                for s, v in waits:
                    eo.wait_ge(sems[s], v)
                n = len(calls)
                for i, (m, kw) in enumerate(calls):
                    ins = getattr(eo, m)(**kw)
                    if kind == "d":
                        ins.then_inc(sems[sname], 16)
                    elif kind == "c" and i == n - 1:
                        ins.then_inc(sems[sname], 1)

        with nc.Block() as block:

            @block.tensor
            def _(e):
                run(e, "pe")

            @block.scalar
            def _(e):
                run(e, "act")

            @block.vector
            def _(e):
                run(e, "dve")

            @block.gpsimd
            def _(e):
                run(e, "pool")

            @block.sync
            def _(e):
                run(e, "sp")


class Arena:
    def __init__(self, nc, base, limit=SBUF_LIMIT):
        self.nc = nc
        self.base = base
        self.top = base
        self.limit = limit
        self.n = 0

    def reset(self):
        self.top = self.base

    def alloc(self, name, shape, dtype):
        esz = 4 if dtype == F32 else 2
        nbytes = int(np.prod(shape[1:])) * esz
        nbytes = (nbytes + 63) // 64 * 64
        off = self.top
        assert off + nbytes <= self.limit, f"SBUF overflow allocating {name}: {off}+{nbytes}"
        self.top += nbytes
        self.n += 1
        return self.nc.alloc_sbuf_tensor_at(f"{name}_{self.n}", list(shape), dtype, offset=off)


def make_consts():
    c = {}
    c["ident"] = np.eye(128, dtype=np.float32)
    c["ones"] = np.ones((128, 128), dtype=np.float32)
    k = np.arange(128)[:, None]
    q = np.arange(128)[None, :]
    c["mask_prev"] = np.where(k >= q, 0.0, -30000.0).astype(np.float32)
    c["mask_next"] = np.where(k <= q, 0.0, -30000.0).astype(np.float32)
    kk = np.arange(128)[:, None]
    ll = np.arange(128)[None, :]
    c["tri_f"] = (kk <= ll).astype(np.float32)
    c["tri_b"] = (kk >= ll).astype(np.float32)
    c["U_f"] = (kk > ll).astype(np.float32)
    c["U_b"] = (kk < ll).astype(np.float32)
    c["mneg_f"] = np.where(kk <= ll, 0.0, -1.0e5).astype(np.float32)
    c["mneg_b"] = np.where(kk >= ll, 0.0, -1.0e5).astype(np.float32)
    return c


CONST_ORDER = ["ident", "ones", "mask_prev", "mask_next", "tri_f", "tri_b", "U_f", "U_b", "mneg_f", "mneg_b"]


def make_onehots():
    oh1 = np.zeros((32, 32, 128), np.float32)
    for h in range(32):
        oh1[h, h, :] = 1.0
    oh2 = np.zeros((32, 16, 128), np.float32)
    for hp in range(16):
        oh2[2 * hp, hp, 0:64] = 1.0
        oh2[2 * hp + 1, hp, 64:128] = 1.0
    return np.ascontiguousarray(np.concatenate([oh1.reshape(32, -1), oh2.reshape(32, -1)], axis=1))


def make_rope():
    n = TLAT
    row = (np.arange(n) // 64).astype(np.float32)
    col = (np.arange(n) % 64).astype(np.float32)
    inv = (10000.0 ** (-np.arange(16, dtype=np.float32) / 16)).astype(np.float32)
    tab = np.zeros((64, 2, n), np.float32)
    for d in range(64):
        blk, half, f = d // 32, (d % 32) // 16, d % 16
        pos = row if blk == 0 else col
        ang = (pos * inv[f]).astype(np.float32)
        tab[d, 0] = np.cos(ang).astype(np.float32)
        tab[d, 1] = np.sin(ang).astype(np.float32) * (-1.0 if half == 0 else 1.0)
    return np.ascontiguousarray(np.concatenate([tab, tab], axis=0))


def tiles_for(include_ctx=True):
    tl = []
    if include_ctx:
        tl.append((0, TCTX, 1))
    for k in range(TLAT // 512):
        tl.append((TCTX + 512 * k, 512, 0))
    return tl


class Builder:
    def __init__(self, cfg):
        self.cfg = cfg
        self.nc = bass.Bass("TRN2", target_bir_lowering=False)
        self.es = ExitStack()
        self.P = Prog(self.nc, self.es)
        self.dram = {}

    def din(self, name, shape, dtype=F32):
        self.dram[name] = self.nc.dram_tensor(name, list(shape), dtype, kind="ExternalInput").ap()
        return self.dram[name]

    def dout(self, name, shape, dtype=F32):
        self.dram[name] = self.nc.dram_tensor(name, list(shape), dtype, kind="ExternalOutput").ap()
        return self.dram[name]

    def dscratch(self, name, shape, dtype=F32):
        kind = "ExternalOutput" if name in self.cfg.get("debug_outs", ()) else "Internal"
        self.dram[name] = self.nc.dram_tensor(name, list(shape), dtype, kind=kind).ap()
        return self.dram[name]

    def setup(self):
        nc, P = self.nc, self.P
        self.din("hin", [D, T])
        self.din("csrc", [128, KC * 2])
        self.din("consts", [128, 128 * len(CONST_ORDER)])
        self.din("w_mod", [DEPTH, D, 6 * D])
        self.din("b_mod", [DEPTH * 48, 128])
        self.din("norm_g", [DEPTH * 2 * KC, 128])
        self.din("final_g", [KC, 128])
        self.din("ffn_w_in", [DEPTH, D, 2 * FFN_H])
        self.din("ffn_w_out", [DEPTH, FFN_H, D])
        self.din("gmlp_w_in", [D, 4096])
        self.din("gmlp_ln_g", [1, 2048])
        self.din("gmlp_ln_b", [1, 2048])
        self.din("gmlp_w_s", [8, 128, 128])
        self.din("gmlp_b_s", [1, 1024])
        self.din("gmlp_w_out", [2048, D])
        self.din("attn_w_o", [D, D])
        self.din("ones2k", [1, 2048])
        self.din("ssd_w_in", [2, D, 5184])
        self.din("ssd_conv_w", [240, 128])
        self.din("ssd_conv_b", [48, 128])
        self.din("ssd_conv_b_row", [2, 3072])
        self.din("ssd_norm_w", [32, 128])
        self.din("ssd_a_log", [4, 32])
        self.din("ssd_dt_bias", [2, 64])
        self.din("ssd_d", [2, 32])
        self.din("onehots", [32, 48 * 128])
        self.dscratch("sz", [2048, T], BF16)
        self.dscratch("xbc", [3072, T], BF16)
        self.dscratch("dtok", [T, 64])
        self.dscratch("yf", [2048, T], BF16)
        self.dscratch("ysum", [2048, T], BF16)
        self.din("attn_w_qkv", [D, 1536])
        self.din("attn_sink", [1, 16])
        self.din("rope", [128, 2, TLAT])
        self.dscratch("qT", [D, T], BF16)
        self.dscratch("kT", [512, T], BF16)
        self.dscratch("vtok", [T, 256], BF16)
        self.din("ssd_w_out", [2, 2048, D])
        self.dscratch("h", [D, T])
        self.dscratch("ymix", [2048, T], BF16)
        self.dout("out", [D, TLAT])

        self.ps = [nc.alloc_psum_tensor(f"ps{i}", [128, 512], F32) for i in range(8)]
        self.pa = Arena(nc, SBUF_BASE)
        pa = self.pa
        self.ident = pa.alloc("ident", [128, 128], F32)
        self.ones_bf = pa.alloc("ones_bf", [128, 128], BF16)
        self.ident_bf = pa.alloc("ident_bf", [128, 128], BF16)
        self.mod = pa.alloc("mod", [128, DEPTH, 48, 2], F32)
        self.bmodT = pa.alloc("bmodT", [128, DEPTH * 48], F32)
        self.normgT = pa.alloc("normgT", [128, DEPTH * 2 * KC], F32)
        self.finalgT = pa.alloc("finalgT", [128, KC], F32)
        self.nscale = pa.alloc("nscale", [128, DEPTH, 2, KC, 2], F32)
        self.convwT = pa.alloc("convwT", [128, 240], F32)
        self.convbT = pa.alloc("convbT", [128, 48], F32)
        self.normwT = pa.alloc("normwT", [128, 32], F32)
        self.one_col = pa.alloc("one_col", [128, 1], F32)
        self.scb = pa.alloc("scb", [128, KC, 2], BF16)
        self.eps6 = pa.alloc("eps6", [128, 1], F32)
        self.eps5 = pa.alloc("eps5", [128, 1], F32)
        P.op("dve", [("memset", dict(ap=self.eps6[:], constant=1e-6)), ("memset", dict(ap=self.eps5[:], constant=1e-5)),
                     ("memset", dict(ap=self.one_col[:], constant=1.0))], writes=["eps"])

        cd = self.dram["consts"]
        P.dma("sp", "const", ("dma_start", dict(out=self.ident[:], in_=cd[:, 0:128])), writes=["ident"])
        P.dma("pool", "constp", [("dma_start", dict(out=self.ident_bf[:], in_=cd[:, 0:128])),
                                 ("dma_start", dict(out=self.ones_bf[:], in_=cd[:, 128:256]))],
              writes=["ident_bf", "ones_bf"])
        self.arena = Arena(nc, pa.top)

    def load_vecT(self, src_ap, n, dst_ap, dst_key, slot):
        P = self.P
        st = self.arena.alloc("vstage", [128, 128], F32)
        key = ("vstage", slot)
        P.dma("sp", f"vst{slot}", ("dma_start", dict(out=st[0:n, :], in_=src_ap)), writes=[key])
        ps = self.ps[slot % 2]
        pk = ("ps", slot % 2)
        P.op("pe", ("matmul", dict(out=ps[:, 0:n], lhsT=st[0:n, :], rhs=self.ident[0:n, 0:n], start=True, stop=True)),
             reads=[key, "ident"], writes=[pk])
        P.op("dve", ("tensor_copy", dict(out=dst_ap, in_=ps[:, 0:n])), reads=[pk], writes=[dst_key])

    def prologue(self):
        nc, P, dr = self.nc, self.P, self.dram
        self.arena.reset()
        self.load_vecT(dr["b_mod"][0:96, :], 96, self.bmodT[:, 0:96], "bmodT", 0)
        self.load_vecT(dr["b_mod"][96:192, :], 96, self.bmodT[:, 96:192], "bmodT", 1)
        self.load_vecT(dr["norm_g"][:, :], 64, self.normgT[:, :], "normgT", 2)
        self.load_vecT(dr["final_g"][:, :], 8, self.finalgT[:, :], "finalgT", 3)
        self.load_vecT(dr["ssd_conv_w"][0:120, :], 120, self.convwT[:, 0:120], "convwT", 4)
        self.load_vecT(dr["ssd_conv_w"][120:240, :], 120, self.convwT[:, 120:240], "convwT", 5)
        self.load_vecT(dr["ssd_conv_b"][:, :], 48, self.convbT[:, :], "convbT", 6)
        self.load_vecT(dr["ssd_norm_w"][:, :], 32, self.normwT[:, :], "normwT", 7)
        cs = self.arena.alloc("cs", [128, KC * 2], F32)
        P.dma("sp", "cs", ("dma_start", dict(out=cs[:], in_=dr["csrc"][:, :])), writes=["cs"])
        P.op("act", ("activation", dict(out=self.scb[:].rearrange("p k j -> p (k j)"), in_=cs[:], func=AF.Silu)),
             reads=["cs"], writes=["scb"])
        wm = [self.arena.alloc(f"wm{i}", [128, KC, 1536], BF16) for i in range(2)]
        n = 0
        for i in range(DEPTH):
            if i not in self.cfg["layers_mod"]:
                continue
            pm = self.ps[2 + (i % 2)]
            pmk = ("ps", 2 + (i % 2))
            for q in range(4):
                w = wm[n % 2]
                wk = ("wm", n % 2)
                n += 1
                src = dr["w_mod"][i, :, q * 1536:(q + 1) * 1536].rearrange("(kc p) f -> p kc f", p=128)
                P.dma("pool", f"wm{n % 2}", [("dma_start", dict(out=w[:, 0:4, :], in_=src[:, 0:4, :])),
                                            ("dma_start", dict(out=w[:, 4:8, :], in_=src[:, 4:8, :]))], writes=[wk])
                for fc in range(12):
                    col = (q * 12 + fc) * 2
                    calls = []
                    for kc in range(KC):
                        calls.append(("matmul", dict(out=pm[:, col:col + 2], lhsT=w[:, kc, fc * 128:(fc + 1) * 128],
                                                     rhs=self.scb[:, kc, :], start=(kc == 0), stop=(kc == KC - 1))))
                    P.op("pe", calls, reads=[wk, "scb"], writes=[pmk] if (q == 0 and fc == 0) else [(pmk, "part", q, fc)])
            P.op("dve", ("tensor_tensor", dict(out=self.mod[:, i, :, :],
                                               in0=pm[:, 0:96].rearrange("p (f j) -> p f j", j=2),
                                               in1=self.bmodT[:, i * 48:(i + 1) * 48].unsqueeze(2).to_broadcast([128, 48, 2]),
                                               op=ALU.add)),
                 reads=[pmk, "bmodT"] + [(pmk, "part", q, fc) for q in range(4) for fc in range(12)], writes=["mod"])
            for nrm in range(2):
                part = 1 + 3 * nrm
                P.op("dve", ("scalar_tensor_tensor", dict(
                    out=self.nscale[:, i, nrm, :, :], in0=self.mod[:, i, part * 8:(part + 1) * 8, :], scalar=1.0,
                    in1=self.normgT[:, (i * 2 + nrm) * 8:(i * 2 + nrm + 1) * 8].unsqueeze(2).to_broadcast([128, 8, 2]),
                    op0=ALU.add, op1=ALU.mult)), reads=["mod", "normgT"], writes=["nscale"])
        P.barrier()

    def rstd_from(self, ps_ss, pk_ss, rstd, N, n, eps=1e-6, key="rstd"):
        P = self.P
        eps_ap = self.eps6[:, 0:1] if eps == 1e-6 else self.eps5[:, 0:1]
        P.op("act", ("activation", dict(out=rstd[:, :N], in_=ps_ss[:, :N], func=AF.Sqrt, scale=1.0 / n, bias=eps_ap)),
             reads=[pk_ss, "eps"], writes=[key])
        P.op("dve", ("reciprocal", dict(out=rstd[:, :N], in_=rstd[:, :N])), reads=[key], writes=[key])

    def norm_mod(self, h, hk, a, ak, N, i, nrm, j, sq, rstd, tmp, ps_ss, pk_ss, sq_extra_writes=()):
        P = self.P
        shift_part = 0 if nrm == 0 else 3
        P.op("act", ("activation", dict(out=sq[:, :, :N], in_=h[:, :, :N], func=AF.Square)), reads=[hk], writes=["sq"] + list(sq_extra_writes))
        calls = [("matmul", dict(out=ps_ss[:, :N], lhsT=self.ones_bf[:], rhs=sq[:, kc, :N], start=(kc == 0), stop=(kc == KC - 1)))
                 for kc in range(KC)]
        P.op("pe", calls, reads=["sq", "ones_bf"], writes=[pk_ss])
        self.rstd_from(ps_ss, pk_ss, rstd, N, D)
        for kc in range(KC):
            tb = tmp[kc % 2]
            tk = ("tmp", kc % 2)
            P.op("dve", ("scalar_tensor_tensor", dict(out=tb[:, :N], in0=h[:, kc, :N], scalar=self.nscale[:, i, nrm, kc, j:j + 1],
                                                      in1=rstd[:, :N], op0=ALU.mult, op1=ALU.mult)),
                 reads=[hk, "rstd", "nscale"], writes=[tk])
            P.op("act", ("activation", dict(out=a[:, kc, :N], in_=tb[:, :N], func=AF.Identity,
                                            bias=self.mod[:, i, shift_part * 8 + kc, j:j + 1], scale=1.0)),
                 reads=[tk, "mod"], writes=[(ak, kc)])

    def ffn_phase_pipelined(self, i, src, dst, final=False):
        nc, P, dr = self.nc, self.P, self.dram
        ar = self.arena
        ar.reset()
        win = ar.alloc("win", [128, KC, 2 * FFN_H], BF16)
        wout = ar.alloc("wout", [128, FJ, D], BF16)
        hN = ar.alloc("hN", [128, KC, 512], F32)
        X = [ar.alloc(f"X{k}", [128, KC, 512], BF16) for k in range(2)]
        hid = ar.alloc("hid", [128, FJ, 512], BF16)
        rstd = ar.alloc("rstd", [128, 512], F32)
        tmp = [ar.alloc(f"tmp{k}", [128, 512], F32) for k in range(2)]
        sg = [ar.alloc(f"sg{k}", [128, 512], BF16) for k in range(2)]
        hr = [ar.alloc(f"hr{k}", [128, 512], F32) for k in range(2)]
        rstd2 = ar.alloc("rstd2", [128, 512], F32) if final else None
        wsrc = dr["ffn_w_in"][i].rearrange("(kc p) f -> p kc f", p=128)
        for jp in range(FJ // 2):
            P.dma("pool", f"wg{jp}", [("dma_start", dict(out=win[:, :, off + jp * 256:off + (jp + 1) * 256], in_=wsrc[:, :, off + jp * 256:off + (jp + 1) * 256]))
                                      for off in (0, FFN_H)], writes=[("win", jp)])
        wosrc = dr["ffn_w_out"][i].rearrange("(j p) m -> p j m", p=128)
        for jj in range(0, FJ, 2):
            P.dma("pool", "wout", ("dma_start", dict(out=wout[:, jj:jj + 2, :], in_=wosrc[:, jj:jj + 2, :])), writes=[("wout", jj), ("wout", jj + 1)])
        tl = tiles_for(include_ctx=not final)

        def do_norm(ti):
            t0, N, j = tl[ti]
            xb = ti % 2
            P.dma("sp", "hload", ("dma_start", dict(out=hN[:, :, :N], in_=src[:, t0:t0 + N].rearrange("(kc p) t -> p kc t", p=128))),
                  reads=[("dram", src.name, t0)], writes=["hN"])
            ak = ("X", xb)
            self.norm_mod(hN, "hN", X[xb], ak, N, i, 1, j, X[xb], rstd, tmp, self.ps[0], ("ps", 0),
                          sq_extra_writes=[(ak, kc) for kc in range(KC)])

        do_norm(0)
        nh = 0
        for ti, (t0, N, j) in enumerate(tl):
            a = X[ti % 2]
            ak = ("X", ti % 2)
            akeys = [(ak, kc) for kc in range(KC)]
            for jj in range(FJ):
                pg, pgk = self.ps[1 + 2 * (jj % 2)], ("ps", 1 + 2 * (jj % 2))
                pu, puk = self.ps[2 + 2 * (jj % 2)], ("ps", 2 + 2 * (jj % 2))
                calls = [("matmul", dict(out=pg[:, :N], lhsT=win[:, kc, jj * 128:(jj + 1) * 128], rhs=a[:, kc, :N],
                                         start=(kc == 0), stop=(kc == KC - 1))) for kc in range(KC)]
                P.op("pe", calls, reads=akeys + [("win", jj // 2)], writes=[pgk])
                calls = [("matmul", dict(out=pu[:, :N], lhsT=win[:, kc, FFN_H + jj * 128:FFN_H + (jj + 1) * 128], rhs=a[:, kc, :N],
                                         start=(kc == 0), stop=(kc == KC - 1))) for kc in range(KC)]
                P.op("pe", calls, reads=akeys + [("win", jj // 2)], writes=[puk])
                s_ = sg[jj % 2]
                sk = ("sg", jj % 2)
                P.op("act", ("activation", dict(out=s_[:, :N], in_=pg[:, :N], func=AF.Silu)), reads=[pgk], writes=[sk])
                P.op("dve", ("tensor_tensor", dict(out=hid[:, jj, :N], in0=s_[:, :N], in1=pu[:, :N], op=ALU.mult)),
                     reads=[sk, puk], writes=[("hid", jj)])
            if ti + 1 < len(tl):
                do_norm(ti + 1)
            hkeys = [("hid", jj) for jj in range(FJ)]
            wokeys = [("wout", jj) for jj in range(FJ)]
            for mc in range(KC):
                hb_, hbk = hr[nh % 2], ("hr", nh % 2)
                nh += 1
                P.dma("sp", f"hr{(nh - 1) % 2}", ("dma_start", dict(out=hb_[:, :N], in_=src[mc * 128:(mc + 1) * 128, t0:t0 + N])), writes=[hbk])
                po, pok = self.ps[5 + (mc % 2)], ("ps", 5 + (mc % 2))
                calls = [("matmul", dict(out=po[:, :N], lhsT=wout[:, jj, mc * 128:(mc + 1) * 128], rhs=hid[:, jj, :N],
                                         start=(jj == 0), stop=(jj == FJ - 1))) for jj in range(FJ)]
                P.op("pe", calls, reads=hkeys + wokeys, writes=[pok])
                if final:
                    P.op("dve", ("scalar_tensor_tensor", dict(out=hN[:, mc, :N], in0=po[:, :N], scalar=self.mod[:, i, 5 * 8 + mc, j:j + 1],
                                                              in1=hb_[:, :N], op0=ALU.mult, op1=ALU.add)),
                         reads=[pok, hbk, "mod"], writes=["hN"])
                    continue
                P.op("dve", ("scalar_tensor_tensor", dict(out=hb_[:, :N], in0=po[:, :N], scalar=self.mod[:, i, 5 * 8 + mc, j:j + 1],
                                                          in1=hb_[:, :N], op0=ALU.mult, op1=ALU.add)),
                     reads=[pok, hbk, "mod"], writes=[hbk])
                P.dma("sp", f"hr{(nh - 1) % 2}", ("dma_start", dict(out=dst[mc * 128:(mc + 1) * 128, t0:t0 + N], in_=hb_[:, :N])),
                      reads=[hbk], writes=[("dram", dst.name, t0, mc)])
            if final:
                sqf = X[ti % 2]
                xk = [(("X", ti % 2), kc) for kc in range(KC)]
                P.op("act", ("activation", dict(out=sqf[:, :, :N], in_=hN[:, :, :N], func=AF.Square)), reads=["hN"], writes=xk)
                calls = [("matmul", dict(out=self.ps[7][:, :N], lhsT=self.ones_bf[:], rhs=sqf[:, kc, :N], start=(kc == 0), stop=(kc == KC - 1)))
                         for kc in range(KC)]
                P.op("pe", calls, reads=xk + ["ones_bf"], writes=[("ps", 7)])
                self.rstd_from(self.ps[7], ("ps", 7), rstd2, N, D, key="rstd2")
                for kc in range(KC):
                    P.op("dve", ("scalar_tensor_tensor", dict(out=hN[:, kc, :N], in0=hN[:, kc, :N], scalar=self.finalgT[:, kc:kc + 1],
                                                              in1=rstd2[:, :N], op0=ALU.mult, op1=ALU.mult)),
                         reads=["hN", "rstd2", "finalgT"], writes=["hN"])
                P.dma("sp", "ostore", ("dma_start", dict(out=dr["out"][:, t0 - TCTX:t0 - TCTX + N].rearrange("(kc p) t -> p kc t", p=128), in_=hN[:, :, :N])),
                      reads=["hN"], writes=[("dram", "out", t0)])
        P.barrier()

    def ffn_phase(self, i, src, dst, final=False):
        return self.ffn_phase_pipelined(i, src, dst, final=final)
        nc, P, dr = self.nc, self.P, self.dram
        ar = self.arena
        ar.reset()
        win = ar.alloc("win", [128, KC, 2 * FFN_H], BF16)
        wout = ar.alloc("wout", [128, FJ, D], BF16)
        hb = [ar.alloc("h0", [128, KC, 512], F32)]
        ab = [ar.alloc(f"a{k}", [128, KC, 512], BF16) for k in range(1)]
        sq = ar.alloc("sq", [128, KC, 512], BF16)
        hid = ar.alloc("hid", [128, FJ, 512], BF16)
        rstd = ar.alloc("rstd", [128, 512], F32)
        tmp = [ar.alloc(f"tmp{k}", [128, 512], F32) for k in range(2)]
        sg = [ar.alloc(f"sg{k}", [128, 512], F32) for k in range(2)]
        wsrc = dr["ffn_w_in"][i].rearrange("(kc p) f -> p kc f", p=128)
        for jp in range(FJ // 2):
            P.dma("pool", f"wg{jp}", [("dma_start", dict(out=win[:, :, off + jp * 256:off + (jp + 1) * 256], in_=wsrc[:, :, off + jp * 256:off + (jp + 1) * 256]))
                                      for off in (0, FFN_H)], writes=[("win", jp)])
        wosrc = dr["ffn_w_out"][i].rearrange("(j p) m -> p j m", p=128)
        for jj in range(0, FJ, 2):
            P.dma("pool", "wout", ("dma_start", dict(out=wout[:, jj:jj + 2, :], in_=wosrc[:, jj:jj + 2, :])), writes=[("wout", jj), ("wout", jj + 1)])
        tl = tiles_for(include_ctx=not final)
        for ti, (t0, N, j) in enumerate(tl):
            h = hb[0]
            hk = "h0"
            a = ab[0]
            ak = ("a", 0)
            hsrc = src[:, t0:t0 + N].rearrange("(kc p) t -> p kc t", p=128)
            P.dma("sp", "hload", ("dma_start", dict(out=h[:, :, :N], in_=hsrc)), reads=[("dram", src.name, t0)], writes=[hk])
            self.norm_mod(h, hk, a, ak, N, i, 1, j, sq, rstd, tmp, self.ps[0], ("ps", 0))
            akeys = [(ak, kc) for kc in range(KC)]
            for jj in range(FJ):
                pg, pgk = self.ps[1 + 2 * (jj % 2)], ("ps", 1 + 2 * (jj % 2))
                pu, puk = self.ps[2 + 2 * (jj % 2)], ("ps", 2 + 2 * (jj % 2))
                calls = [("matmul", dict(out=pg[:, :N], lhsT=win[:, kc, jj * 128:(jj + 1) * 128], rhs=a[:, kc, :N],
                                         start=(kc == 0), stop=(kc == KC - 1))) for kc in range(KC)]
                P.op("pe", calls, reads=akeys + [("win", jj // 2)], writes=[pgk])
                calls = [("matmul", dict(out=pu[:, :N], lhsT=win[:, kc, FFN_H + jj * 128:FFN_H + (jj + 1) * 128], rhs=a[:, kc, :N],
                                         start=(kc == 0), stop=(kc == KC - 1))) for kc in range(KC)]
                P.op("pe", calls, reads=akeys + [("win", jj // 2)], writes=[puk])
                s = sg[jj % 2]
                sk = ("sg", jj % 2)
                P.op("act", ("activation", dict(out=s[:, :N], in_=pg[:, :N], func=AF.Silu)), reads=[pgk], writes=[sk])
                P.op("dve", ("tensor_tensor", dict(out=hid[:, jj, :N], in0=s[:, :N], in1=pu[:, :N], op=ALU.mult)),
                     reads=[sk, puk], writes=[("hid", jj)])
            hkeys = [("hid", jj) for jj in range(FJ)]
            wokeys = [("wout", jj) for jj in range(FJ)]
            for mc in range(KC):
                po, pok = self.ps[5 + (mc % 2)], ("ps", 5 + (mc % 2))
                calls = [("matmul", dict(out=po[:, :N], lhsT=wout[:, jj, mc * 128:(mc + 1) * 128], rhs=hid[:, jj, :N],
                                         start=(jj == 0), stop=(jj == FJ - 1))) for jj in range(FJ)]
                P.op("pe", calls, reads=hkeys + wokeys, writes=[pok])
                P.op("dve", ("scalar_tensor_tensor", dict(out=h[:, mc, :N], in0=po[:, :N], scalar=self.mod[:, i, 5 * 8 + mc, j:j + 1],
                                                          in1=h[:, mc, :N], op0=ALU.mult, op1=ALU.add)),
                     reads=[pok, hk, "mod"], writes=[hk])
            if not final:
                hdst = dst[:, t0:t0 + N].rearrange("(kc p) t -> p kc t", p=128)
                P.dma("sp", "hstore", ("dma_start", dict(out=hdst, in_=h[:, :, :N])), reads=[hk], writes=[("dram", dst.name, t0)])
            else:
                P.op("act", ("activation", dict(out=sq[:, :, :N], in_=h[:, :, :N], func=AF.Square)), reads=[hk], writes=["sq"])
                calls = [("matmul", dict(out=self.ps[0][:, :N], lhsT=self.ones_bf[:], rhs=sq[:, kc, :N], start=(kc == 0), stop=(kc == KC - 1)))
                         for kc in range(KC)]
                P.op("pe", calls, reads=["sq", "ones_bf"], writes=[("ps", 0)])
                self.rstd_from(self.ps[0], ("ps", 0), rstd, N, D)
                for kc in range(KC):
                    P.op("dve", ("scalar_tensor_tensor", dict(out=h[:, kc, :N], in0=h[:, kc, :N], scalar=self.finalgT[:, kc:kc + 1],
                                                              in1=rstd[:, :N], op0=ALU.mult, op1=ALU.mult)),
                         reads=[hk, "rstd", "finalgT"], writes=[hk])
                odst = dr["out"][:, t0 - TCTX:t0 - TCTX + N].rearrange("(kc p) t -> p kc t", p=128)
                P.dma("sp", "ostore", ("dma_start", dict(out=odst, in_=h[:, :, :N])), reads=[hk], writes=[("dram", "out", t0)])
        P.barrier()

    def gmlp_phase(self, i, src, include_ctx=True):
        nc, P, dr = self.nc, self.P, self.dram
        ar = self.arena
        ar.reset()
        win = ar.alloc("gwin", [128, KC, 4096], BF16)
        wsT = ar.alloc("wsT", [128, 8, 128], BF16)
        gbc = ar.alloc("gbc", [128, 2048], F32)
        lhs2 = ar.alloc("lhs2", [2, 2048], BF16)
        rhs2 = ar.alloc("rhs2", [2, 8, 128], BF16)
        wsld = ar.alloc("wsld", [128, 8, 128], F32)
        hb = ar.alloc("h0", [128, KC, 512], F32)
        X = [ar.alloc(f"X{k}", [128, KC, 512], BF16) for k in range(2)]
        rstd = ar.alloc("rstd", [128, 512], F32)
        tmp = [ar.alloc(f"tmp{k}", [128, 512], F32) for k in range(2)]
        u = ar.alloc("u", [128, 16, 512], BF16)
        vt = [ar.alloc(f"vt{k}", [128, 2048], F32) for k in range(2)]
        vng = ar.alloc("vng", [128, 4, 2048], BF16)
        m = ar.alloc("m", [128, 16, 512], BF16)
        stats = ar.alloc("stats", [128, 4, 6], F32)
        mv = ar.alloc("mv", [128, 2], F32)
        lrs = ar.alloc("lrs", [128, 1], F32)
        wsrc = dr["gmlp_w_in"].rearrange("(kc p) f -> p kc f", p=128)
        for cg in range(8):
            P.dma("pool", f"wg{cg}", ("dma_start", dict(out=win[:, :, cg * 512:(cg + 1) * 512], in_=wsrc[:, :, cg * 512:(cg + 1) * 512])), writes=[("win", cg)])
        winkeys = [("win", cg) for cg in range(8)]
        P.dma("sp", "gsm", [("dma_start", dict(out=wsld[:], in_=dr["gmlp_w_s"].rearrange("g t s -> t g s"))),
                            ("dma_start", dict(out=gbc[:], in_=dr["gmlp_ln_g"].partition_broadcast(128)))],
              writes=["wsld", "gbc"])
        P.dma("pool", "gsm2", [("dma_start", dict(out=lhs2[0:1, :], in_=dr["gmlp_ln_b"][:, :])),
                               ("dma_start", dict(out=lhs2[1:2, :], in_=dr["ones2k"][:, :])),
                               ("dma_start", dict(out=rhs2[1:2, :, :], in_=dr["gmlp_b_s"].rearrange("o (g t) -> o g t", g=8)))],
              writes=["lhs2", "rhs2b"])
        for g in range(8):
            ps, pk = self.ps[g % 2], ("ps", g % 2)
            P.op("pe", ("matmul", dict(out=ps[:, 0:128], lhsT=wsld[:, g, :], rhs=self.ident[:], start=True, stop=True)),
                 reads=["wsld", "ident"], writes=[pk])
            P.op("dve", ("tensor_copy", dict(out=wsT[:, g, :], in_=ps[:, 0:128])), reads=[pk], writes=[("wsT", g)])
        for g in range(8):
            ps, pk = self.ps[2 + g % 2], ("ps", 2 + g % 2)
            P.op("pe", ("matmul", dict(out=ps[0:1, 0:128], lhsT=self.ones_bf[:, 0:1], rhs=wsT[:, g, :], start=True, stop=True)),
                 reads=[("wsT", g), "ones_bf"], writes=[pk])
            P.op("dve", ("tensor_copy", dict(out=rhs2[0:1, g, :], in_=ps[0:1, 0:128])), reads=[pk], writes=[("rhs2a", g)])
        wskeys = [("wsT", g) for g in range(8)] + [("rhs2a", g) for g in range(8)] + ["rhs2b", "lhs2"]
        ymix = dr["ymix"]
        tl_ = tiles_for(include_ctx)

        def do_norm(ti):
            t0, N, j = tl_[ti]
            P.dma("sp", "hload", ("dma_start", dict(out=hb[:, :, :N], in_=src[:, t0:t0 + N].rearrange("(kc p) t -> p kc t", p=128))),
                  reads=[("dram", src.name, t0)], writes=["h0"])
            ak_ = ("X", ti % 2)
            self.norm_mod(hb, "h0", X[ti % 2], ak_, N, i, 0, j, X[ti % 2], rstd, tmp, self.ps[0], ("ps", 0),
                          sq_extra_writes=[(ak_, kc) for kc in range(KC)])

        do_norm(0)
        for ti, (t0, N, j) in enumerate(tl_):
            a = X[ti % 2]
            akeys = [(("X", ti % 2), kc) for kc in range(KC)]
            nsub = N // 128
            for dc in range(16):
                ps, pk = self.ps[1 + dc % 2], ("ps", 1 + dc % 2)
                calls = [("matmul", dict(out=ps[:, :N], lhsT=win[:, kc, dc * 128:(dc + 1) * 128], rhs=a[:, kc, :N],
                                         start=(kc == 0), stop=(kc == KC - 1))) for kc in range(KC)]
                P.op("pe", calls, reads=akeys + winkeys, writes=[pk])
                P.op("act", ("activation", dict(out=u[:, dc, :N], in_=ps[:, :N], func=AF.Gelu)), reads=[pk], writes=[("u", dc)])
            for sub in range(nsub):
                v = vt[sub % 2]
                vk = ("vt", sub % 2)
                for cb in range(4):
                    ps, pk = self.ps[3 + cb % 2], ("ps", 3 + cb % 2)
                    calls = [("matmul", dict(out=ps[:, :], lhsT=a[:, kc, sub * 128:(sub + 1) * 128],
                                             rhs=win[:, kc, 2048 + cb * 512:2048 + (cb + 1) * 512],
                                             start=(kc == 0), stop=(kc == KC - 1))) for kc in range(KC)]
                    P.op("pe", calls, reads=akeys + winkeys, writes=[pk])
                    P.op("act", ("activation", dict(out=v[:, cb * 512:(cb + 1) * 512], in_=ps[:, :], func=AF.Gelu)),
                         reads=[pk], writes=[(vk, cb)])
                    P.op("dve", ("bn_stats", dict(out=stats[:, cb, :], in_=v[:, cb * 512:(cb + 1) * 512])), reads=[(vk, cb)], writes=[("stats", cb)])
                P.op("dve", ("bn_aggr", dict(out=mv[:], in_=stats[:].rearrange("p a b -> p (a b)"))),
                     reads=[("stats", cb) for cb in range(4)], writes=["mv"])
                P.op("act", ("activation", dict(out=lrs[:], in_=mv[:, 1:2], func=AF.Sqrt, scale=1.0, bias=self.eps5[:, 0:1])),
                     reads=["mv", "eps"], writes=["lrs"])
                P.op("dve", ("reciprocal", dict(out=lrs[:], in_=lrs[:])), reads=["lrs"], writes=["lrs"])
                P.op("dve", ("tensor_scalar", dict(out=v[:], in0=v[:], scalar1=mv[:, 0:1], scalar2=lrs[:, 0:1], op0=ALU.subtract, op1=ALU.mult)),
                     reads=[(vk, cb) for cb in range(4)] + ["mv", "lrs"], writes=[(vk, cb) for cb in range(4)])
                P.op("pool", ("tensor_tensor", dict(out=vng[:, sub, :], in0=v[:], in1=gbc[:], op=ALU.mult)),
                     reads=[(vk, cb) for cb in range(4)] + ["gbc"], writes=[("vng", sub)])
            if ti + 1 < len(tl_):
                do_norm(ti + 1)
            vkeys = [("vng", sub) for sub in range(nsub)]
            for dc in range(16):
                g = dc // 2
                ps, pk = self.ps[5 + dc % 2], ("ps", 5 + dc % 2)
                calls = []
                for sub in range(nsub):
                    calls.append(("matmul", dict(out=ps[:, sub * 128:(sub + 1) * 128], lhsT=vng[:, sub, dc * 128:(dc + 1) * 128],
                                                 rhs=wsT[:, g, :], start=True, stop=False)))
                    calls.append(("matmul", dict(out=ps[:, sub * 128:(sub + 1) * 128], lhsT=lhs2[:, dc * 128:(dc + 1) * 128],
                                                 rhs=rhs2[:, g, :], start=False, stop=True)))
                P.op("pe", calls, reads=vkeys + wskeys, writes=[pk])
                P.op("dve", ("tensor_tensor", dict(out=m[:, dc, :N], in0=u[:, dc, :N], in1=ps[:, :N], op=ALU.mult)),
                     reads=[pk, ("u", dc)], writes=[("m", dc)])
            ydst = ymix[:, t0:t0 + N].rearrange("(kc p) t -> p kc t", p=128)
            P.dma("sp", "ystore", ("dma_start", dict(out=ydst, in_=m[:, :, :N])), reads=[("m", dc) for dc in range(16)],
                  writes=[("dram", "ymix", t0)])
        P.barrier()

    def ssd_proj_phase(self, i, src):
        nc, P, dr = self.nc, self.P, self.dram
        js = i // 3
        ar = self.arena
        ar.reset()
        win = ar.alloc("swin", [128, KC, 5184], BF16)
        hb = ar.alloc("h0", [128, KC, 512], F32)
        X = [ar.alloc(f"X{k}", [128, KC, 512], BF16) for k in range(2)]
        rstd = ar.alloc("rstd", [128, 512], F32)
        tmp = [ar.alloc(f"tmp{k}", [128, 512], F32) for k in range(2)]
        zsb = ar.alloc("zsb", [128, 16, 512], BF16)
        xpre = ar.alloc("xpre", [128, 24, 518], BF16)
        diag = ar.alloc("diag", [128, 24, 5, 128], BF16)
        xo = [ar.alloc(f"xo{k}", [128, 514], BF16) for k in range(4)]
        dtb = ar.alloc("dtb", [128, 64], F32)
        dx = [ar.alloc(f"dx{k}", [128, 4, 64], F32) for k in range(5)]
        wsrc = dr["ssd_w_in"][js].rearrange("(kc p) f -> p kc f", p=128)
        for cg in range(10):
            P.dma("pool", f"wg{cg}", ("dma_start", dict(out=win[:, :, cg * 512:(cg + 1) * 512], in_=wsrc[:, :, cg * 512:(cg + 1) * 512])), writes=[("win", cg)])
        P.dma("pool", "wg10", ("dma_start", dict(out=win[:, :, 5120:5184], in_=wsrc[:, :, 5120:5184])), writes=[("win", 10)])
        winkeys = [("win", 10)]
        P.dma("sp", "dtb", ("dma_start", dict(out=dtb[:], in_=dr["ssd_dt_bias"][js:js + 1, :].partition_broadcast(128))), writes=["dtb"])
        for ch in range(24):
            calls = [("tensor_scalar", dict(out=diag[:, ch, tap, :], in0=self.ident[:], scalar1=self.convwT[:, js * 120 + tap * 24 + ch:js * 120 + tap * 24 + ch + 1],
                                            scalar2=None, op0=ALU.mult)) for tap in range(5)]
            P.op("dve", calls, reads=["ident", "convwT"], writes=[("diag", ch)])
        no = 0
        tl_ = tiles_for(True)

        def do_norm(ti):
            t0, N, j = tl_[ti]
            P.dma("sp", "hload", ("dma_start", dict(out=hb[:, :, :N], in_=src[:, t0:t0 + N].rearrange("(kc p) t -> p kc t", p=128))),
                  reads=[("dram", src.name, t0)], writes=["h0"])
            ak_ = ("X", ti % 2)
            self.norm_mod(hb, "h0", X[ti % 2], ak_, N, i, 0, j, X[ti % 2], rstd, tmp, self.ps[0], ("ps", 0),
                          sq_extra_writes=[(ak_, kc) for kc in range(KC)])

        do_norm(0)
        for ti, (t0, N, j) in enumerate(tl_):
            a = X[ti % 2]
            akeys = [(("X", ti % 2), kc) for kc in range(KC)]
            nsub = N // 128
            s0, s1 = (0, TCTX) if j == 1 else (TCTX, T)
            first, last = (t0 == s0), (t0 + N == s1)
            if first:
                P.op("dve", ("memset", dict(ap=xpre[:, :, 0:4], constant=0.0)), writes=["xpre_h"])
            if last:
                P.op("dve", ("memset", dict(ap=xpre[:, :, 4 + N:6 + N], constant=0.0)), writes=["xpre_t"])
            for c in range(40):
                ps, pk = self.ps[1 + c % 4], ("ps", 1 + c % 4)
                calls = [("matmul", dict(out=ps[:, :N], lhsT=win[:, kc, c * 128:(c + 1) * 128], rhs=a[:, kc, :N],
                                         start=(kc == 0), stop=(kc == KC - 1))) for kc in range(KC)]
                P.op("pe", calls, reads=akeys + [("win", c // 4)], writes=[pk])
                if c < 16:
                    P.op("act", ("activation", dict(out=zsb[:, c, :N], in_=ps[:, :N], func=AF.Silu)), reads=[pk], writes=[("zsb", c)])
                else:
                    P.op("dve", ("tensor_copy", dict(out=xpre[:, c - 16, 4:4 + N], in_=ps[:, :N])), reads=[pk], writes=[("xpre", c - 16)])
            pd, pdk = self.ps[5], ("ps", 5)
            for sub in range(nsub):
                calls = [("matmul", dict(out=pd[:, sub * 64:(sub + 1) * 64], lhsT=a[:, kc, sub * 128:(sub + 1) * 128], rhs=win[:, kc, 5120:5184],
                                         start=(kc == 0), stop=(kc == KC - 1))) for kc in range(KC)]
                P.op("pe", calls, reads=akeys + winkeys, writes=[pdk] if sub == 0 else [(pdk, sub)])
            pdr = [pdk] + [(pdk, sub) for sub in range(1, nsub)]
            xb, nx, ab, ee, rr = dx
            P.op("dve", [("tensor_tensor", dict(out=xb[:, :nsub, :], in0=pd[:, 0:nsub * 64].rearrange("p (s c) -> p s c", c=64),
                                               in1=dtb[:].unsqueeze(1).to_broadcast([128, nsub, 64]), op=ALU.add)),
                         ("tensor_scalar", dict(out=nx[:, :nsub, :], in0=xb[:, :nsub, :], scalar1=-1.0, scalar2=None, op0=ALU.mult)),
                         ("tensor_tensor", dict(out=ab[:, :nsub, :], in0=xb[:, :nsub, :], in1=nx[:, :nsub, :], op=ALU.max)),
                         ("tensor_scalar", dict(out=rr[:, :nsub, :], in0=xb[:, :nsub, :], scalar1=0.0, scalar2=None, op0=ALU.max))],
                 reads=pdr + ["dtb"], writes=["dx", "dxr"])
            P.op("act", [("activation", dict(out=ee[:, :nsub, :], in_=ab[:, :nsub, :], func=AF.Exp, scale=-1.0)),
                         ("activation", dict(out=ee[:, :nsub, :], in_=ee[:, :nsub, :], func=AF.Ln, bias=self.one_col[:, 0:1], scale=1.0))],
                 reads=["dx", "eps"], writes=["dxe"])
            P.op("dve", ("tensor_tensor", dict(out=rr[:, :nsub, :], in0=rr[:, :nsub, :], in1=ee[:, :nsub, :], op=ALU.add)),
                 reads=["dx", "dxe"], writes=["dxr"])
            P.dma("sp", "dtst", ("dma_start", dict(out=dr["dtok"][t0:t0 + N, :].rearrange("(s p) c -> p s c", p=128), in_=rr[:, :nsub, :])),
                  reads=["dxr"], writes=[("dram", "dtok", t0)])
            P.dma("sp", "zst", ("dma_start", dict(out=dr["sz"][:, t0:t0 + N].rearrange("(c p) t -> p c t", p=128), in_=zsb[:, :, :N])),
                  reads=[("zsb", c) for c in range(16)], writes=[("dram", "sz", t0)])
            if ti + 1 < len(tl_):
                do_norm(ti + 1)
            p0 = 2 if first else 0
            p1 = N + 2 if last else N
            groups = [(p0, p1)] if p1 - p0 <= 512 else [(p0, 512), (512, p1)]
            tok0 = t0 - 2 + p0
            for ch in range(24):
                o_, ok_ = xo[no % 4], ("xo", no % 4)
                no += 1
                for gi_, (pa_, pb_) in enumerate(groups):
                    ps, pk = self.ps[6 + (ch + gi_) % 2], ("ps", 6 + (ch + gi_) % 2)
                    calls = [("matmul", dict(out=ps[:, 0:pb_ - pa_], lhsT=diag[:, ch, tap, :], rhs=xpre[:, ch, pa_ + tap:pb_ + tap],
                                             start=(tap == 0), stop=(tap == 4))) for tap in range(5)]
                    P.op("pe", calls, reads=[("xpre", ch), "xpre_h", "xpre_t", ("diag", ch)], writes=[pk])
                    P.op("act", ("activation", dict(out=o_[:, pa_ - p0:pb_ - p0], in_=ps[:, 0:pb_ - pa_], func=AF.Silu,
                                                    bias=self.convbT[:, js * 24 + ch:js * 24 + ch + 1], scale=1.0)),
                         reads=[pk, "convbT"], writes=[ok_ if gi_ == 0 else (ok_, gi_)])
                P.dma("sp", f"xo{(no - 1) % 4}", ("dma_start", dict(out=dr["xbc"][ch * 128:(ch + 1) * 128, tok0:tok0 + (p1 - p0)], in_=o_[:, 0:p1 - p0])),
                      reads=[ok_] + [(ok_, g_) for g_ in range(1, len(groups))], writes=[("dram", "xbc", ti, ch)])
            if not last:
                P.op("dve", ("tensor_copy", dict(out=xpre[:, :, 0:4], in_=xpre[:, :, N:N + 4])), reads=[("xpre", ch) for ch in range(24)], writes=["xpre_h"])
        P.barrier()

    def ssd_scan_phase(self, i, d, store_ctx=True):
        nc, P, dr = self.nc, self.P, self.dram
        js = i // 3
        fwd = (d == 0)
        ar = self.arena
        ar.reset()
        cd = dr["consts"]
        ci = {n: k for k, n in enumerate(CONST_ORDER)}
        def cslice(n):
            return cd[:, ci[n] * 128:(ci[n] + 1) * 128]
        tri = ar.alloc("tri", [128, 128], F32)
        Um = ar.alloc("Um", [128, 128], F32)
        Ubf = ar.alloc("Ubf", [128, 128], BF16)
        onesf = ar.alloc("onesf", [128, 128], F32)
        A_bc = ar.alloc("A_bc", [128, 32], F32)
        D_bc = ar.alloc("D_bc", [128, 32], F32)
        S = ar.alloc("S", [128, 2048], F32)
        Sbf = ar.alloc("Sbf", [128, 2048], BF16)
        sfx = "f" if fwd else "b"
        P.dma("sp", "sc1", [("dma_start", dict(out=tri[:], in_=cslice("tri_" + sfx))), ("dma_start", dict(out=Um[:], in_=cslice("U_" + sfx))),
                            ("dma_start", dict(out=onesf[:], in_=cslice("ones"))),
                            ("dma_start", dict(out=A_bc[:], in_=dr["ssd_a_log"][2 * js + d:2 * js + d + 1, :].partition_broadcast(128))),
                            ("dma_start", dict(out=D_bc[:], in_=dr["ssd_d"][js:js + 1, :].partition_broadcast(128)))],
              writes=["tri", "Um", "onesf", "A_bc", "D_bc"])
        P.dma("pool", "sc2", ("dma_start", dict(out=Ubf[:], in_=cslice("U_" + sfx))), writes=["Ubf"])
        P.op("act", ("activation", dict(out=A_bc[:], in_=A_bc[:], func=AF.Exp)), reads=["A_bc"], writes=["A_bc"])
        P.op("dve", ("tensor_scalar", dict(out=A_bc[:], in0=A_bc[:], scalar1=-1.0, scalar2=None, op0=ALU.mult)), reads=["A_bc"], writes=["A_bc"])
        P.op("dve", [("memset", dict(ap=S[:], constant=0.0)), ("memset", dict(ap=Sbf[:], constant=0.0))],
             writes=[("S", g) for g in range(4)] + [("Sbf", g) for g in range(4)])
        NB = 2
        xcs4 = [ar.alloc(f"xcs4{k}", [128, 24, 512], BF16) for k in range(2)]
        dtk4 = [ar.alloc(f"dtk4{k}", [128, 4, 64], F32) for k in range(2)]
        xs = [ar.alloc(f"xs{k}", [128, 2048], BF16) for k in range(NB)]
        btok = [ar.alloc(f"btok{k}", [128, 512], BF16) for k in range(NB)]
        atok = [ar.alloc(f"atok{k}", [128, 32], F32) for k in range(NB)]
        R = [ar.alloc(f"R{k}", [128, 32, 128], BF16) for k in range(NB)]
        wend = [ar.alloc(f"wend{k}", [128, 32], F32) for k in range(NB)]
        eL = [ar.alloc(f"eL{k}", [128, 32], F32) for k in range(NB)]
        dtw = [ar.alloc(f"dtw{k}", [128, 32], F32) for k in range(NB)]
        xdt = [ar.alloc(f"xdt{k}", [128, 32, 64], BF16) for k in range(NB)]
        xdtw = [ar.alloc(f"xdtw{k}", [128, 32, 64], BF16) for k in range(NB)]
        cbs = [ar.alloc(f"cbs{k}", [128, 4, 128], BF16) for k in range(NB)]
        dec = [ar.alloc(f"dec{k}", [128, 4, 128], BF16) for k in range(2)]
        Mall = [ar.alloc(f"Mall{k}", [128, 32, 128], BF16) for k in range(NB)]
        E4 = [ar.alloc(f"E4{k}", [128, 512], F32) for k in range(2)]
        t4 = [ar.alloc(f"t4{k}", [128, 512], F32) for k in range(2)]
        yout = [ar.alloc(f"yout{k}", [128, 16, 128], BF16) for k in range(NB)]
        supers = []
        for (s0, s1) in [(0, TCTX), (TCTX, T)]:
            tl = [(t, min(512, s1 - t)) for t in range(s0, s1, 512)]
            if not fwd:
                tl = tl[::-1]
            supers += tl
        dcol = slice(32 * d, 32 * d + 32)
        ydst_name = "yf" if fwd else "ysum"

        def issue_loads(si):
            T0, NN = supers[si]
            k = si % 2
            P.dma("sp", f"xcs4{k}", ("dma_start", dict(out=xcs4[k][:, :, :NN], in_=dr["xbc"][:, T0:T0 + NN].rearrange("(c p) t -> p c t", p=128))),
                  writes=[("xcs4", k)])
            P.dma("sp", f"dtk4{k}", ("dma_start", dict(out=dtk4[k][:, 0:NN // 128, :], in_=dr["dtok"][T0:T0 + NN, :].rearrange("(s p) c -> p s c", p=128))),
                  reads=[("dram", "dtok", T0)], writes=[("dtk4", k)])

        chunks = []
        for si, (T0, NN) in enumerate(supers):
            subs = list(range(NN // 128))
            if not fwd:
                subs = subs[::-1]
            for sub in subs:
                chunks.append((si, sub, T0 + 128 * sub, T0 < TCTX))

        def xc_of(ci_):
            si, sub, t0, _ = chunks[ci_]
            k4 = si % 2
            return (lambda ch: xcs4[k4][:, ch, sub * 128:(sub + 1) * 128]), ("xcs4", k4), dtk4[k4][:, sub, dcol], ("dtk4", k4)

        def stage_a(ci_):
            b = ci_ % NB
            xc_, xck, dt_, dkk = xc_of(ci_)
            for q in range(5):
                ps, pk = self.ps[q % 2], ("ps", q % 2)
                calls = [("matmul", dict(out=ps[:, cc * 128:(cc + 1) * 128], lhsT=xc_(4 * q + cc), rhs=self.ident_bf[:], start=True, stop=True))
                         for cc in range(4)]
                P.op("pe", calls, reads=[xck, "ident_bf"], writes=[pk])
                if q < 4:
                    P.op("act", ("activation", dict(out=xs[b][:, q * 512:(q + 1) * 512], in_=ps[:, :], func=AF.Copy)), reads=[pk], writes=[("xs", b, q)])
                else:
                    P.op("act", ("activation", dict(out=btok[b][:], in_=ps[:, :], func=AF.Copy)), reads=[pk], writes=[("btok", b)])
            xskeys = [("xs", b, q) for q in range(4)]
            P.op("dve", ("tensor_tensor", dict(out=atok[b][:], in0=dt_, in1=A_bc[:], op=ALU.mult)), reads=[dkk, "A_bc"], writes=[("atok", b)])
            P.op("dve", ("tensor_tensor", dict(out=R[b][:], in0=atok[b][:].unsqueeze(2).to_broadcast([128, 32, 128]),
                                               in1=tri[:].unsqueeze(1).to_broadcast([128, 32, 128]), op=ALU.mult)),
                 reads=[("atok", b), "tri"], writes=[("R", b)])
            pc, pck = self.ps[2], ("ps", 2)
            P.op("pe", [("matmul", dict(out=pc[:, 128:160], lhsT=Um[:], rhs=atok[b][:], start=True, stop=True)),
                        ("matmul", dict(out=pc[:, 160:192], lhsT=onesf[:], rhs=atok[b][:], start=True, stop=True))],
                 reads=[("atok", b), "Um", "onesf"], writes=[pck])
            P.op("act", [("activation", dict(out=wend[b][:], in_=pc[:, 128:160], func=AF.Exp)),
                         ("activation", dict(out=eL[b][:], in_=pc[:, 160:192], func=AF.Exp))], reads=[pck], writes=[("wend", b), ("eL", b)])
            P.op("dve", ("tensor_tensor", dict(out=dtw[b][:], in0=dt_, in1=wend[b][:], op=ALU.mult)), reads=[dkk, ("wend", b)], writes=[("dtw", b)])
            xs3 = xs[b][:].rearrange("p (h q) -> p h q", q=64)
            P.op("dve", ("tensor_tensor", dict(out=xdt[b][:], in0=xs3, in1=dt_.unsqueeze(2).to_broadcast([128, 32, 64]), op=ALU.mult)),
                 reads=xskeys + [dkk], writes=[("xdt", b)])
            P.op("dve", ("tensor_tensor", dict(out=xdtw[b][:], in0=xs3, in1=dtw[b][:].unsqueeze(2).to_broadcast([128, 32, 64]), op=ALU.mult)),
                 reads=xskeys + [("dtw", b)], writes=[("xdtw", b)])
            pcb, pcbk = self.ps[3], ("ps", 3)
            calls = [("matmul", dict(out=pcb[:, g * 128:(g + 1) * 128], lhsT=xc_(16 + g), rhs=xc_(20 + g), start=True, stop=True)) for g in range(4)]
            P.op("pe", calls, reads=[xck], writes=[pcbk])
            P.op("dve", ("tensor_tensor", dict(out=cbs[b][:], in0=pcb[:, :].rearrange("p (a b) -> p a b", b=128),
                                               in1=tri[:].unsqueeze(1).to_broadcast([128, 4, 128]), op=ALU.mult)), reads=[pcbk, "tri"], writes=[("cbs", b)])
            for q in range(8):
                g = q // 2
                h0 = 4 * q
                ps, pk = self.ps[6 + q % 2], ("ps", 6 + q % 2)
                P.op("pe", ("matmul", dict(out=ps[:, :], lhsT=Ubf[:], rhs=R[b][:, h0:h0 + 4, :], start=True, stop=True)), reads=[("R", b), "Ubf"], writes=[pk])
                dq, dqk = dec[q % 2], ("dec", q % 2)
                P.op("act", ("activation", dict(out=dq[:].rearrange("p a b -> p (a b)"), in_=ps[:, :], func=AF.Exp)), reads=[pk], writes=[dqk])
                P.op("dve", ("tensor_tensor", dict(out=Mall[b][:, h0:h0 + 4, :], in0=dq[:], in1=cbs[b][:, g, :].unsqueeze(1).to_broadcast([128, 4, 128]), op=ALU.mult)),
                     reads=[dqk, ("cbs", b)], writes=[("Mall", b, q)])

        def stage_b(ci_):
            b = ci_ % NB
            si, sub, t0, is_ctx = chunks[ci_]
            xc_, xck, dt_, dkk = xc_of(ci_)
            for g in range(4):
                pe_, pek = self.ps[4], ("ps", 4)
                calls = [("matmul", dict(out=pe_[64 * e:64 * e + 64, :], lhsT=self.ones_bf[:, 64 * e:64 * e + 64], rhs=R[b][:, 8 * g + e:8 * g + 8:2, :],
                                         start=True, stop=True)) for e in range(2)]
                P.op("pe", calls, reads=[("R", b), "ones_bf"], writes=[pek])
                e4, e4k = E4[g % 2], ("E4", g % 2)
                P.op("act", ("activation", dict(out=e4[:], in_=pe_[:, :], func=AF.Exp)), reads=[pek], writes=[e4k])
                pg, pgk = self.ps[5], ("ps", 5)
                calls = [("matmul", dict(out=pg[:, hp * 128:(hp + 1) * 128], lhsT=Sbf[:, (4 * g + hp) * 128:(4 * g + hp + 1) * 128], rhs=xc_(20 + g),
                                         start=True, stop=True)) for hp in range(4)]
                P.op("pe", calls, reads=[("Sbf", g), xck], writes=[pgk])
                py, pyk = self.ps[2 + g % 2], ("ps", 2 + g % 2)
                calls = []
                for hp in range(4):
                    for e in range(2):
                        h = 8 * g + 2 * hp + e
                        calls.append(("matmul", dict(out=py[64 * e:64 * e + 64, hp * 128:(hp + 1) * 128], lhsT=xdt[b][:, h, :], rhs=Mall[b][:, h, :],
                                                     start=True, stop=True)))
                P.op("pe", calls, reads=[("xdt", b), ("Mall", b, 2 * g), ("Mall", b, 2 * g + 1)], writes=[pyk])
                tq, tqk = t4[g % 2], ("t4", g % 2)
                P.op("dve", ("tensor_tensor", dict(out=tq[:], in0=pg[:, :], in1=e4[:], op=ALU.mult)), reads=[pgk, e4k], writes=[tqk])
                yo = yout[b][:, 4 * g:4 * g + 4, :]
                P.op("dve", ("tensor_tensor", dict(out=yo, in0=py[:, :].rearrange("p (a b) -> p a b", b=128), in1=tq[:].rearrange("p (a b) -> p a b", b=128), op=ALU.add)),
                     reads=[pyk, tqk], writes=[("yout", b, g)])
            if store_ctx or not is_ctx:
                P.dma("sp", f"yst{b}", ("dma_start", dict(out=dr[ydst_name][:, t0:t0 + 128].rearrange("(c p) t -> p c t", p=128), in_=yout[b][:])),
                      reads=[("yout", b, g) for g in range(4)], writes=[("dram", ydst_name, t0)])
            for g in range(4):
                pS, pSk = self.ps[4 + g % 2], ("ps", 4 + g % 2)
                P.op("pe", ("matmul", dict(out=pS[:, :], lhsT=btok[b][:, g * 128:(g + 1) * 128], rhs=xdtw[b][:, 8 * g:8 * g + 8, :], start=True, stop=True)),
                     reads=[("btok", b), ("xdtw", b)], writes=[pSk])
                Sg = S[:, g * 512:(g + 1) * 512]
                P.op("dve", ("tensor_tensor", dict(out=Sg.rearrange("p (h q) -> p h q", q=64), in0=Sg.rearrange("p (h q) -> p h q", q=64),
                                                    in1=eL[b][:, 8 * g:8 * g + 8].unsqueeze(2).to_broadcast([128, 8, 64]), op=ALU.mult)),
                     reads=[("S", g), ("eL", b), ("Sbf", g)], writes=[("S", g)])
                P.op("dve", ("tensor_tensor", dict(out=Sg, in0=Sg, in1=pS[:, :], op=ALU.add)), reads=[("S", g), pSk], writes=[("S", g)])
                P.op("act", ("activation", dict(out=Sbf[:, g * 512:(g + 1) * 512], in_=Sg, func=AF.Copy)), reads=[("S", g)], writes=[("Sbf", g)])

        issue_loads(0)
        loaded = {0}
        stage_a(0)
        for ci_ in range(len(chunks)):
            for look in (1, 2, 3, 4, 5):
                if ci_ + look < len(chunks):
                    sj = chunks[ci_ + look][0]
                    if sj not in loaded and sj <= chunks[ci_][0] + 1:
                        issue_loads(sj)
                        loaded.add(sj)
            if ci_ + 1 < len(chunks):
                stage_a(ci_ + 1)
            stage_b(ci_)
        P.barrier()

    @staticmethod
    def _tile_of(t):
        if t < TCTX:
            return 0
        return TCTX + ((t - TCTX) // 512) * 512

    def ssd_out_phase(self, i, src, dst, include_ctx=True):
        nc, P, dr = self.nc, self.P, self.dram
        js = i // 3
        ar = self.arena
        ar.reset()
        NT = 256
        wo = ar.alloc("wo", [128, 16, D], BF16)
        yb = ar.alloc("yb", [128, 16, NT], F32)
        yfb = [ar.alloc(f"yfb{k}", [128, 16, NT], BF16) for k in range(2)]
        ybb = [ar.alloc(f"ybb{k}", [128, 16, NT], BF16) for k in range(2)]
        xt = [ar.alloc(f"xt{k}", [128, 16, NT], BF16) for k in range(2)]
        zb = [ar.alloc(f"zb{k}", [128, 16, NT], BF16) for k in range(2)]
        hbs = [ar.alloc(f"h{k}", [128, KC, NT], F32) for k in range(2)]
        sq = ar.alloc("sq", [128, 16, NT], BF16)
        gn = [ar.alloc(f"gn{k}", [128, 16, NT], BF16) for k in range(2)]
        rstd = ar.alloc("rstd", [128, NT], F32)
        D_bc = ar.alloc("D_bc", [128, 32], F32)
        Dpp = ar.alloc("Dpp", [128, 16], F32)
        wsrc = dr["ssd_w_out"][js].rearrange("(k p) m -> p k m", p=128)
        for kk in range(0, 16, 4):
            P.dma("pool", "wo", ("dma_start", dict(out=wo[:, kk:kk + 4, :], in_=wsrc[:, kk:kk + 4, :])), writes=[("wo", kk)])
        wkeys = [("wo", kk) for kk in range(0, 16, 4)]
        P.dma("sp", "dbc", ("dma_start", dict(out=D_bc[:], in_=dr["ssd_d"][js:js + 1, :].partition_broadcast(128))), writes=["D_bc"])
        P.op("dve", [("tensor_copy", dict(out=Dpp[0:64, :], in_=D_bc[0:64, 0:32:2])), ("tensor_copy", dict(out=Dpp[64:128, :], in_=D_bc[64:128, 1:32:2]))],
             reads=["D_bc"], writes=["Dpp"])
        tl = []
        if include_ctx:
            tl.append((0, NT, 1))
        tl += [(TCTX + NT * k, NT, 0) for k in range(TLAT // NT)]

        def loads(ti):
            t0, N, j = tl[ti]
            k = ti % 2
            ykeys = [("dram", "ysum", tt) for tt in range(t0, t0 + N, 128)] + [("dram", "yf", tt) for tt in range(t0, t0 + N, 128)]
            P.dma("sp", f"yload{k}", [("dma_start", dict(out=ybb[k][:, :, :N], in_=dr["ysum"][:, t0:t0 + N].rearrange("(c p) t -> p c t", p=128))),
                                      ("dma_start", dict(out=yfb[k][:, :, :N], in_=dr["yf"][:, t0:t0 + N].rearrange("(c p) t -> p c t", p=128)))],
                  reads=ykeys, writes=[("ybb", k), ("yfb", k)])
            P.dma("sp", f"zload{k}", [("dma_start", dict(out=xt[k][:, :, :N], in_=dr["xbc"][0:2048, t0:t0 + N].rearrange("(c p) t -> p c t", p=128))),
                                      ("dma_start", dict(out=zb[k][:, :, :N], in_=dr["sz"][:, t0:t0 + N].rearrange("(c p) t -> p c t", p=128)))],
                  writes=[("zb", k), ("xt", k)])
            P.dma("sp", f"hload{k}", ("dma_start", dict(out=hbs[k][:, :, :N], in_=src[:, t0:t0 + N].rearrange("(kc p) t -> p kc t", p=128))),
                  writes=[("h", k)])

        loads(0)
        for ti, (t0, N, j) in enumerate(tl):
            k = ti % 2
            hb, hk = hbs[k], ("h", k)
            if ti + 1 < len(tl):
                loads(ti + 1)
            for half in range(2):
                cs = slice(8 * half, 8 * half + 8)
                P.op("dve", ("tensor_tensor", dict(out=yb[:, cs, :N], in0=yfb[k][:, cs, :N], in1=ybb[k][:, cs, :N], op=ALU.add)),
                     reads=[("yfb", k), ("ybb", k)], writes=["yb"])
            for c in range(16):
                P.op("dve", ("scalar_tensor_tensor", dict(out=yb[:, c, :N], in0=xt[k][:, c, :N], scalar=Dpp[:, c:c + 1], in1=yb[:, c, :N],
                                                          op0=ALU.mult, op1=ALU.add)), reads=["yb", ("xt", k), "Dpp"], writes=["yb"])
            for half in range(2):
                cs = slice(8 * half, 8 * half + 8)
                P.op("dve", ("tensor_tensor", dict(out=yb[:, cs, :N], in0=yb[:, cs, :N], in1=zb[k][:, cs, :N], op=ALU.mult)), reads=["yb", ("zb", k)], writes=["yb"])
                P.op("act", ("activation", dict(out=sq[:, cs, :N], in_=yb[:, cs, :N], func=AF.Square)), reads=["yb"], writes=[("sq", half)])
            calls = [("matmul", dict(out=self.ps[0][:, :N], lhsT=self.ones_bf[:], rhs=sq[:, c, :N], start=(c == 0), stop=(c == 15))) for c in range(16)]
            P.op("pe", calls, reads=[("sq", 0), ("sq", 1), "ones_bf"], writes=[("ps", 0)])
            self.rstd_from(self.ps[0], ("ps", 0), rstd, N, 2048)
            g_ = gn[k]
            for c in range(16):
                P.op("dve", ("scalar_tensor_tensor", dict(out=g_[:, c, :N], in0=yb[:, c, :N], scalar=self.normwT[:, js * 16 + c:js * 16 + c + 1],
                                                          in1=rstd[:, :N], op0=ALU.mult, op1=ALU.mult)),
                     reads=["yb", "rstd", "normwT"], writes=[("gn", k, c)])
            gkeys = [("gn", k, c) for c in range(16)]
            for mc in range(KC):
                ps, pk = self.ps[1 + mc % 4], ("ps", 1 + mc % 4)
                calls = [("matmul", dict(out=ps[:, :N], lhsT=wo[:, kk, mc * 128:(mc + 1) * 128], rhs=g_[:, kk, :N],
                                         start=(kk == 0), stop=(kk == 15))) for kk in range(16)]
                P.op("pe", calls, reads=gkeys + wkeys, writes=[pk])
                P.op("dve", ("scalar_tensor_tensor", dict(out=hb[:, mc, :N], in0=ps[:, :N], scalar=self.mod[:, i, 2 * 8 + mc, j:j + 1],
                                                          in1=hb[:, mc, :N], op0=ALU.mult, op1=ALU.add)),
                     reads=[pk, hk, "mod"], writes=[hk])
            P.dma("sp", f"hload{k}", ("dma_start", dict(out=dst[:, t0:t0 + N].rearrange("(kc p) t -> p kc t", p=128), in_=hb[:, :, :N])),
                  reads=[hk], writes=[("dram", dst.name, t0)])
        P.barrier()

    def attn_proj_phase(self, i, src):
        nc, P, dr = self.nc, self.P, self.dram
        ar = self.arena
        ar.reset()
        wq = ar.alloc("wq", [128, KC, 1024], BF16)
        wqs = ar.alloc("wqs", [128, KC, 1024], BF16)
        wk = ar.alloc("wk", [128, KC, 4, 128], BF16)
        wks = ar.alloc("wks", [128, KC, 4, 128], BF16)
        wv = ar.alloc("wv", [128, KC, 256], BF16)
        hb = ar.alloc("h0", [128, KC, 512], F32)
        X = [ar.alloc(f"X{k}", [128, KC, 512], BF16) for k in range(2)]
        rstd = ar.alloc("rstd", [128, 512], F32)
        tmp = [ar.alloc(f"tmp{k}", [128, 512], F32) for k in range(2)]
        rp = [ar.alloc(f"rp{k}", [128, 2, 512], F32) for k in range(2)]
        t1 = [ar.alloc(f"t1{k}", [128, 512], F32) for k in range(2)]
        t2 = [ar.alloc(f"t2{k}", [128, 512], F32) for k in range(2)]
        qsb = [ar.alloc(f"qsb{k}", [128, 8, 512], BF16) for k in range(2)]
        ksb = [ar.alloc(f"ksb{k}", [128, 4, 512], BF16) for k in range(2)]
        vsb = [ar.alloc(f"vsb{k}", [128, 4, 256], BF16) for k in range(2)]
        W = dr["attn_w_qkv"]
        wsrc = W.rearrange("(kc p) f -> p kc f", p=128)
        P.dma("pool", "win", [("dma_start", dict(out=wq[:, 0:4, :], in_=wsrc[:, 0:4, 0:1024])),
                              ("dma_start", dict(out=wq[:, 4:8, :], in_=wsrc[:, 4:8, 0:1024])),
                              ("dma_start", dict(out=wv[:], in_=wsrc[:, :, 1280:1536]))], writes=["wq", "wv"])
        ksrc = wsrc[:, :, 1024:1280].rearrange("p kc (g d) -> p kc g d", g=4)
        P.dma("pool", "wk", [("dma_start", dict(out=wk[:, kc, :, 64 * e:64 * e + 64], in_=ksrc[:, kc, :, :])) for kc in range(KC) for e in range(2)],
              writes=["wk"])
        def sw(t):
            return t.rearrange("p kc (hb half f) -> p kc hb half f", half=2, f=16)
        P.op("act", [("activation", dict(out=sw(wqs[:])[:, :, :, 0, :], in_=sw(wq[:])[:, :, :, 1, :], func=AF.Copy)),
                     ("activation", dict(out=sw(wqs[:])[:, :, :, 1, :], in_=sw(wq[:])[:, :, :, 0, :], func=AF.Copy))],
             reads=["wq"], writes=["wqs"])
        def swk(t):
            return t.rearrange("p kc g (hb half f) -> p (kc g) hb half f", half=2, f=16)
        P.op("act", [("activation", dict(out=swk(wks[:])[:, :, :, 0, :], in_=swk(wk[:])[:, :, :, 1, :], func=AF.Copy)),
                     ("activation", dict(out=swk(wks[:])[:, :, :, 1, :], in_=swk(wk[:])[:, :, :, 0, :], func=AF.Copy))],
             reads=["wk"], writes=["wks"])
        wkeys = ["wq", "wqs", "wk", "wks", "wv"]
        tl_ = tiles_for(True)

        def do_norm(ti):
            t0, N, j = tl_[ti]
            P.dma("sp", "hload", ("dma_start", dict(out=hb[:, :, :N], in_=src[:, t0:t0 + N].rearrange("(kc p) t -> p kc t", p=128))),
                  reads=[("dram", src.name, t0)], writes=["h0"])
            ak_ = ("X", ti % 2)
            self.norm_mod(hb, "h0", X[ti % 2], ak_, N, i, 0, j, X[ti % 2], rstd, tmp, self.ps[0], ("ps", 0),
                          sq_extra_writes=[(ak_, kc) for kc in range(KC)])

        do_norm(0)
        for ti, (t0, N, j) in enumerate(tl_):
            a = X[ti % 2]
            akeys = [(("X", ti % 2), kc) for kc in range(KC)]
            lat = (j == 0)
            r = rp[ti % 2]
            rk = ("rp", ti % 2)
            if lat:
                P.dma("sp", f"rp{ti % 2}", ("dma_start", dict(out=r[:, :, :N], in_=dr["rope"][:, :, t0 - TCTX:t0 - TCTX + N])), writes=[rk])
            qs_, qk_ = qsb[ti % 2], ("qsb", ti % 2)
            ks_, kk_ = ksb[ti % 2], ("ksb", ti % 2)
            vs_, vk_ = vsb[ti % 2], ("vsb", ti % 2)
            n = 0
            for which, nchunk, wpl, wsw, dst_, dk_ in (("q", 8, wq, wqs, qs_, qk_), ("k", 4, wk, wks, ks_, kk_)):
                for c in range(nchunk):
                    pa_, pak = self.ps[1 + 2 * (n % 2)], ("ps", 1 + 2 * (n % 2))
                    pb_, pbk = self.ps[2 + 2 * (n % 2)], ("ps", 2 + 2 * (n % 2))
                    def wsl(w, kc):
                        return w[:, kc, c * 128:(c + 1) * 128] if which == "q" else w[:, kc, c, :]
                    calls = [("matmul", dict(out=pa_[:, :N], lhsT=wsl(wpl, kc), rhs=a[:, kc, :N], start=(kc == 0), stop=(kc == KC - 1)))
                             for kc in range(KC)]
                    P.op("pe", calls, reads=akeys + wkeys, writes=[pak])
                    if lat:
                        calls = [("matmul", dict(out=pb_[:, :N], lhsT=wsl(wsw, kc), rhs=a[:, kc, :N], start=(kc == 0), stop=(kc == KC - 1)))
                                 for kc in range(KC)]
                        P.op("pe", calls, reads=akeys + wkeys, writes=[pbk])
                        x1, x1k = t1[n % 2], ("t1", n % 2)
                        x2, x2k = t2[n % 2], ("t2", n % 2)
                        P.op("dve", ("tensor_tensor", dict(out=x1[:, :N], in0=pa_[:, :N], in1=r[:, 0, :N], op=ALU.mult)), reads=[pak, rk], writes=[x1k])
                        P.op("dve", ("tensor_tensor", dict(out=x2[:, :N], in0=pb_[:, :N], in1=r[:, 1, :N], op=ALU.mult)), reads=[pbk, rk], writes=[x2k])
                        P.op("pool", ("tensor_tensor", dict(out=dst_[:, c, :N], in0=x1[:, :N], in1=x2[:, :N], op=ALU.add)), reads=[x1k, x2k], writes=[dk_])
                    else:
                        P.op("act", ("activation", dict(out=dst_[:, c, :N], in_=pa_[:, :N], func=AF.Copy)), reads=[pak], writes=[dk_])
                    n += 1
            if ti + 1 < len(tl_):
                do_norm(ti + 1)
            for sub in range(N // 128):
                ps, pk = self.ps[5 + sub % 2], ("ps", 5 + sub % 2)
                calls = [("matmul", dict(out=ps[:, 0:256], lhsT=a[:, kc, sub * 128:(sub + 1) * 128], rhs=wv[:, kc, :],
                                         start=(kc == 0), stop=(kc == KC - 1))) for kc in range(KC)]
                P.op("pe", calls, reads=akeys + wkeys, writes=[pk])
                P.op("act", ("activation", dict(out=vs_[:, sub, :], in_=ps[:, 0:256], func=AF.Copy)), reads=[pk], writes=[vk_])
            P.dma("sp", f"qst{ti % 2}", ("dma_start", dict(out=dr["qT"][:, t0:t0 + N].rearrange("(c p) t -> p c t", p=128), in_=qs_[:, :, :N])),
                  reads=[qk_], writes=[("dram", "qT", t0)])
            P.dma("sp", f"kst{ti % 2}", ("dma_start", dict(out=dr["kT"][:, t0:t0 + N].rearrange("(c p) t -> p c t", p=128), in_=ks_[:, :, :N])),
                  reads=[kk_], writes=[("dram", "kT", t0)])
            P.dma("sp", f"vst{ti % 2}", ("dma_start", dict(out=dr["vtok"][t0:t0 + N, :].rearrange("(b p) c -> p b c", p=128), in_=vs_[:, 0:N // 128, :])),
                  reads=[vk_], writes=[("dram", "vtok", t0)])
        P.barrier()

    def attn_core_phase(self):
        nc, P, dr = self.nc, self.P, self.dram
        ar = self.arena
        ar.reset()
        kc_ = ar.alloc("kctx", [128, 4, 256], BF16)
        vc_ = ar.alloc("vctx", [128, 2, 256], BF16)
        msk = ar.alloc("msk", [128, 2, 2, 128], BF16)
        esk = ar.alloc("esk", [128, 16], F32)
        q4 = [ar.alloc(f"q4{k}", [128, 8, 512], BF16) for k in range(2)]
        kb4 = [ar.alloc(f"kb4{k}", [128, 4, 768], BF16) for k in range(2)]
        vb4 = [ar.alloc(f"vb4{k}", [128, 6, 256], BF16) for k in range(2)]
        ot4 = [ar.alloc(f"ot4{k}", [128, 8, 512], BF16) for k in range(2)]
        pt = [ar.alloc(f"pt{k}", [128, 5, 512], BF16) for k in range(2)]
        rden = [ar.alloc(f"rden{k}", [128, 4, 128], F32) for k in range(2)]
        cd = dr["consts"]
        P.dma("pool", "acst", [("dma_start", dict(out=msk[:, 0, 0, :], in_=cd[:, 256:384])), ("dma_start", dict(out=msk[:, 0, 1, :], in_=cd[:, 256:384])),
                               ("dma_start", dict(out=msk[:, 1, 0, :], in_=cd[:, 384:512])), ("dma_start", dict(out=msk[:, 1, 1, :], in_=cd[:, 384:512]))],
              writes=["msk"])
        P.dma("sp", "acst2", [("dma_start", dict(out=esk[:], in_=dr["attn_sink"].partition_broadcast(128))),
                              ("dma_start", dict(out=kc_[:], in_=dr["kT"][:, 0:256].rearrange("(c p) t -> p c t", p=128))),
                              ("dma_start", dict(out=vc_[:], in_=dr["vtok"][0:256, :].rearrange("(b p) c -> p b c", p=128)))],
              reads=[("dram", "kT", 0), ("dram", "vtok", 0)], writes=["esk", "kctx", "vctx"])
        P.op("act", ("activation", dict(out=esk[:], in_=esk[:], func=AF.Exp)), reads=["esk"], writes=["esk"])
        gi = 0
        for ti, (t0, N, j) in enumerate(tiles_for(True)):
            lat = (j == 0)
            q_, qk_ = q4[ti % 2], ("q4", ti % 2)
            kb_, kbk = kb4[ti % 2], ("kb4", ti % 2)
            vb_, vbk = vb4[ti % 2], ("vb4", ti % 2)
            o_, ok_ = ot4[ti % 2], ("ot4", ti % 2)
            P.dma("sp", f"q4{ti % 2}", ("dma_start", dict(out=q_[:, :, :N], in_=dr["qT"][:, t0:t0 + N].rearrange("(c p) t -> p c t", p=128))),
                  reads=[("dram", "qT", t0)], writes=[qk_])
            if lat:
                lo = max(t0 - 128, TCTX)
                hi = min(t0 + 640, T)
                off = lo - (t0 - 128)
                nb0 = off // 128
                nbl = (hi - lo) // 128
                rk = [("dram", "kT", tt) for tt in (t0 - 512, t0, t0 + 512) if TCTX <= tt < T]
                rv = [("dram", "vtok", tt) for tt in (t0 - 512, t0, t0 + 512) if TCTX <= tt < T]
                P.dma("sp", f"kb4{ti % 2}", ("dma_start", dict(out=kb_[:, :, off:off + (hi - lo)], in_=dr["kT"][:, lo:hi].rearrange("(c p) t -> p c t", p=128))),
                      reads=rk, writes=[kbk])
                P.dma("sp", f"vb4{ti % 2}", ("dma_start", dict(out=vb_[:, nb0:nb0 + nbl, :], in_=dr["vtok"][lo:hi, :].rearrange("(b p) c -> p b c", p=128))),
                      reads=rv, writes=[vbk])
            for qb in range(N // 128):
                kbl = [("c", 0, None), ("c", 1, None)]
                if lat:
                    pos = t0 + qb * 128
                    if pos - 128 >= TCTX:
                        kbl.append(("b", qb, 0))
                    kbl.append(("b", qb + 1, None))
                    if pos + 128 < T:
                        kbl.append(("b", qb + 2, 1))
                nkb = len(kbl)
                for g in range(4):
                    p_, pk_ = pt[gi % 2], ("pt", gi % 2)
                    for bi, (kind, bidx, mi) in enumerate(kbl):
                        rds = [qk_, "msk", "ident_bf"] + (["kctx"] if kind == "c" else [kbk])
                        for e in range(2):
                            ps, pk = self.ps[2 * (bi % 2) + e], ("ps", 2 * (bi % 2) + e)
                            calls = []
                            kT_ap = (kc_[64 * e:64 * e + 64, g, bidx * 128:(bidx + 1) * 128] if kind == "c"
                                     else kb_[64 * e:64 * e + 64, g, bidx * 128:(bidx + 1) * 128])
                            calls.append(("matmul", dict(out=ps[:, 0:256], lhsT=kT_ap,
                                                         rhs=q_[64 * e:64 * e + 64, 2 * g:2 * g + 2, qb * 128:(qb + 1) * 128],
                                                         start=True, stop=(mi is None))))
                            if mi is not None:
                                calls.append(("matmul", dict(out=ps[:, 0:256], lhsT=self.ident_bf[:],
                                                             rhs=msk[:, mi, :, :], start=False, stop=True)))
                            P.op("pe", calls, reads=rds, writes=[pk])
                            P.op("act", ("activation", dict(out=p_[:, bi, 256 * e:256 * e + 256], in_=ps[:, 0:256], func=AF.Exp, scale=0.125)),
                                 reads=[pk], writes=[(pk_, bi, e)])
                    pkeys = [(pk_, bi, e) for bi in range(nkb) for e in range(2)]
                    pd, pdk = self.ps[4 + gi % 2], ("ps", 4 + gi % 2)
                    calls = [("matmul", dict(out=pd[:, :], lhsT=self.ones_bf[:], rhs=p_[:, bi, :], start=(bi == 0), stop=(bi == nkb - 1)))
                             for bi in range(nkb)]
                    P.op("pe", calls, reads=pkeys + ["ones_bf"], writes=[pdk])
                    rd, rdk = rden[gi % 2], ("rden", gi % 2)
                    P.op("dve", [("tensor_tensor", dict(out=rd[:], in0=pd[:, :].rearrange("p (h q) -> p h q", h=4),
                                                       in1=esk[:, 4 * g:4 * g + 4].unsqueeze(2).to_broadcast([128, 4, 128]), op=ALU.add)),
                                 ("reciprocal", dict(out=rd[:], in_=rd[:]))], reads=[pdk, "esk"], writes=[rdk])
                    po, pok = self.ps[6 + gi % 2], ("ps", 6 + gi % 2)
                    calls = []
                    for cc in range(2):
                        for e in range(2):
                            colb = e * 2 + cc
                            for bi, (kind, bidx, mi) in enumerate(kbl):
                                v_ap = (vc_[:, bidx, g * 64:(g + 1) * 64] if kind == "c" else vb_[:, bidx, g * 64:(g + 1) * 64])
                                calls.append(("matmul", dict(out=po[64 * e:64 * e + 64, cc * 128:(cc + 1) * 128], lhsT=v_ap,
                                                             rhs=p_[:, bi, colb * 128:(colb + 1) * 128], start=(bi == 0), stop=(bi == nkb - 1))))
                    P.op("pe", calls, reads=pkeys + ["vctx", vbk], writes=[pok])
                    calls = []
                    for cc in range(2):
                        for e in range(2):
                            colb = e * 2 + cc
                            calls.append(("tensor_tensor", dict(out=o_[64 * e:64 * e + 64, 2 * g + cc, qb * 128:(qb + 1) * 128],
                                                                in0=po[64 * e:64 * e + 64, cc * 128:(cc + 1) * 128],
                                                                in1=rd[64 * e:64 * e + 64, colb, :], op=ALU.mult)))
                    P.op("dve", calls, reads=[pok, rdk], writes=[(ok_, g, qb)])
                    gi += 1
            P.dma("sp", f"ot4{ti % 2}", ("dma_start", dict(out=dr["ymix"][0:1024, t0:t0 + N].rearrange("(c p) t -> p c t", p=128), in_=o_[:, :, :N])),
                  reads=[(ok_, g, qb) for g in range(4) for qb in range(N // 128)], writes=[("dram", "ymix", t0)])
        P.barrier()

    def outproj_phase(self, i, src, dst, w_ap, kin, include_ctx=True):
        nc, P, dr = self.nc, self.P, self.dram
        ar = self.arena
        ar.reset()
        nk = kin // 128
        wo = ar.alloc("wo", [128, nk, D], BF16)
        hb = [ar.alloc(f"h{k}", [128, KC, 512], F32) for k in range(2)]
        yb = [ar.alloc(f"y{k}", [128, nk, 512], BF16) for k in range(2)]
        wsrc = w_ap.rearrange("(k p) m -> p k m", p=128)
        for kk in range(0, nk, 4):
            P.dma("pool", "wo", ("dma_start", dict(out=wo[:, kk:kk + 4, :], in_=wsrc[:, kk:kk + 4, :])), writes=[("wo", kk)])
        wkeys = [("wo", kk) for kk in range(0, nk, 4)]
        ymix = dr["ymix"]
        tl_ = tiles_for(include_ctx)

        def loads(ti):
            t0, N, j = tl_[ti]
            P.dma("sp", f"yload{ti % 2}", ("dma_start", dict(out=yb[ti % 2][:, :, :N], in_=ymix[0:kin, t0:t0 + N].rearrange("(k p) t -> p k t", p=128))),
                  reads=[("dram", "ymix", t0)], writes=[("y", ti % 2)])
            P.dma("sp", f"hload{ti % 2}", ("dma_start", dict(out=hb[ti % 2][:, :, :N], in_=src[:, t0:t0 + N].rearrange("(kc p) t -> p kc t", p=128))),
                  reads=[("dram", src.name, t0)], writes=[("h", ti % 2)])

        loads(0)
        for ti, (t0, N, j) in enumerate(tl_):
            h, hk = hb[ti % 2], ("h", ti % 2)
            y, yk = yb[ti % 2], ("y", ti % 2)
            if ti + 1 < len(tl_):
                loads(ti + 1)
            for mc in range(KC):
                ps, pk = self.ps[mc % 4], ("ps", mc % 4)
                calls = [("matmul", dict(out=ps[:, :N], lhsT=wo[:, kk, mc * 128:(mc + 1) * 128], rhs=y[:, kk, :N],
                                         start=(kk == 0), stop=(kk == nk - 1))) for kk in range(nk)]
                P.op("pe", calls, reads=[yk] + wkeys, writes=[pk])
                P.op("dve", ("scalar_tensor_tensor", dict(out=h[:, mc, :N], in0=ps[:, :N], scalar=self.mod[:, i, 2 * 8 + mc, j:j + 1],
                                                          in1=h[:, mc, :N], op0=ALU.mult, op1=ALU.add)),
                     reads=[pk, hk, "mod"], writes=[hk])
            P.dma("sp", f"hstore{ti % 2}", ("dma_start", dict(out=dst[:, t0:t0 + N].rearrange("(kc p) t -> p kc t", p=128), in_=h[:, :, :N])),
                  reads=[hk], writes=[("dram", dst.name, t0)])
        P.barrier()

    def build(self):
        self.setup()
        self.prologue()
        dr = self.dram
        for ph in self.cfg["phases"]:
            kind = ph[0]
            if kind == "ffn":
                _, i, srcn, final = ph
                self.ffn_phase(i, dr[srcn], dr["h"], final=final)
            elif kind == "ssd_proj":
                self.ssd_proj_phase(ph[1], dr[ph[2]])
            elif kind == "ssd_scan":
                self.ssd_scan_phase(ph[1], ph[2], store_ctx=ph[3])
            elif kind == "ssd_out":
                self.ssd_out_phase(ph[1], dr[ph[2]], dr["h"], include_ctx=ph[3])
            elif kind == "attn_proj":
                self.attn_proj_phase(ph[1], dr[ph[2]])
            elif kind == "attn_core":
                self.attn_core_phase()
            elif kind == "gmlp":
                _, i, srcn = ph
                self.gmlp_phase(i, dr[srcn])
            elif kind == "outproj":
                _, i, srcn, wname, kin, inc_ctx = ph
                w_ap = dr[wname] if wname != "ssd_w_out" else dr[wname][i // 3]
                self.outproj_phase(i, dr[srcn], dr["h"], w_ap, kin, include_ctx=inc_ctx)
        self.P.barrier()
        self.P.emit()
        return self.nc


def layer_phases(i, srcn):
    last = i == DEPTH - 1
    kind = i % 3
    ph = []
    if kind == 0:
        ph.append(("ssd_proj", i, srcn))
        ph.append(("ssd_scan", i, 0, True))
        ph.append(("ssd_scan", i, 1, not last))
        ph.append(("ssd_out", i, srcn, not last))
    if kind == 1:
        ph.append(("attn_proj", i, srcn))
        ph.append(("attn_core",))
        ph.append(("outproj", i, srcn, "attn_w_o", 1024, not last))
    if kind == 2:
        ph.append(("gmlp", i, srcn))
        ph.append(("outproj", i, srcn, "gmlp_w_out", 2048, not last))
    ph.append(("ffn", i, "h", last))
    return ph


FULL_CFG = dict(layers_mod=[0, 1, 2, 3],
                phases=[p for i in range(DEPTH) for p in layer_phases(i, "hin" if i == 0 else "h")],
                debug_outs=())


def host_inputs(inputs, b):
    x = np.asarray(inputs["x"][b], dtype=np.float32)
    ctx = np.asarray(inputs["ctx"][b], dtype=np.float32)
    hin = np.ascontiguousarray(np.concatenate([ctx, x], axis=0).T)
    c = np.asarray(inputs["c"][b], dtype=np.float32).reshape(KC, 128)
    cc = np.asarray(inputs["c_ctx"], dtype=np.float32).reshape(KC, 128)
    csrc = np.ascontiguousarray(np.stack([c, cc], axis=-1).transpose(1, 0, 2).reshape(128, KC * 2))
    cst = make_consts()
    consts = np.ascontiguousarray(np.concatenate([cst[k] for k in CONST_ORDER], axis=1))
    m = {
        "hin": hin, "csrc": csrc, "consts": consts,
        "w_mod": np.asarray(inputs["w_mod"], np.float32),
        "b_mod": np.ascontiguousarray(np.asarray(inputs["b_mod"], np.float32).reshape(DEPTH * 48, 128)),
        "norm_g": np.ascontiguousarray(np.asarray(inputs["norm_g"], np.float32).reshape(DEPTH * 2 * KC, 128)),
        "final_g": np.ascontiguousarray(np.asarray(inputs["final_g"], np.float32).reshape(KC, 128)),
        "ffn_w_in": np.asarray(inputs["ffn_w_in"], np.float32),
        "ffn_w_out": np.asarray(inputs["ffn_w_out"], np.float32),
        "gmlp_w_in": np.asarray(inputs["gmlp_w_in"][0], np.float32),
        "gmlp_ln_g": np.asarray(inputs["gmlp_ln_g"], np.float32).reshape(1, 2048),
        "gmlp_ln_b": np.asarray(inputs["gmlp_ln_b"], np.float32).reshape(1, 2048),
        "gmlp_w_s": np.asarray(inputs["gmlp_w_s"][0], np.float32),
        "gmlp_b_s": np.asarray(inputs["gmlp_b_s"], np.float32).reshape(1, 1024),
        "gmlp_w_out": np.asarray(inputs["gmlp_w_out"][0], np.float32),
        "attn_w_o": np.asarray(inputs["attn_w_o"][0], np.float32),
        "ssd_w_out": np.asarray(inputs["ssd_w_out"], np.float32),
        "ones2k": np.ones((1, 2048), np.float32),
        "ssd_w_in": np.asarray(inputs["ssd_w_in"], np.float32),
        "ssd_conv_w": np.ascontiguousarray(np.asarray(inputs["ssd_conv_w"], np.float32).reshape(240, 128)),
        "ssd_conv_b": np.ascontiguousarray(np.asarray(inputs["ssd_conv_b"], np.float32).reshape(48, 128)),
        "ssd_conv_b_row": np.asarray(inputs["ssd_conv_b"], np.float32),
        "ssd_norm_w": np.ascontiguousarray(np.asarray(inputs["ssd_norm_w"], np.float32).reshape(32, 128)),
        "ssd_a_log": np.ascontiguousarray(np.asarray(inputs["ssd_a_log"], np.float32).reshape(4, 32)),
        "ssd_dt_bias": np.ascontiguousarray(np.asarray(inputs["ssd_dt_bias"], np.float32).reshape(2, 64)),
        "ssd_d": np.asarray(inputs["ssd_d"], np.float32),
        "onehots": make_onehots(),
        "attn_w_qkv": np.asarray(inputs["attn_w_qkv"][0], np.float32),
        "attn_sink": np.ascontiguousarray(np.asarray(inputs["attn_sink"], np.float32).reshape(4, 2, 2).transpose(0, 2, 1).reshape(1, 16)),
        "rope": make_rope(),
    }
    return m


def kernel(**inputs):
    nc = Builder(FULL_CFG).build()
    in_maps = [host_inputs(inputs, b) for b in range(8)]
    res = run_bass_kernel_spmd(nc, in_maps, core_ids=list(range(8)))
    out = np.stack([np.ascontiguousarray(res.results[b]["out"].T) for b in range(8)], axis=0)
    return out.astype(np.float32)
```

```python
import numpy as np
from contextlib import ExitStack
import concourse.bass as bass
import concourse.mybir as mybir
from concourse.bass_utils import run_bass_kernel_spmd

F32 = mybir.dt.float32
BF16 = mybir.dt.bfloat16
AF = mybir.ActivationFunctionType
ALU = mybir.AluOpType

D = 1024
KC = 8
TCTX = 256
TLAT = 8192
T = TCTX + TLAT
DEPTH = 4
FFN_H = 2816
FJ = FFN_H // 128
SBUF_BASE = 16512
SBUF_LIMIT = 229344

SELF_WAIT = True


class Prog:
    CE = ["pe", "act", "dve", "pool"]

    def __init__(self, nc, es):
        self.nc = nc
        self.es = es
        self.ops = {e: [] for e in self.CE + ["sp"]}
        self.cnt = {e: 0 for e in self.CE}
        self.sems = {}
        for e in self.CE:
            self.sems["c_" + e] = es.enter_context(nc.semaphore("c_" + e))
        self.dcnt = {}
        self.known = {e: {} for e in self.CE + ["sp"]}
        self.last_w = {}
        self.readers = {}

    def _deps(self, eng, reads, writes):
        ev = {}

        def need(s, v):
            if ev.get(s, 0) < v:
                ev[s] = v

        for k in reads:
            e = self.last_w.get(k)
            if e is not None:
                need(*e)
        for k in writes:
            e = self.last_w.get(k)
            if e is not None:
                need(*e)
            for s, v in self.readers.get(k, {}).items():
                need(s, v)
        waits = []
        for s, v in ev.items():
            if (not SELF_WAIT) and s == "c_" + eng:
                continue
            if self.known[eng].get(s, 0) < v:
                self.known[eng][s] = v
                waits.append((s, v))
        return waits

    def _commit(self, event, reads, writes):
        s, v = event
        for k in reads:
            r = self.readers.setdefault(k, {})
            if r.get(s, 0) < v:
                r[s] = v
        for k in writes:
            self.last_w[k] = event
            self.readers[k] = {}

    def op(self, eng, calls, reads=(), writes=()):
        if isinstance(calls, tuple):
            calls = [calls]
        waits = self._deps(eng, reads, writes)
        self.cnt[eng] += 1
        event = ("c_" + eng, self.cnt[eng])
        self.ops[eng].append((waits, calls, "c", event[0]))
        self._commit(event, reads, writes)

    def dma(self, q, semkey, calls, reads=(), writes=()):
        if isinstance(calls, tuple):
            calls = [calls]
        name = "d_" + semkey
        if name not in self.sems:
            self.sems[name] = self.es.enter_context(self.nc.semaphore(name))
            self.dcnt[name] = 0
        waits = self._deps(q, reads, writes)
        self.dcnt[name] += 16 * len(calls)
        event = (name, self.dcnt[name])
        self.ops[q].append((waits, calls, "d", name))
        self._commit(event, reads, writes)

    def barrier(self):
        for e in self.CE + ["sp"]:
            waits = []
            allv = [("c_" + x, self.cnt[x]) for x in self.CE] + list(self.dcnt.items())
            for s, v in allv:
                if v > 0 and self.known[e].get(s, 0) < v:
                    if s == "c_" + e and not SELF_WAIT:
                        continue
                    self.known[e][s] = v
                    waits.append((s, v))
            if waits:
                self.ops[e].append((waits, [], "n", None))

    def emit(self):
        nc = self.nc
        sems = self.sems

        def run(eo, name):
            for waits, calls, kind, sname in self.ops[name]:
                for s, v in waits:
                    eo.wait_ge(sems[s], v)
                n = len(calls)
                for i, (m, kw) in enumerate(calls):
                    ins = getattr(eo, m)(**kw)
                    if kind == "d":
                        ins.then_inc(sems[sname], 16)
                    elif kind == "c" and i == n - 1:
                        ins.then_inc(sems[sname], 1)

        with nc.Block() as block:

            @block.tensor
            def _(e):
                run(e, "pe")

            @block.scalar
            def _(e):
                run(e, "act")

            @block.vector
            def _(e):
                run(e, "dve")

            @block.gpsimd
            def _(e):
                run(e, "pool")

            @block.sync
            def _(e):
                run(e, "sp")


class Arena:
    def __init__(self, nc, base, limit=SBUF_LIMIT):
        self.nc = nc
        self.base = base
        self.top = base
        self.limit = limit
        self.n = 0

    def reset(self):
        self.top = self.base

    def alloc(self, name, shape, dtype):
        esz = 4 if dtype == F32 else 2
        nbytes = int(np.prod(shape[1:])) * esz
        nbytes = (nbytes + 63) // 64 * 64
        off = self.top
        assert off + nbytes <= self.limit, f"SBUF overflow allocating {name}: {off}+{nbytes}"
        self.top += nbytes
        self.n += 1
        return self.nc.alloc_sbuf_tensor_at(f"{name}_{self.n}", list(shape), dtype, offset=off)


def make_consts():
    c = {}
    c["ident"] = np.eye(128, dtype=np.float32)
    c["ones"] = np.ones((128, 128), dtype=np.float32)
    k = np.arange(128)[:, None]
    q = np.arange(128)[None, :]
    c["mask_prev"] = np.where(k >= q, 0.0, -30000.0).astype(np.float32)
    c["mask_next"] = np.where(k <= q, 0.0, -30000.0).astype(np.float32)
    kk = np.arange(128)[:, None]
    ll = np.arange(128)[None, :]
    c["tri_f"] = (kk <= ll).astype(np.float32)
    c["tri_b"] = (kk >= ll).astype(np.float32)
    c["U_f"] = (kk > ll).astype(np.float32)
    c["U_b"] = (kk < ll).astype(np.float32)
    c["mneg_f"] = np.where(kk <= ll, 0.0, -1.0e5).astype(np.float32)
    c["mneg_b"] = np.where(kk >= ll, 0.0, -1.0e5).astype(np.float32)
    return c


CONST_ORDER = ["ident", "ones", "mask_prev", "mask_next", "tri_f", "tri_b", "U_f", "U_b", "mneg_f", "mneg_b"]


def make_onehots():
    oh1 = np.zeros((32, 32, 128), np.float32)
    for h in range(32):
        oh1[h, h, :] = 1.0
    oh2 = np.zeros((32, 16, 128), np.float32)
    for hp in range(16):
        oh2[2 * hp, hp, 0:64] = 1.0
        oh2[2 * hp + 1, hp, 64:128] = 1.0
    return np.ascontiguousarray(np.concatenate([oh1.reshape(32, -1), oh2.reshape(32, -1)], axis=1))


def make_rope():
    n = TLAT
    row = (np.arange(n) // 64).astype(np.float32)
    col = (np.arange(n) % 64).astype(np.float32)
    inv = (10000.0 ** (-np.arange(16, dtype=np.float32) / 16)).astype(np.float32)
    tab = np.zeros((64, 2, n), np.float32)
    for d in range(64):
        blk, half, f = d // 32, (d % 32) // 16, d % 16
        pos = row if blk == 0 else col
        ang = (pos * inv[f]).astype(np.float32)
        tab[d, 0] = np.cos(ang).astype(np.float32)
        tab[d, 1] = np.sin(ang).astype(np.float32) * (-1.0 if half == 0 else 1.0)
    return np.ascontiguousarray(np.concatenate([tab, tab], axis=0))


def tiles_for(include_ctx=True):
    tl = []
    if include_ctx:
        tl.append((0, TCTX, 1))
    for k in range(TLAT // 512):
        tl.append((TCTX + 512 * k, 512, 0))
    return tl


class Builder:
    def __init__(self, cfg):
        self.cfg = cfg
        self.nc = bass.Bass("TRN2", target_bir_lowering=False)
        self.es = ExitStack()
        self.P = Prog(self.nc, self.es)
        self.dram = {}

    def din(self, name, shape, dtype=F32):
        self.dram[name] = self.nc.dram_tensor(name, list(shape), dtype, kind="ExternalInput").ap()
        return self.dram[name]

    def dout(self, name, shape, dtype=F32):
        self.dram[name] = self.nc.dram_tensor(name, list(shape), dtype, kind="ExternalOutput").ap()
        return self.dram[name]

    def dscratch(self, name, shape, dtype=F32):
        kind = "ExternalOutput" if name in self.cfg.get("debug_outs", ()) else "Internal"
        self.dram[name] = self.nc.dram_tensor(name, list(shape), dtype, kind=kind).ap()
        return self.dram[name]

    def setup(self):
        nc, P = self.nc, self.P
        self.din("hin", [D, T])
        self.din("csrc", [128, KC * 2])
        self.din("consts", [128, 128 * len(CONST_ORDER)])
        self.din("w_mod", [DEPTH, D, 6 * D])
        self.din("b_mod", [DEPTH * 48, 128])
        self.din("norm_g", [DEPTH * 2 * KC, 128])
        self.din("final_g", [KC, 128])
        self.din("ffn_w_in", [DEPTH, D, 2 * FFN_H])
        self.din("ffn_w_out", [DEPTH, FFN_H, D])
        self.din("gmlp_w_in", [D, 4096])
        self.din("gmlp_ln_g", [1, 2048])
        self.din("gmlp_ln_b", [1, 2048])
        self.din("gmlp_w_s", [8, 128, 128])
        self.din("gmlp_b_s", [1, 1024])
        self.din("gmlp_w_out", [2048, D])
        self.din("attn_w_o", [D, D])
        self.din("ones2k", [1, 2048])
        self.din("ssd_w_in", [2, D, 5184])
        self.din("ssd_conv_w", [240, 128])
        self.din("ssd_conv_b", [48, 128])
        self.din("ssd_conv_b_row", [2, 3072])
        self.din("ssd_norm_w", [32, 128])
        self.din("ssd_a_log", [4, 32])
        self.din("ssd_dt_bias", [2, 64])
        self.din("ssd_d", [2, 32])
        self.din("onehots", [32, 48 * 128])
        self.dscratch("sz", [2048, T], BF16)
        self.dscratch("xbc", [3072, T], BF16)
        self.dscratch("dtok", [T, 64])
        self.dscratch("yf", [2048, T], BF16)
        self.dscratch("ysum", [2048, T], BF16)
        self.din("attn_w_qkv", [D, 1536])
        self.din("attn_sink", [1, 16])
        self.din("rope", [128, 2, TLAT])
        self.dscratch("qT", [D, T], BF16)
        self.dscratch("kT", [512, T], BF16)
        self.dscratch("vtok", [T, 256], BF16)
        self.din("ssd_w_out", [2, 2048, D])
        self.dscratch("h", [D, T])
        self.dscratch("ymix", [2048, T], BF16)
        self.dout("out", [D, TLAT])

        self.ps = [nc.alloc_psum_tensor(f"ps{i}", [128, 512], F32) for i in range(8)]
        self.pa = Arena(nc, SBUF_BASE)
        pa = self.pa
        self.ident = pa.alloc("ident", [128, 128], F32)
        self.ones_bf = pa.alloc("ones_bf", [128, 128], BF16)
        self.ident_bf = pa.alloc("ident_bf", [128, 128], BF16)
        self.mod = pa.alloc("mod", [128, DEPTH, 48, 2], F32)
        self.bmodT = pa.alloc("bmodT", [128, DEPTH * 48], F32)
        self.normgT = pa.alloc("normgT", [128, DEPTH * 2 * KC], F32)
        self.finalgT = pa.alloc("finalgT", [128, KC], F32)
        self.nscale = pa.alloc("nscale", [128, DEPTH, 2, KC, 2], F32)
        self.convwT = pa.alloc("convwT", [128, 240], F32)
        self.convbT = pa.alloc("convbT", [128, 48], F32)
        self.normwT = pa.alloc("normwT", [128, 32], F32)
        self.one_col = pa.alloc("one_col", [128, 1], F32)
        self.scb = pa.alloc("scb", [128, KC, 2], BF16)
        self.eps6 = pa.alloc("eps6", [128, 1], F32)
        self.eps5 = pa.alloc("eps5", [128, 1], F32)
        P.op("dve", [("memset", dict(ap=self.eps6[:], constant=1e-6)), ("memset", dict(ap=self.eps5[:], constant=1e-5)),
                     ("memset", dict(ap=self.one_col[:], constant=1.0))], writes=["eps"])

        cd = self.dram["consts"]
        P.dma("sp", "const", ("dma_start", dict(out=self.ident[:], in_=cd[:, 0:128])), writes=["ident"])
        P.dma("pool", "constp", [("dma_start", dict(out=self.ident_bf[:], in_=cd[:, 0:128])),
                                 ("dma_start", dict(out=self.ones_bf[:], in_=cd[:, 128:256]))],
              writes=["ident_bf", "ones_bf"])
        self.arena = Arena(nc, pa.top)

    def load_vecT(self, src_ap, n, dst_ap, dst_key, slot):
        P = self.P
        st = self.arena.alloc("vstage", [128, 128], F32)
        key = ("vstage", slot)
        P.dma("sp", f"vst{slot}", ("dma_start", dict(out=st[0:n, :], in_=src_ap)), writes=[key])
        ps = self.ps[slot % 2]
        pk = ("ps", slot % 2)
        P.op("pe", ("matmul", dict(out=ps[:, 0:n], lhsT=st[0:n, :], rhs=self.ident[0:n, 0:n], start=True, stop=True)),
             reads=[key, "ident"], writes=[pk])
        P.op("dve", ("tensor_copy", dict(out=dst_ap, in_=ps[:, 0:n])), reads=[pk], writes=[dst_key])

    def prologue(self):
        nc, P, dr = self.nc, self.P, self.dram
        self.arena.reset()
        self.load_vecT(dr["b_mod"][0:96, :], 96, self.bmodT[:, 0:96], "bmodT", 0)
        self.load_vecT(dr["b_mod"][96:192, :], 96, self.bmodT[:, 96:192], "bmodT", 1)
        self.load_vecT(dr["norm_g"][:, :], 64, self.normgT[:, :], "normgT", 2)
        self.load_vecT(dr["final_g"][:, :], 8, self.finalgT[:, :], "finalgT", 3)
        self.load_vecT(dr["ssd_conv_w"][0:120, :], 120, self.convwT[:, 0:120], "convwT", 4)
        self.load_vecT(dr["ssd_conv_w"][120:240, :], 120, self.convwT[:, 120:240], "convwT", 5)
        self.load_vecT(dr["ssd_conv_b"][:, :], 48, self.convbT[:, :], "convbT", 6)
        self.load_vecT(dr["ssd_norm_w"][:, :], 32, self.normwT[:, :], "normwT", 7)
        cs = self.arena.alloc("cs", [128, KC * 2], F32)
        P.dma("sp", "cs", ("dma_start", dict(out=cs[:], in_=dr["csrc"][:, :])), writes=["cs"])
        P.op("act", ("activation", dict(out=self.scb[:].rearrange("p k j -> p (k j)"), in_=cs[:], func=AF.Silu)),
             reads=["cs"], writes=["scb"])
        wm = [self.arena.alloc(f"wm{i}", [128, KC, 1536], BF16) for i in range(2)]
        n = 0
        for i in range(DEPTH):
            if i not in self.cfg["layers_mod"]:
                continue
            pm = self.ps[2 + (i % 2)]
            pmk = ("ps", 2 + (i % 2))
            for q in range(4):
                w = wm[n % 2]
                wk = ("wm", n % 2)
                n += 1
                src = dr["w_mod"][i, :, q * 1536:(q + 1) * 1536].rearrange("(kc p) f -> p kc f", p=128)
                P.dma("pool", f"wm{n % 2}", [("dma_start", dict(out=w[:, 0:4, :], in_=src[:, 0:4, :])),
                                            ("dma_start", dict(out=w[:, 4:8, :], in_=src[:, 4:8, :]))], writes=[wk])
                for fc in range(12):
                    col = (q * 12 + fc) * 2
                    calls = []
                    for kc in range(KC):
                        calls.append(("matmul", dict(out=pm[:, col:col + 2], lhsT=w[:, kc, fc * 128:(fc + 1) * 128],
                                                     rhs=self.scb[:, kc, :], start=(kc == 0), stop=(kc == KC - 1))))
                    P.op("pe", calls, reads=[wk, "scb"], writes=[pmk] if (q == 0 and fc == 0) else [(pmk, "part", q, fc)])
            P.op("dve", ("tensor_tensor", dict(out=self.mod[:, i, :, :],
                                               in0=pm[:, 0:96].rearrange("p (f j) -> p f j", j=2),
                                               in1=self.bmodT[:, i * 48:(i + 1) * 48].unsqueeze(2).to_broadcast([128, 48, 2]),
                                               op=ALU.add)),
                 reads=[pmk, "bmodT"] + [(pmk, "part", q, fc) for q in range(4) for fc in range(12)], writes=["mod"])
            for nrm in range(2):
                part = 1 + 3 * nrm
                P.op("dve", ("scalar_tensor_tensor", dict(
                    out=self.nscale[:, i, nrm, :, :], in0=self.mod[:, i, part * 8:(part + 1) * 8, :], scalar=1.0,
                    in1=self.normgT[:, (i * 2 + nrm) * 8:(i * 2 + nrm + 1) * 8].unsqueeze(2).to_broadcast([128, 8, 2]),
                    op0=ALU.add, op1=ALU.mult)), reads=["mod", "normgT"], writes=["nscale"])
        P.barrier()

    def rstd_from(self, ps_ss, pk_ss, rstd, N, n, eps=1e-6, key="rstd"):
        P = self.P
        eps_ap = self.eps6[:, 0:1] if eps == 1e-6 else self.eps5[:, 0:1]
        P.op("act", ("activation", dict(out=rstd[:, :N], in_=ps_ss[:, :N], func=AF.Sqrt, scale=1.0 / n, bias=eps_ap)),
             reads=[pk_ss, "eps"], writes=[key])
        P.op("dve", ("reciprocal", dict(out=rstd[:, :N], in_=rstd[:, :N])), reads=[key], writes=[key])

    def norm_mod(self, h, hk, a, ak, N, i, nrm, j, sq, rstd, tmp, ps_ss, pk_ss, sq_extra_writes=()):
        P = self.P
        shift_part = 0 if nrm == 0 else 3
        P.op("act", ("activation", dict(out=sq[:, :, :N], in_=h[:, :, :N], func=AF.Square)), reads=[hk], writes=["sq"] + list(sq_extra_writes))
        calls = [("matmul", dict(out=ps_ss[:, :N], lhsT=self.ones_bf[:], rhs=sq[:, kc, :N], start=(kc == 0), stop=(kc == KC - 1)))
                 for kc in range(KC)]
        P.op("pe", calls, reads=["sq", "ones_bf"], writes=[pk_ss])
        self.rstd_from(ps_ss, pk_ss, rstd, N, D)
        for kc in range(KC):
            tb = tmp[kc % 2]
            tk = ("tmp", kc % 2)
            P.op("dve", ("scalar_tensor_tensor", dict(out=tb[:, :N], in0=h[:, kc, :N], scalar=self.nscale[:, i, nrm, kc, j:j + 1],
                                                      in1=rstd[:, :N], op0=ALU.mult, op1=ALU.mult)),
                 reads=[hk, "rstd", "nscale"], writes=[tk])
            P.op("act", ("activation", dict(out=a[:, kc, :N], in_=tb[:, :N], func=AF.Identity,
                                            bias=self.mod[:, i, shift_part * 8 + kc, j:j + 1], scale=1.0)),
                 reads=[tk, "mod"], writes=[(ak, kc)])

    def ffn_phase_pipelined(self, i, src, dst, final=False):
        nc, P, dr = self.nc, self.P, self.dram
        ar = self.arena
        ar.reset()
        win = ar.alloc("win", [128, KC, 2 * FFN_H], BF16)
        wout = ar.alloc("wout", [128, FJ, D], BF16)
        hN = ar.alloc("hN", [128, KC, 512], F32)
        X = [ar.alloc(f"X{k}", [128, KC, 512], BF16) for k in range(2)]
        hid = ar.alloc("hid", [128, FJ, 512], BF16)
        rstd = ar.alloc("rstd", [128, 512], F32)
        tmp = [ar.alloc(f"tmp{k}", [128, 512], F32) for k in range(2)]
        sg = [ar.alloc(f"sg{k}", [128, 512], BF16) for k in range(2)]
        hr = [ar.alloc(f"hr{k}", [128, 512], F32) for k in range(2)]
        sqc = [ar.alloc(f"sqc{k}", [128, 512], BF16) for k in range(2)] if final else None
        wsrc = dr["ffn_w_in"][i].rearrange("(kc p) f -> p kc f", p=128)
        for jp in range(FJ // 2):
            P.dma("pool", f"wg{jp}", [("dma_start", dict(out=win[:, :, off + jp * 256:off + (jp + 1) * 256], in_=wsrc[:, :, off + jp * 256:off + (jp + 1) * 256]))
                                      for off in (0, FFN_H)], writes=[("win", jp)])
        wosrc = dr["ffn_w_out"][i].rearrange("(j p) m -> p j m", p=128)
        for jj in range(0, FJ, 2):
            P.dma("pool", "wout", ("dma_start", dict(out=wout[:, jj:jj + 2, :], in_=wosrc[:, jj:jj + 2, :])), writes=[("wout", jj), ("wout", jj + 1)])
        tl = tiles_for(include_ctx=not final)

        def do_norm(ti):
            t0, N, j = tl[ti]
            xb = ti % 2
            P.dma("sp", "hload", ("dma_start", dict(out=hN[:, :, :N], in_=src[:, t0:t0 + N].rearrange("(kc p) t -> p kc t", p=128))),
                  reads=[("dram", src.name, t0)], writes=["hN"])
            ak = ("X", xb)
            self.norm_mod(hN, "hN", X[xb], ak, N, i, 1, j, X[xb], rstd, tmp, self.ps[0], ("ps", 0),
                          sq_extra_writes=[(ak, kc) for kc in range(KC)])

        do_norm(0)
        nh = 0
        for ti, (t0, N, j) in enumerate(tl):
            a = X[ti % 2]
            ak = ("X", ti % 2)
            akeys = [(ak, kc) for kc in range(KC)]
            for jj in range(FJ):
                pg, pgk = self.ps[1 + 2 * (jj % 2)], ("ps", 1 + 2 * (jj % 2))
                pu, puk = self.ps[2 + 2 * (jj % 2)], ("ps", 2 + 2 * (jj % 2))
                calls = [("matmul", dict(out=pg[:, :N], lhsT=win[:, kc, jj * 128:(jj + 1) * 128], rhs=a[:, kc, :N],
                                         start=(kc == 0), stop=(kc == KC - 1))) for kc in range(KC)]
                P.op("pe", calls, reads=akeys + [("win", jj // 2)], writes=[pgk])
                calls = [("matmul", dict(out=pu[:, :N], lhsT=win[:, kc, FFN_H + jj * 128:FFN_H + (jj + 1) * 128], rhs=a[:, kc, :N],
                                         start=(kc == 0), stop=(kc == KC - 1))) for kc in range(KC)]
                P.op("pe", calls, reads=akeys + [("win", jj // 2)], writes=[puk])
                s_ = sg[jj % 2]
                sk = ("sg", jj % 2)
                P.op("act", ("activation", dict(out=s_[:, :N], in_=pg[:, :N], func=AF.Silu)), reads=[pgk], writes=[sk])
                P.op("dve", ("tensor_tensor", dict(out=hid[:, jj, :N], in0=s_[:, :N], in1=pu[:, :N], op=ALU.mult)),
                     reads=[sk, puk], writes=[("hid", jj)])
            if ti + 1 < len(tl):
                do_norm(ti + 1)
            hkeys = [("hid", jj) for jj in range(FJ)]
            wokeys = [("wout", jj) for jj in range(FJ)]
            for mc in range(KC):
                hb_, hbk = hr[nh % 2], ("hr", nh % 2)
                nh += 1
                P.dma("sp", f"hr{(nh - 1) % 2}", ("dma_start", dict(out=hb_[:, :N], in_=src[mc * 128:(mc + 1) * 128, t0:t0 + N])), writes=[hbk])
                po, pok = self.ps[5 + (mc % 2)], ("ps", 5 + (mc % 2))
                calls = [("matmul", dict(out=po[:, :N], lhsT=wout[:, jj, mc * 128:(mc + 1) * 128], rhs=hid[:, jj, :N],
                                         start=(jj == 0), stop=(jj == FJ - 1))) for jj in range(FJ)]
                P.op("pe", calls, reads=hkeys + wokeys, writes=[pok])
                P.op("dve", ("scalar_tensor_tensor", dict(out=hb_[:, :N], in0=po[:, :N], scalar=self.mod[:, i, 5 * 8 + mc, j:j + 1],
                                                          in1=hb_[:, :N], op0=ALU.mult, op1=ALU.add)),
                     reads=[pok, hbk, "mod"], writes=[hbk])
                P.dma("sp", f"hr{(nh - 1) % 2}", ("dma_start", dict(out=dst[mc * 128:(mc + 1) * 128, t0:t0 + N], in_=hb_[:, :N])),
                      reads=[hbk], writes=[("dram", dst.name, t0, mc)])
                if final:
                    sc_, sck = sqc[mc % 2], ("sqc", mc % 2)
                    P.op("act", ("activation", dict(out=sc_[:, :N], in_=hb_[:, :N], func=AF.Square)), reads=[hbk], writes=[sck])
                    P.op("pe", ("matmul", dict(out=self.ps[7][:, :N], lhsT=self.ones_bf[:], rhs=sc_[:, :N], start=(mc == 0), stop=(mc == KC - 1))),
                         reads=[sck, "ones_bf"], writes=[("ps", 7)] if mc in (0, KC - 1) else [("ps", 7, mc)])
            if final:
                self.rstd_from(self.ps[7], ("ps", 7), rstd, N, D)
                for mc in range(KC):
                    hb_, hbk = hr[nh % 2], ("hr", nh % 2)
                    nh += 1
                    P.dma("sp", f"hr{(nh - 1) % 2}", ("dma_start", dict(out=hb_[:, :N], in_=dst[mc * 128:(mc + 1) * 128, t0:t0 + N])),
                          reads=[("dram", dst.name, t0, mc)], writes=[hbk])
                    P.op("dve", ("scalar_tensor_tensor", dict(out=hb_[:, :N], in0=hb_[:, :N], scalar=self.finalgT[:, mc:mc + 1],
                                                              in1=rstd[:, :N], op0=ALU.mult, op1=ALU.mult)),
                         reads=[hbk, "rstd", "finalgT"] + [("ps", 7, m_) for m_ in range(1, KC)], writes=[hbk])
                    P.dma("sp", f"hr{(nh - 1) % 2}", ("dma_start", dict(out=dr["out"][mc * 128:(mc + 1) * 128, t0 - TCTX:t0 - TCTX + N], in_=hb_[:, :N])),
                          reads=[hbk], writes=[("dram", "out", t0, mc)])
        P.barrier()

    def ffn_phase(self, i, src, dst, final=False):
        if not final:
            return self.ffn_phase_pipelined(i, src, dst)
        nc, P, dr = self.nc, self.P, self.dram
        ar = self.arena
        ar.reset()
        win = ar.alloc("win", [128, KC, 2 * FFN_H], BF16)
        wout = ar.alloc("wout", [128, FJ, D], BF16)
        hb = [ar.alloc("h0", [128, KC, 512], F32)]
        ab = [ar.alloc(f"a{k}", [128, KC, 512], BF16) for k in range(1)]
        sq = ar.alloc("sq", [128, KC, 512], BF16)
        hid = ar.alloc("hid", [128, FJ, 512], BF16)
        rstd = ar.alloc("rstd", [128, 512], F32)
        tmp = [ar.alloc(f"tmp{k}", [128, 512], F32) for k in range(2)]
        sg = [ar.alloc(f"sg{k}", [128, 512], F32) for k in range(2)]
        wsrc = dr["ffn_w_in"][i].rearrange("(kc p) f -> p kc f", p=128)
        for jp in range(FJ // 2):
            P.dma("pool", f"wg{jp}", [("dma_start", dict(out=win[:, :, off + jp * 256:off + (jp + 1) * 256], in_=wsrc[:, :, off + jp * 256:off + (jp + 1) * 256]))
                                      for off in (0, FFN_H)], writes=[("win", jp)])
        wosrc = dr["ffn_w_out"][i].rearrange("(j p) m -> p j m", p=128)
        for jj in range(0, FJ, 2):
            P.dma("pool", "wout", ("dma_start", dict(out=wout[:, jj:jj + 2, :], in_=wosrc[:, jj:jj + 2, :])), writes=[("wout", jj), ("wout", jj + 1)])
        tl = tiles_for(include_ctx=not final)
        for ti, (t0, N, j) in enumerate(tl):
            h = hb[0]
            hk = "h0"
            a = ab[0]
            ak = ("a", 0)
            hsrc = src[:, t0:t0 + N].rearrange("(kc p) t -> p kc t", p=128)
            P.dma("sp", "hload", ("dma_start", dict(out=h[:, :, :N], in_=hsrc)), reads=[("dram", src.name, t0)], writes=[hk])
            self.norm_mod(h, hk, a, ak, N, i, 1, j, sq, rstd, tmp, self.ps[0], ("ps", 0))
            akeys = [(ak, kc) for kc in range(KC)]
            for jj in range(FJ):
                pg, pgk = self.ps[1 + 2 * (jj % 2)], ("ps", 1 + 2 * (jj % 2))
                pu, puk = self.ps[2 + 2 * (jj % 2)], ("ps", 2 + 2 * (jj % 2))
                calls = [("matmul", dict(out=pg[:, :N], lhsT=win[:, kc, jj * 128:(jj + 1) * 128], rhs=a[:, kc, :N],
                                         start=(kc == 0), stop=(kc == KC - 1))) for kc in range(KC)]
                P.op("pe", calls, reads=akeys + [("win", jj // 2)], writes=[pgk])
                calls = [("matmul", dict(out=pu[:, :N], lhsT=win[:, kc, FFN_H + jj * 128:FFN_H + (jj + 1) * 128], rhs=a[:, kc, :N],
                                         start=(kc == 0), stop=(kc == KC - 1))) for kc in range(KC)]
                P.op("pe", calls, reads=akeys + [("win", jj // 2)], writes=[puk])
                s = sg[jj % 2]
                sk = ("sg", jj % 2)
                P.op("act", ("activation", dict(out=s[:, :N], in_=pg[:, :N], func=AF.Silu)), reads=[pgk], writes=[sk])
                P.op("dve", ("tensor_tensor", dict(out=hid[:, jj, :N], in0=s[:, :N], in1=pu[:, :N], op=ALU.mult)),
                     reads=[sk, puk], writes=[("hid", jj)])
            hkeys = [("hid", jj) for jj in range(FJ)]
            wokeys = [("wout", jj) for jj in range(FJ)]
            for mc in range(KC):
                po, pok = self.ps[5 + (mc % 2)], ("ps", 5 + (mc % 2))
                calls = [("matmul", dict(out=po[:, :N], lhsT=wout[:, jj, mc * 128:(mc + 1) * 128], rhs=hid[:, jj, :N],
                                         start=(jj == 0), stop=(jj == FJ - 1))) for jj in range(FJ)]
                P.op("pe", calls, reads=hkeys + wokeys, writes=[pok])
                P.op("dve", ("scalar_tensor_tensor", dict(out=h[:, mc, :N], in0=po[:, :N], scalar=self.mod[:, i, 5 * 8 + mc, j:j + 1],
                                                          in1=h[:, mc, :N], op0=ALU.mult, op1=ALU.add)),
                     reads=[pok, hk, "mod"], writes=[hk])
            if not final:
                hdst = dst[:, t0:t0 + N].rearrange("(kc p) t -> p kc t", p=128)
                P.dma("sp", "hstore", ("dma_start", dict(out=hdst, in_=h[:, :, :N])), reads=[hk], writes=[("dram", dst.name, t0)])
            else:
                P.op("act", ("activation", dict(out=sq[:, :, :N], in_=h[:, :, :N], func=AF.Square)), reads=[hk], writes=["sq"])
                calls = [("matmul", dict(out=self.ps[0][:, :N], lhsT=self.ones_bf[:], rhs=sq[:, kc, :N], start=(kc == 0), stop=(kc == KC - 1)))
                         for kc in range(KC)]
                P.op("pe", calls, reads=["sq", "ones_bf"], writes=[("ps", 0)])
                self.rstd_from(self.ps[0], ("ps", 0), rstd, N, D)
                for kc in range(KC):
                    P.op("dve", ("scalar_tensor_tensor", dict(out=h[:, kc, :N], in0=h[:, kc, :N], scalar=self.finalgT[:, kc:kc + 1],
                                                              in1=rstd[:, :N], op0=ALU.mult, op1=ALU.mult)),
                         reads=[hk, "rstd", "finalgT"], writes=[hk])
                odst = dr["out"][:, t0 - TCTX:t0 - TCTX + N].rearrange("(kc p) t -> p kc t", p=128)
                P.dma("sp", "ostore", ("dma_start", dict(out=odst, in_=h[:, :, :N])), reads=[hk], writes=[("dram", "out", t0)])
        P.barrier()

    def gmlp_phase(self, i, src, include_ctx=True):
        nc, P, dr = self.nc, self.P, self.dram
        ar = self.arena
        ar.reset()
        win = ar.alloc("gwin", [128, KC, 4096], BF16)
        wsT = ar.alloc("wsT", [128, 8, 128], BF16)
        gbc = ar.alloc("gbc", [128, 2048], F32)
        lhs2 = ar.alloc("lhs2", [2, 2048], BF16)
        rhs2 = ar.alloc("rhs2", [2, 8, 128], BF16)
        wsld = ar.alloc("wsld", [128, 8, 128], F32)
        hb = ar.alloc("h0", [128, KC, 512], F32)
        X = [ar.alloc(f"X{k}", [128, KC, 512], BF16) for k in range(2)]
        rstd = ar.alloc("rstd", [128, 512], F32)
        tmp = [ar.alloc(f"tmp{k}", [128, 512], F32) for k in range(2)]
        u = ar.alloc("u", [128, 16, 512], BF16)
        vt = [ar.alloc(f"vt{k}", [128, 2048], F32) for k in range(2)]
        vng = ar.alloc("vng", [128, 4, 2048], BF16)
        m = ar.alloc("m", [128, 16, 512], BF16)
        stats = ar.alloc("stats", [128, 4, 6], F32)
        mv = ar.alloc("mv", [128, 2], F32)
        lrs = ar.alloc("lrs", [128, 1], F32)
        wsrc = dr["gmlp_w_in"].rearrange("(kc p) f -> p kc f", p=128)
        for cg in range(8):
            P.dma("pool", f"wg{cg}", ("dma_start", dict(out=win[:, :, cg * 512:(cg + 1) * 512], in_=wsrc[:, :, cg * 512:(cg + 1) * 512])), writes=[("win", cg)])
        winkeys = [("win", cg) for cg in range(8)]
        P.dma("sp", "gsm", [("dma_start", dict(out=wsld[:], in_=dr["gmlp_w_s"].rearrange("g t s -> t g s"))),
                            ("dma_start", dict(out=gbc[:], in_=dr["gmlp_ln_g"].partition_broadcast(128)))],
              writes=["wsld", "gbc"])
        P.dma("pool", "gsm2", [("dma_start", dict(out=lhs2[0:1, :], in_=dr["gmlp_ln_b"][:, :])),
                               ("dma_start", dict(out=lhs2[1:2, :], in_=dr["ones2k"][:, :])),
                               ("dma_start", dict(out=rhs2[1:2, :, :], in_=dr["gmlp_b_s"].rearrange("o (g t) -> o g t", g=8)))],
              writes=["lhs2", "rhs2b"])
        for g in range(8):
            ps, pk = self.ps[g % 2], ("ps", g % 2)
            P.op("pe", ("matmul", dict(out=ps[:, 0:128], lhsT=wsld[:, g, :], rhs=self.ident[:], start=True, stop=True)),
                 reads=["wsld", "ident"], writes=[pk])
            P.op("dve", ("tensor_copy", dict(out=wsT[:, g, :], in_=ps[:, 0:128])), reads=[pk], writes=[("wsT", g)])
        for g in range(8):
            ps, pk = self.ps[2 + g % 2], ("ps", 2 + g % 2)
            P.op("pe", ("matmul", dict(out=ps[0:1, 0:128], lhsT=self.ones_bf[:, 0:1], rhs=wsT[:, g, :], start=True, stop=True)),
                 reads=[("wsT", g), "ones_bf"], writes=[pk])
            P.op("dve", ("tensor_copy", dict(out=rhs2[0:1, g, :], in_=ps[0:1, 0:128])), reads=[pk], writes=[("rhs2a", g)])
        wskeys = [("wsT", g) for g in range(8)] + [("rhs2a", g) for g in range(8)] + ["rhs2b", "lhs2"]
        ymix = dr["ymix"]
        tl_ = tiles_for(include_ctx)

        def do_norm(ti):
            t0, N, j = tl_[ti]
            P.dma("sp", "hload", ("dma_start", dict(out=hb[:, :, :N], in_=src[:, t0:t0 + N].rearrange("(kc p) t -> p kc t", p=128))),
                  reads=[("dram", src.name, t0)], writes=["h0"])
            ak_ = ("X", ti % 2)
            self.norm_mod(hb, "h0", X[ti % 2], ak_, N, i, 0, j, X[ti % 2], rstd, tmp, self.ps[0], ("ps", 0),
                          sq_extra_writes=[(ak_, kc) for kc in range(KC)])

        do_norm(0)
        for ti, (t0, N, j) in enumerate(tl_):
            a = X[ti % 2]
            akeys = [(("X", ti % 2), kc) for kc in range(KC)]
            nsub = N // 128
            for dc in range(16):
                ps, pk = self.ps[1 + dc % 2], ("ps", 1 + dc % 2)
                calls = [("matmul", dict(out=ps[:, :N], lhsT=win[:, kc, dc * 128:(dc + 1) * 128], rhs=a[:, kc, :N],
                                         start=(kc == 0), stop=(kc == KC - 1))) for kc in range(KC)]
                P.op("pe", calls, reads=akeys + winkeys, writes=[pk])
                P.op("act", ("activation", dict(out=u[:, dc, :N], in_=ps[:, :N], func=AF.Gelu)), reads=[pk], writes=[("u", dc)])
            for sub in range(nsub):
                v = vt[sub % 2]
                vk = ("vt", sub % 2)
                for cb in range(4):
                    ps, pk = self.ps[3 + cb % 2], ("ps", 3 + cb % 2)
                    calls = [("matmul", dict(out=ps[:, :], lhsT=a[:, kc, sub * 128:(sub + 1) * 128],
                                             rhs=win[:, kc, 2048 + cb * 512:2048 + (cb + 1) * 512],
                                             start=(kc == 0), stop=(kc == KC - 1))) for kc in range(KC)]
                    P.op("pe", calls, reads=akeys + winkeys, writes=[pk])
                    P.op("act", ("activation", dict(out=v[:, cb * 512:(cb + 1) * 512], in_=ps[:, :], func=AF.Gelu)),
                         reads=[pk], writes=[(vk, cb)])
                    P.op("dve", ("bn_stats", dict(out=stats[:, cb, :], in_=v[:, cb * 512:(cb + 1) * 512])), reads=[(vk, cb)], writes=[("stats", cb)])
                P.op("dve", ("bn_aggr", dict(out=mv[:], in_=stats[:].rearrange("p a b -> p (a b)"))),
                     reads=[("stats", cb) for cb in range(4)], writes=["mv"])
                P.op("act", ("activation", dict(out=lrs[:], in_=mv[:, 1:2], func=AF.Sqrt, scale=1.0, bias=self.eps5[:, 0:1])),
                     reads=["mv", "eps"], writes=["lrs"])
                P.op("dve", ("reciprocal", dict(out=lrs[:], in_=lrs[:])), reads=["lrs"], writes=["lrs"])
                P.op("dve", ("tensor_scalar", dict(out=v[:], in0=v[:], scalar1=mv[:, 0:1], scalar2=lrs[:, 0:1], op0=ALU.subtract, op1=ALU.mult)),
                     reads=[(vk, cb) for cb in range(4)] + ["mv", "lrs"], writes=[(vk, cb) for cb in range(4)])
                P.op("pool", ("tensor_tensor", dict(out=vng[:, sub, :], in0=v[:], in1=gbc[:], op=ALU.mult)),
                     reads=[(vk, cb) for cb in range(4)] + ["gbc"], writes=[("vng", sub)])
            if ti + 1 < len(tl_):
                do_norm(ti + 1)
            vkeys = [("vng", sub) for sub in range(nsub)]
            for dc in range(16):
                g = dc // 2
                ps, pk = self.ps[5 + dc % 2], ("ps", 5 + dc % 2)
                calls = []
                for sub in range(nsub):
                    calls.append(("matmul", dict(out=ps[:, sub * 128:(sub + 1) * 128], lhsT=vng[:, sub, dc * 128:(dc + 1) * 128],
                                                 rhs=wsT[:, g, :], start=True, stop=False)))
                    calls.append(("matmul", dict(out=ps[:, sub * 128:(sub + 1) * 128], lhsT=lhs2[:, dc * 128:(dc + 1) * 128],
                                                 rhs=rhs2[:, g, :], start=False, stop=True)))
                P.op("pe", calls, reads=vkeys + wskeys, writes=[pk])
                P.op("dve", ("tensor_tensor", dict(out=m[:, dc, :N], in0=u[:, dc, :N], in1=ps[:, :N], op=ALU.mult)),
                     reads=[pk, ("u", dc)], writes=[("m", dc)])
            ydst = ymix[:, t0:t0 + N].rearrange("(kc p) t -> p kc t", p=128)
            P.dma("sp", "ystore", ("dma_start", dict(out=ydst, in_=m[:, :, :N])), reads=[("m", dc) for dc in range(16)],
                  writes=[("dram", "ymix", t0)])
        P.barrier()

    def ssd_proj_phase(self, i, src):
        nc, P, dr = self.nc, self.P, self.dram
        js = i // 3
        ar = self.arena
        ar.reset()
        win = ar.alloc("swin", [128, KC, 5184], BF16)
        hb = ar.alloc("h0", [128, KC, 512], F32)
        X = [ar.alloc(f"X{k}", [128, KC, 512], BF16) for k in range(2)]
        rstd = ar.alloc("rstd", [128, 512], F32)
        tmp = [ar.alloc(f"tmp{k}", [128, 512], F32) for k in range(2)]
        zsb = ar.alloc("zsb", [128, 16, 512], BF16)
        xpre = ar.alloc("xpre", [128, 24, 518], BF16)
        diag = ar.alloc("diag", [128, 24, 5, 128], BF16)
        xo = [ar.alloc(f"xo{k}", [128, 514], BF16) for k in range(4)]
        dtb = ar.alloc("dtb", [128, 64], F32)
        dx = [ar.alloc(f"dx{k}", [128, 4, 64], F32) for k in range(5)]
        wsrc = dr["ssd_w_in"][js].rearrange("(kc p) f -> p kc f", p=128)
        for cg in range(10):
            P.dma("pool", f"wg{cg}", ("dma_start", dict(out=win[:, :, cg * 512:(cg + 1) * 512], in_=wsrc[:, :, cg * 512:(cg + 1) * 512])), writes=[("win", cg)])
        P.dma("pool", "wg10", ("dma_start", dict(out=win[:, :, 5120:5184], in_=wsrc[:, :, 5120:5184])), writes=[("win", 10)])
        winkeys = [("win", 10)]
        P.dma("sp", "dtb", ("dma_start", dict(out=dtb[:], in_=dr["ssd_dt_bias"][js:js + 1, :].partition_broadcast(128))), writes=["dtb"])
        for ch in range(24):
            calls = [("tensor_scalar", dict(out=diag[:, ch, tap, :], in0=self.ident[:], scalar1=self.convwT[:, js * 120 + tap * 24 + ch:js * 120 + tap * 24 + ch + 1],
                                            scalar2=None, op0=ALU.mult)) for tap in range(5)]
            P.op("dve", calls, reads=["ident", "convwT"], writes=[("diag", ch)])
        no = 0
        tl_ = tiles_for(True)

        def do_norm(ti):
            t0, N, j = tl_[ti]
            P.dma("sp", "hload", ("dma_start", dict(out=hb[:, :, :N], in_=src[:, t0:t0 + N].rearrange("(kc p) t -> p kc t", p=128))),
                  reads=[("dram", src.name, t0)], writes=["h0"])
            ak_ = ("X", ti % 2)
            self.norm_mod(hb, "h0", X[ti % 2], ak_, N, i, 0, j, X[ti % 2], rstd, tmp, self.ps[0], ("ps", 0),
                          sq_extra_writes=[(ak_, kc) for kc in range(KC)])

        do_norm(0)
        for ti, (t0, N, j) in enumerate(tl_):
            a = X[ti % 2]
            akeys = [(("X", ti % 2), kc) for kc in range(KC)]
            nsub = N // 128
            s0, s1 = (0, TCTX) if j == 1 else (TCTX, T)
            first, last = (t0 == s0), (t0 + N == s1)
            if first:
                P.op("dve", ("memset", dict(ap=xpre[:, :, 0:4], constant=0.0)), writes=["xpre_h"])
            if last:
                P.op("dve", ("memset", dict(ap=xpre[:, :, 4 + N:6 + N], constant=0.0)), writes=["xpre_t"])
            for c in range(40):
                ps, pk = self.ps[1 + c % 4], ("ps", 1 + c % 4)
                calls = [("matmul", dict(out=ps[:, :N], lhsT=win[:, kc, c * 128:(c + 1) * 128], rhs=a[:, kc, :N],
                                         start=(kc == 0), stop=(kc == KC - 1))) for kc in range(KC)]
                P.op("pe", calls, reads=akeys + [("win", c // 4)], writes=[pk])
                if c < 16:
                    P.op("act", ("activation", dict(out=zsb[:, c, :N], in_=ps[:, :N], func=AF.Silu)), reads=[pk], writes=[("zsb", c)])
                else:
                    P.op("dve", ("tensor_copy", dict(out=xpre[:, c - 16, 4:4 + N], in_=ps[:, :N])), reads=[pk], writes=[("xpre", c - 16)])
            pd, pdk = self.ps[5], ("ps", 5)
            for sub in range(nsub):
                calls = [("matmul", dict(out=pd[:, sub * 64:(sub + 1) * 64], lhsT=a[:, kc, sub * 128:(sub + 1) * 128], rhs=win[:, kc, 5120:5184],
                                         start=(kc == 0), stop=(kc == KC - 1))) for kc in range(KC)]
                P.op("pe", calls, reads=akeys + winkeys, writes=[pdk] if sub == 0 else [(pdk, sub)])
            pdr = [pdk] + [(pdk, sub) for sub in range(1, nsub)]
            xb, nx, ab, ee, rr = dx
            P.op("dve", [("tensor_tensor", dict(out=xb[:, :nsub, :], in0=pd[:, 0:nsub * 64].rearrange("p (s c) -> p s c", c=64),
                                               in1=dtb[:].unsqueeze(1).to_broadcast([128, nsub, 64]), op=ALU.add)),
                         ("tensor_scalar", dict(out=nx[:, :nsub, :], in0=xb[:, :nsub, :], scalar1=-1.0, scalar2=None, op0=ALU.mult)),
                         ("tensor_tensor", dict(out=ab[:, :nsub, :], in0=xb[:, :nsub, :], in1=nx[:, :nsub, :], op=ALU.max)),
                         ("tensor_scalar", dict(out=rr[:, :nsub, :], in0=xb[:, :nsub, :], scalar1=0.0, scalar2=None, op0=ALU.max))],
                 reads=pdr + ["dtb"], writes=["dx", "dxr"])
            P.op("act", [("activation", dict(out=ee[:, :nsub, :], in_=ab[:, :nsub, :], func=AF.Exp, scale=-1.0)),
                         ("activation", dict(out=ee[:, :nsub, :], in_=ee[:, :nsub, :], func=AF.Ln, bias=self.one_col[:, 0:1], scale=1.0))],
                 reads=["dx", "eps"], writes=["dxe"])
            P.op("dve", ("tensor_tensor", dict(out=rr[:, :nsub, :], in0=rr[:, :nsub, :], in1=ee[:, :nsub, :], op=ALU.add)),
                 reads=["dx", "dxe"], writes=["dxr"])
            P.dma("sp", "dtst", ("dma_start", dict(out=dr["dtok"][t0:t0 + N, :].rearrange("(s p) c -> p s c", p=128), in_=rr[:, :nsub, :])),
                  reads=["dxr"], writes=[("dram", "dtok", t0)])
            P.dma("sp", "zst", ("dma_start", dict(out=dr["sz"][:, t0:t0 + N].rearrange("(c p) t -> p c t", p=128), in_=zsb[:, :, :N])),
                  reads=[("zsb", c) for c in range(16)], writes=[("dram", "sz", t0)])
            if ti + 1 < len(tl_):
                do_norm(ti + 1)
            p0 = 2 if first else 0
            p1 = N + 2 if last else N
            groups = [(p0, p1)] if p1 - p0 <= 512 else [(p0, 512), (512, p1)]
            tok0 = t0 - 2 + p0
            for ch in range(24):
                o_, ok_ = xo[no % 4], ("xo", no % 4)
                no += 1
                for gi_, (pa_, pb_) in enumerate(groups):
                    ps, pk = self.ps[6 + (ch + gi_) % 2], ("ps", 6 + (ch + gi_) % 2)
                    calls = [("matmul", dict(out=ps[:, 0:pb_ - pa_], lhsT=diag[:, ch, tap, :], rhs=xpre[:, ch, pa_ + tap:pb_ + tap],
                                             start=(tap == 0), stop=(tap == 4))) for tap in range(5)]
                    P.op("pe", calls, reads=[("xpre", ch), "xpre_h", "xpre_t", ("diag", ch)], writes=[pk])
                    P.op("act", ("activation", dict(out=o_[:, pa_ - p0:pb_ - p0], in_=ps[:, 0:pb_ - pa_], func=AF.Silu,
                                                    bias=self.convbT[:, js * 24 + ch:js * 24 + ch + 1], scale=1.0)),
                         reads=[pk, "convbT"], writes=[ok_ if gi_ == 0 else (ok_, gi_)])
                P.dma("sp", f"xo{(no - 1) % 4}", ("dma_start", dict(out=dr["xbc"][ch * 128:(ch + 1) * 128, tok0:tok0 + (p1 - p0)], in_=o_[:, 0:p1 - p0])),
                      reads=[ok_] + [(ok_, g_) for g_ in range(1, len(groups))], writes=[("dram", "xbc", ti, ch)])
            if not last:
                P.op("dve", ("tensor_copy", dict(out=xpre[:, :, 0:4], in_=xpre[:, :, N:N + 4])), reads=[("xpre", ch) for ch in range(24)], writes=["xpre_h"])
        P.barrier()

    def ssd_scan_phase(self, i, d, store_ctx=True):
        nc, P, dr = self.nc, self.P, self.dram
        js = i // 3
        fwd = (d == 0)
        ar = self.arena
        ar.reset()
        cd = dr["consts"]
        ci = {n: k for k, n in enumerate(CONST_ORDER)}
        def cslice(n):
            return cd[:, ci[n] * 128:(ci[n] + 1) * 128]
        tri = ar.alloc("tri", [128, 128], F32)
        Um = ar.alloc("Um", [128, 128], F32)
        Ubf = ar.alloc("Ubf", [128, 128], BF16)
        onesf = ar.alloc("onesf", [128, 128], F32)
        A_bc = ar.alloc("A_bc", [128, 32], F32)
        D_bc = ar.alloc("D_bc", [128, 32], F32)
        S = ar.alloc("S", [128, 2048], F32)
        Sbf = ar.alloc("Sbf", [128, 2048], BF16)
        sfx = "f" if fwd else "b"
        P.dma("sp", "sc1", [("dma_start", dict(out=tri[:], in_=cslice("tri_" + sfx))), ("dma_start", dict(out=Um[:], in_=cslice("U_" + sfx))),
                            ("dma_start", dict(out=onesf[:], in_=cslice("ones"))),
                            ("dma_start", dict(out=A_bc[:], in_=dr["ssd_a_log"][2 * js + d:2 * js + d + 1, :].partition_broadcast(128))),
                            ("dma_start", dict(out=D_bc[:], in_=dr["ssd_d"][js:js + 1, :].partition_broadcast(128)))],
              writes=["tri", "Um", "onesf", "A_bc", "D_bc"])
        P.dma("pool", "sc2", ("dma_start", dict(out=Ubf[:], in_=cslice("U_" + sfx))), writes=["Ubf"])
        P.op("act", ("activation", dict(out=A_bc[:], in_=A_bc[:], func=AF.Exp)), reads=["A_bc"], writes=["A_bc"])
        P.op("dve", ("tensor_scalar", dict(out=A_bc[:], in0=A_bc[:], scalar1=-1.0, scalar2=None, op0=ALU.mult)), reads=["A_bc"], writes=["A_bc"])
        P.op("dve", [("memset", dict(ap=S[:], constant=0.0)), ("memset", dict(ap=Sbf[:], constant=0.0))],
             writes=[("S", g) for g in range(4)] + [("Sbf", g) for g in range(4)])
        NB = 2
        xcs4 = [ar.alloc(f"xcs4{k}", [128, 24, 512], BF16) for k in range(2)]
        dtk4 = [ar.alloc(f"dtk4{k}", [128, 4, 64], F32) for k in range(2)]
        xs = [ar.alloc(f"xs{k}", [128, 2048], BF16) for k in range(NB)]
        btok = [ar.alloc(f"btok{k}", [128, 512], BF16) for k in range(NB)]
        atok = [ar.alloc(f"atok{k}", [128, 32], F32) for k in range(NB)]
        R = [ar.alloc(f"R{k}", [128, 32, 128], BF16) for k in range(NB)]
        wend = [ar.alloc(f"wend{k}", [128, 32], F32) for k in range(NB)]
        eL = [ar.alloc(f"eL{k}", [128, 32], F32) for k in range(NB)]
        dtw = [ar.alloc(f"dtw{k}", [128, 32], F32) for k in range(NB)]
        xdt = [ar.alloc(f"xdt{k}", [128, 32, 64], BF16) for k in range(NB)]
        xdtw = [ar.alloc(f"xdtw{k}", [128, 32, 64], BF16) for k in range(NB)]
        cbs = [ar.alloc(f"cbs{k}", [128, 4, 128], BF16) for k in range(NB)]
        dec = [ar.alloc(f"dec{k}", [128, 4, 128], BF16) for k in range(2)]
        Mall = [ar.alloc(f"Mall{k}", [128, 32, 128], BF16) for k in range(NB)]
        E4 = [ar.alloc(f"E4{k}", [128, 512], F32) for k in range(2)]
        t4 = [ar.alloc(f"t4{k}", [128, 512], F32) for k in range(2)]
        yout = [ar.alloc(f"yout{k}", [128, 16, 128], BF16) for k in range(NB)]
        supers = []
        for (s0, s1) in [(0, TCTX), (TCTX, T)]:
            tl = [(t, min(512, s1 - t)) for t in range(s0, s1, 512)]
            if not fwd:
                tl = tl[::-1]
            supers += tl
        dcol = slice(32 * d, 32 * d + 32)
        ydst_name = "yf" if fwd else "ysum"

        def issue_loads(si):
            T0, NN = supers[si]
            k = si % 2
            P.dma("sp", f"xcs4{k}", ("dma_start", dict(out=xcs4[k][:, :, :NN], in_=dr["xbc"][:, T0:T0 + NN].rearrange("(c p) t -> p c t", p=128))),
                  writes=[("xcs4", k)])
            P.dma("sp", f"dtk4{k}", ("dma_start", dict(out=dtk4[k][:, 0:NN // 128, :], in_=dr["dtok"][T0:T0 + NN, :].rearrange("(s p) c -> p s c", p=128))),
                  reads=[("dram", "dtok", T0)], writes=[("dtk4", k)])

        chunks = []
        for si, (T0, NN) in enumerate(supers):
            subs = list(range(NN // 128))
            if not fwd:
                subs = subs[::-1]
            for sub in subs:
                chunks.append((si, sub, T0 + 128 * sub, T0 < TCTX))

        def xc_of(ci_):
            si, sub, t0, _ = chunks[ci_]
            k4 = si % 2
            return (lambda ch: xcs4[k4][:, ch, sub * 128:(sub + 1) * 128]), ("xcs4", k4), dtk4[k4][:, sub, dcol], ("dtk4", k4)

        def stage_a(ci_):
            b = ci_ % NB
            xc_, xck, dt_, dkk = xc_of(ci_)
            for q in range(5):
                ps, pk = self.ps[q % 2], ("ps", q % 2)
                calls = [("matmul", dict(out=ps[:, cc * 128:(cc + 1) * 128], lhsT=xc_(4 * q + cc), rhs=self.ident_bf[:], start=True, stop=True))
                         for cc in range(4)]
                P.op("pe", calls, reads=[xck, "ident_bf"], writes=[pk])
                if q < 4:
                    P.op("act", ("activation", dict(out=xs[b][:, q * 512:(q + 1) * 512], in_=ps[:, :], func=AF.Copy)), reads=[pk], writes=[("xs", b, q)])
                else:
                    P.op("act", ("activation", dict(out=btok[b][:], in_=ps[:, :], func=AF.Copy)), reads=[pk], writes=[("btok", b)])
            xskeys = [("xs", b, q) for q in range(4)]
            P.op("dve", ("tensor_tensor", dict(out=atok[b][:], in0=dt_, in1=A_bc[:], op=ALU.mult)), reads=[dkk, "A_bc"], writes=[("atok", b)])
            P.op("dve", ("tensor_tensor", dict(out=R[b][:], in0=atok[b][:].unsqueeze(2).to_broadcast([128, 32, 128]),
                                               in1=tri[:].unsqueeze(1).to_broadcast([128, 32, 128]), op=ALU.mult)),
                 reads=[("atok", b), "tri"], writes=[("R", b)])
            pc, pck = self.ps[2], ("ps", 2)
            P.op("pe", [("matmul", dict(out=pc[:, 128:160], lhsT=Um[:], rhs=atok[b][:], start=True, stop=True)),
                        ("matmul", dict(out=pc[:, 160:192], lhsT=onesf[:], rhs=atok[b][:], start=True, stop=True))],
                 reads=[("atok", b), "Um", "onesf"], writes=[pck])
            P.op("act", [("activation", dict(out=wend[b][:], in_=pc[:, 128:160], func=AF.Exp)),
                         ("activation", dict(out=eL[b][:], in_=pc[:, 160:192], func=AF.Exp))], reads=[pck], writes=[("wend", b), ("eL", b)])
            P.op("dve", ("tensor_tensor", dict(out=dtw[b][:], in0=dt_, in1=wend[b][:], op=ALU.mult)), reads=[dkk, ("wend", b)], writes=[("dtw", b)])
            xs3 = xs[b][:].rearrange("p (h q) -> p h q", q=64)
            P.op("dve", ("tensor_tensor", dict(out=xdt[b][:], in0=xs3, in1=dt_.unsqueeze(2).to_broadcast([128, 32, 64]), op=ALU.mult)),
                 reads=xskeys + [dkk], writes=[("xdt", b)])
            P.op("dve", ("tensor_tensor", dict(out=xdtw[b][:], in0=xs3, in1=dtw[b][:].unsqueeze(2).to_broadcast([128, 32, 64]), op=ALU.mult)),
                 reads=xskeys + [("dtw", b)], writes=[("xdtw", b)])
            pcb, pcbk = self.ps[3], ("ps", 3)
            calls = [("matmul", dict(out=pcb[:, g * 128:(g + 1) * 128], lhsT=xc_(16 + g), rhs=xc_(20 + g), start=True, stop=True)) for g in range(4)]
            P.op("pe", calls, reads=[xck], writes=[pcbk])
            P.op("dve", ("tensor_tensor", dict(out=cbs[b][:], in0=pcb[:, :].rearrange("p (a b) -> p a b", b=128),
                                               in1=tri[:].unsqueeze(1).to_broadcast([128, 4, 128]), op=ALU.mult)), reads=[pcbk, "tri"], writes=[("cbs", b)])
            for q in range(8):
                g = q // 2
                h0 = 4 * q
                ps, pk = self.ps[6 + q % 2], ("ps", 6 + q % 2)
                P.op("pe", ("matmul", dict(out=ps[:, :], lhsT=Ubf[:], rhs=R[b][:, h0:h0 + 4, :], start=True, stop=True)), reads=[("R", b), "Ubf"], writes=[pk])
                dq, dqk = dec[q % 2], ("dec", q % 2)
                P.op("act", ("activation", dict(out=dq[:].rearrange("p a b -> p (a b)"), in_=ps[:, :], func=AF.Exp)), reads=[pk], writes=[dqk])
                P.op("dve", ("tensor_tensor", dict(out=Mall[b][:, h0:h0 + 4, :], in0=dq[:], in1=cbs[b][:, g, :].unsqueeze(1).to_broadcast([128, 4, 128]), op=ALU.mult)),
                     reads=[dqk, ("cbs", b)], writes=[("Mall", b, q)])

        def stage_b(ci_):
            b = ci_ % NB
            si, sub, t0, is_ctx = chunks[ci_]
            xc_, xck, dt_, dkk = xc_of(ci_)
            for g in range(4):
                pe_, pek = self.ps[4], ("ps", 4)
                calls = [("matmul", dict(out=pe_[64 * e:64 * e + 64, :], lhsT=self.ones_bf[:, 64 * e:64 * e + 64], rhs=R[b][:, 8 * g + e:8 * g + 8:2, :],
                                         start=True, stop=True)) for e in range(2)]
                P.op("pe", calls, reads=[("R", b), "ones_bf"], writes=[pek])
                e4, e4k = E4[g % 2], ("E4", g % 2)
                P.op("act", ("activation", dict(out=e4[:], in_=pe_[:, :], func=AF.Exp)), reads=[pek], writes=[e4k])
                pg, pgk = self.ps[5], ("ps", 5)
                calls = [("matmul", dict(out=pg[:, hp * 128:(hp + 1) * 128], lhsT=Sbf[:, (4 * g + hp) * 128:(4 * g + hp + 1) * 128], rhs=xc_(20 + g),
                                         start=True, stop=True)) for hp in range(4)]
                P.op("pe", calls, reads=[("Sbf", g), xck], writes=[pgk])
                py, pyk = self.ps[2 + g % 2], ("ps", 2 + g % 2)
                calls = []
                for hp in range(4):
                    for e in range(2):
                        h = 8 * g + 2 * hp + e
                        calls.append(("matmul", dict(out=py[64 * e:64 * e + 64, hp * 128:(hp + 1) * 128], lhsT=xdt[b][:, h, :], rhs=Mall[b][:, h, :],
                                                     start=True, stop=True)))
                P.op("pe", calls, reads=[("xdt", b), ("Mall", b, 2 * g), ("Mall", b, 2 * g + 1)], writes=[pyk])
                tq, tqk = t4[g % 2], ("t4", g % 2)
                P.op("dve", ("tensor_tensor", dict(out=tq[:], in0=pg[:, :], in1=e4[:], op=ALU.mult)), reads=[pgk, e4k], writes=[tqk])
                yo = yout[b][:, 4 * g:4 * g + 4, :]
                P.op("dve", ("tensor_tensor", dict(out=yo, in0=py[:, :].rearrange("p (a b) -> p a b", b=128), in1=tq[:].rearrange("p (a b) -> p a b", b=128), op=ALU.add)),
                     reads=[pyk, tqk], writes=[("yout", b, g)])
            if store_ctx or not is_ctx:
                P.dma("sp", f"yst{b}", ("dma_start", dict(out=dr[ydst_name][:, t0:t0 + 128].rearrange("(c p) t -> p c t", p=128), in_=yout[b][:])),
                      reads=[("yout", b, g) for g in range(4)], writes=[("dram", ydst_name, t0)])
            for g in range(4):
                pS, pSk = self.ps[4 + g % 2], ("ps", 4 + g % 2)
                P.op("pe", ("matmul", dict(out=pS[:, :], lhsT=btok[b][:, g * 128:(g + 1) * 128], rhs=xdtw[b][:, 8 * g:8 * g + 8, :], start=True, stop=True)),
                     reads=[("btok", b), ("xdtw", b)], writes=[pSk])
                Sg = S[:, g * 512:(g + 1) * 512]
                P.op("dve", ("tensor_tensor", dict(out=Sg.rearrange("p (h q) -> p h q", q=64), in0=Sg.rearrange("p (h q) -> p h q", q=64),
                                                    in1=eL[b][:, 8 * g:8 * g + 8].unsqueeze(2).to_broadcast([128, 8, 64]), op=ALU.mult)),
                     reads=[("S", g), ("eL", b), ("Sbf", g)], writes=[("S", g)])
                P.op("dve", ("tensor_tensor", dict(out=Sg, in0=Sg, in1=pS[:, :], op=ALU.add)), reads=[("S", g), pSk], writes=[("S", g)])
                P.op("act", ("activation", dict(out=Sbf[:, g * 512:(g + 1) * 512], in_=Sg, func=AF.Copy)), reads=[("S", g)], writes=[("Sbf", g)])

        issue_loads(0)
        loaded = {0}
        stage_a(0)
        for ci_ in range(len(chunks)):
            for look in (1, 2, 3, 4, 5):
                if ci_ + look < len(chunks):
                    sj = chunks[ci_ + look][0]
                    if sj not in loaded and sj <= chunks[ci_][0] + 1:
                        issue_loads(sj)
                        loaded.add(sj)
            if ci_ + 1 < len(chunks):
                stage_a(ci_ + 1)
            stage_b(ci_)
        P.barrier()

    @staticmethod
    def _tile_of(t):
        if t < TCTX:
            return 0
        return TCTX + ((t - TCTX) // 512) * 512

    def ssd_out_phase(self, i, src, dst, include_ctx=True):
        nc, P, dr = self.nc, self.P, self.dram
        js = i // 3
        ar = self.arena
        ar.reset()
        NT = 256
        wo = ar.alloc("wo", [128, 16, D], BF16)
        yb = ar.alloc("yb", [128, 16, NT], F32)
        yfb = [ar.alloc(f"yfb{k}", [128, 16, NT], BF16) for k in range(2)]
        ybb = [ar.alloc(f"ybb{k}", [128, 16, NT], BF16) for k in range(2)]
        xt = [ar.alloc(f"xt{k}", [128, 16, NT], BF16) for k in range(2)]
        zb = [ar.alloc(f"zb{k}", [128, 16, NT], BF16) for k in range(2)]
        hbs = [ar.alloc(f"h{k}", [128, KC, NT], F32) for k in range(2)]
        sq = ar.alloc("sq", [128, 16, NT], BF16)
        gn = [ar.alloc(f"gn{k}", [128, 16, NT], BF16) for k in range(2)]
        rstd = ar.alloc("rstd", [128, NT], F32)
        D_bc = ar.alloc("D_bc", [128, 32], F32)
        Dpp = ar.alloc("Dpp", [128, 16], F32)
        wsrc = dr["ssd_w_out"][js].rearrange("(k p) m -> p k m", p=128)
        for kk in range(0, 16, 4):
            P.dma("pool", "wo", ("dma_start", dict(out=wo[:, kk:kk + 4, :], in_=wsrc[:, kk:kk + 4, :])), writes=[("wo", kk)])
        wkeys = [("wo", kk) for kk in range(0, 16, 4)]
        P.dma("sp", "dbc", ("dma_start", dict(out=D_bc[:], in_=dr["ssd_d"][js:js + 1, :].partition_broadcast(128))), writes=["D_bc"])
        P.op("dve", [("tensor_copy", dict(out=Dpp[0:64, :], in_=D_bc[0:64, 0:32:2])), ("tensor_copy", dict(out=Dpp[64:128, :], in_=D_bc[64:128, 1:32:2]))],
             reads=["D_bc"], writes=["Dpp"])
        tl = []
        if include_ctx:
            tl.append((0, NT, 1))
        tl += [(TCTX + NT * k, NT, 0) for k in range(TLAT // NT)]

        def loads(ti):
            t0, N, j = tl[ti]
            k = ti % 2
            ykeys = [("dram", "ysum", tt) for tt in range(t0, t0 + N, 128)] + [("dram", "yf", tt) for tt in range(t0, t0 + N, 128)]
            P.dma("sp", f"yload{k}", [("dma_start", dict(out=ybb[k][:, :, :N], in_=dr["ysum"][:, t0:t0 + N].rearrange("(c p) t -> p c t", p=128))),
                                      ("dma_start", dict(out=yfb[k][:, :, :N], in_=dr["yf"][:, t0:t0 + N].rearrange("(c p) t -> p c t", p=128)))],
                  reads=ykeys, writes=[("ybb", k), ("yfb", k)])
            P.dma("sp", f"zload{k}", [("dma_start", dict(out=xt[k][:, :, :N], in_=dr["xbc"][0:2048, t0:t0 + N].rearrange("(c p) t -> p c t", p=128))),
                                      ("dma_start", dict(out=zb[k][:, :, :N], in_=dr["sz"][:, t0:t0 + N].rearrange("(c p) t -> p c t", p=128)))],
                  writes=[("zb", k), ("xt", k)])
            P.dma("sp", f"hload{k}", ("dma_start", dict(out=hbs[k][:, :, :N], in_=src[:, t0:t0 + N].rearrange("(kc p) t -> p kc t", p=128))),
                  writes=[("h", k)])

        loads(0)
        for ti, (t0, N, j) in enumerate(tl):
            k = ti % 2
            hb, hk = hbs[k], ("h", k)
            if ti + 1 < len(tl):
                loads(ti + 1)
            for half in range(2):
                cs = slice(8 * half, 8 * half + 8)
                P.op("dve", ("tensor_tensor", dict(out=yb[:, cs, :N], in0=yfb[k][:, cs, :N], in1=ybb[k][:, cs, :N], op=ALU.add)),
                     reads=[("yfb", k), ("ybb", k)], writes=["yb"])
            for c in range(16):
                P.op("dve", ("scalar_tensor_tensor", dict(out=yb[:, c, :N], in0=xt[k][:, c, :N], scalar=Dpp[:, c:c + 1], in1=yb[:, c, :N],
                                                          op0=ALU.mult, op1=ALU.add)), reads=["yb", ("xt", k), "Dpp"], writes=["yb"])
            for half in range(2):
                cs = slice(8 * half, 8 * half + 8)
                P.op("dve", ("tensor_tensor", dict(out=yb[:, cs, :N], in0=yb[:, cs, :N], in1=zb[k][:, cs, :N], op=ALU.mult)), reads=["yb", ("zb", k)], writes=["yb"])
                P.op("act", ("activation", dict(out=sq[:, cs, :N], in_=yb[:, cs, :N], func=AF.Square)), reads=["yb"], writes=[("sq", half)])
            calls = [("matmul", dict(out=self.ps[0][:, :N], lhsT=self.ones_bf[:], rhs=sq[:, c, :N], start=(c == 0), stop=(c == 15))) for c in range(16)]
            P.op("pe", calls, reads=[("sq", 0), ("sq", 1), "ones_bf"], writes=[("ps", 0)])
            self.rstd_from(self.ps[0], ("ps", 0), rstd, N, 2048)
            g_ = gn[k]
            for c in range(16):
                P.op("dve", ("scalar_tensor_tensor", dict(out=g_[:, c, :N], in0=yb[:, c, :N], scalar=self.normwT[:, js * 16 + c:js * 16 + c + 1],
                                                          in1=rstd[:, :N], op0=ALU.mult, op1=ALU.mult)),
                     reads=["yb", "rstd", "normwT"], writes=[("gn", k, c)])
            gkeys = [("gn", k, c) for c in range(16)]
            for mc in range(KC):
                ps, pk = self.ps[1 + mc % 4], ("ps", 1 + mc % 4)
                calls = [("matmul", dict(out=ps[:, :N], lhsT=wo[:, kk, mc * 128:(mc + 1) * 128], rhs=g_[:, kk, :N],
                                         start=(kk == 0), stop=(kk == 15))) for kk in range(16)]
                P.op("pe", calls, reads=gkeys + wkeys, writes=[pk])
                P.op("dve", ("scalar_tensor_tensor", dict(out=hb[:, mc, :N], in0=ps[:, :N], scalar=self.mod[:, i, 2 * 8 + mc, j:j + 1],
                                                          in1=hb[:, mc, :N], op0=ALU.mult, op1=ALU.add)),
                     reads=[pk, hk, "mod"], writes=[hk])
            P.dma("sp", f"hload{k}", ("dma_start", dict(out=dst[:, t0:t0 + N].rearrange("(kc p) t -> p kc t", p=128), in_=hb[:, :, :N])),
                  reads=[hk], writes=[("dram", dst.name, t0)])
        P.barrier()

    def attn_proj_phase(self, i, src):
        nc, P, dr = self.nc, self.P, self.dram
        ar = self.arena
        ar.reset()
        wq = ar.alloc("wq", [128, KC, 1024], BF16)
        wqs = ar.alloc("wqs", [128, KC, 1024], BF16)
        wk = ar.alloc("wk", [128, KC, 4, 128], BF16)
        wks = ar.alloc("wks", [128, KC, 4, 128], BF16)
        wv = ar.alloc("wv", [128, KC, 256], BF16)
        hb = ar.alloc("h0", [128, KC, 512], F32)
        X = [ar.alloc(f"X{k}", [128, KC, 512], BF16) for k in range(2)]
        rstd = ar.alloc("rstd", [128, 512], F32)
        tmp = [ar.alloc(f"tmp{k}", [128, 512], F32) for k in range(2)]
        rp = [ar.alloc(f"rp{k}", [128, 2, 512], F32) for k in range(2)]
        t1 = [ar.alloc(f"t1{k}", [128, 512], F32) for k in range(2)]
        t2 = [ar.alloc(f"t2{k}", [128, 512], F32) for k in range(2)]
        qsb = [ar.alloc(f"qsb{k}", [128, 8, 512], BF16) for k in range(2)]
        ksb = [ar.alloc(f"ksb{k}", [128, 4, 512], BF16) for k in range(2)]
        vsb = [ar.alloc(f"vsb{k}", [128, 4, 256], BF16) for k in range(2)]
        W = dr["attn_w_qkv"]
        wsrc = W.rearrange("(kc p) f -> p kc f", p=128)
        P.dma("pool", "win", [("dma_start", dict(out=wq[:, 0:4, :], in_=wsrc[:, 0:4, 0:1024])),
                              ("dma_start", dict(out=wq[:, 4:8, :], in_=wsrc[:, 4:8, 0:1024])),
                              ("dma_start", dict(out=wv[:], in_=wsrc[:, :, 1280:1536]))], writes=["wq", "wv"])
        ksrc = wsrc[:, :, 1024:1280].rearrange("p kc (g d) -> p kc g d", g=4)
        P.dma("pool", "wk", [("dma_start", dict(out=wk[:, kc, :, 64 * e:64 * e + 64], in_=ksrc[:, kc, :, :])) for kc in range(KC) for e in range(2)],
              writes=["wk"])
        def sw(t):
            return t.rearrange("p kc (hb half f) -> p kc hb half f", half=2, f=16)
        P.op("act", [("activation", dict(out=sw(wqs[:])[:, :, :, 0, :], in_=sw(wq[:])[:, :, :, 1, :], func=AF.Copy)),
                     ("activation", dict(out=sw(wqs[:])[:, :, :, 1, :], in_=sw(wq[:])[:, :, :, 0, :], func=AF.Copy))],
             reads=["wq"], writes=["wqs"])
        def swk(t):
            return t.rearrange("p kc g (hb half f) -> p (kc g) hb half f", half=2, f=16)
        P.op("act", [("activation", dict(out=swk(wks[:])[:, :, :, 0, :], in_=swk(wk[:])[:, :, :, 1, :], func=AF.Copy)),
                     ("activation", dict(out=swk(wks[:])[:, :, :, 1, :], in_=swk(wk[:])[:, :, :, 0, :], func=AF.Copy))],
             reads=["wk"], writes=["wks"])
        wkeys = ["wq", "wqs", "wk", "wks", "wv"]
        tl_ = tiles_for(True)

        def do_norm(ti):
            t0, N, j = tl_[ti]
            P.dma("sp", "hload", ("dma_start", dict(out=hb[:, :, :N], in_=src[:, t0:t0 + N].rearrange("(kc p) t -> p kc t", p=128))),
                  reads=[("dram", src.name, t0)], writes=["h0"])
            ak_ = ("X", ti % 2)
            self.norm_mod(hb, "h0", X[ti % 2], ak_, N, i, 0, j, X[ti % 2], rstd, tmp, self.ps[0], ("ps", 0),
                          sq_extra_writes=[(ak_, kc) for kc in range(KC)])

        do_norm(0)
        for ti, (t0, N, j) in enumerate(tl_):
            a = X[ti % 2]
            akeys = [(("X", ti % 2), kc) for kc in range(KC)]
            lat = (j == 0)
            r = rp[ti % 2]
            rk = ("rp", ti % 2)
            if lat:
                P.dma("sp", f"rp{ti % 2}", ("dma_start", dict(out=r[:, :, :N], in_=dr["rope"][:, :, t0 - TCTX:t0 - TCTX + N])), writes=[rk])
            qs_, qk_ = qsb[ti % 2], ("qsb", ti % 2)
            ks_, kk_ = ksb[ti % 2], ("ksb", ti % 2)
            vs_, vk_ = vsb[ti % 2], ("vsb", ti % 2)
            n = 0
            for which, nchunk, wpl, wsw, dst_, dk_ in (("q", 8, wq, wqs, qs_, qk_), ("k", 4, wk, wks, ks_, kk_)):
                for c in range(nchunk):
                    pa_, pak = self.ps[1 + 2 * (n % 2)], ("ps", 1 + 2 * (n % 2))
                    pb_, pbk = self.ps[2 + 2 * (n % 2)], ("ps", 2 + 2 * (n % 2))
                    def wsl(w, kc):
                        return w[:, kc, c * 128:(c + 1) * 128] if which == "q" else w[:, kc, c, :]
                    calls = [("matmul", dict(out=pa_[:, :N], lhsT=wsl(wpl, kc), rhs=a[:, kc, :N], start=(kc == 0), stop=(kc == KC - 1)))
                             for kc in range(KC)]
                    P.op("pe", calls, reads=akeys + wkeys, writes=[pak])
                    if lat:
                        calls = [("matmul", dict(out=pb_[:, :N], lhsT=wsl(wsw, kc), rhs=a[:, kc, :N], start=(kc == 0), stop=(kc == KC - 1)))
                                 for kc in range(KC)]
                        P.op("pe", calls, reads=akeys + wkeys, writes=[pbk])
                        x1, x1k = t1[n % 2], ("t1", n % 2)
                        x2, x2k = t2[n % 2], ("t2", n % 2)
                        P.op("dve", ("tensor_tensor", dict(out=x1[:, :N], in0=pa_[:, :N], in1=r[:, 0, :N], op=ALU.mult)), reads=[pak, rk], writes=[x1k])
                        P.op("dve", ("tensor_tensor", dict(out=x2[:, :N], in0=pb_[:, :N], in1=r[:, 1, :N], op=ALU.mult)), reads=[pbk, rk], writes=[x2k])
                        P.op("pool", ("tensor_tensor", dict(out=dst_[:, c, :N], in0=x1[:, :N], in1=x2[:, :N], op=ALU.add)), reads=[x1k, x2k], writes=[dk_])
                    else:
                        P.op("act", ("activation", dict(out=dst_[:, c, :N], in_=pa_[:, :N], func=AF.Copy)), reads=[pak], writes=[dk_])
                    n += 1
            if ti + 1 < len(tl_):
                do_norm(ti + 1)
            for sub in range(N // 128):
                ps, pk = self.ps[5 + sub % 2], ("ps", 5 + sub % 2)
                calls = [("matmul", dict(out=ps[:, 0:256], lhsT=a[:, kc, sub * 128:(sub + 1) * 128], rhs=wv[:, kc, :],
                                         start=(kc == 0), stop=(kc == KC - 1))) for kc in range(KC)]
                P.op("pe", calls, reads=akeys + wkeys, writes=[pk])
                P.op("act", ("activation", dict(out=vs_[:, sub, :], in_=ps[:, 0:256], func=AF.Copy)), reads=[pk], writes=[vk_])
            P.dma("sp", f"qst{ti % 2}", ("dma_start", dict(out=dr["qT"][:, t0:t0 + N].rearrange("(c p) t -> p c t", p=128), in_=qs_[:, :, :N])),
                  reads=[qk_], writes=[("dram", "qT", t0)])
            P.dma("sp", f"kst{ti % 2}", ("dma_start", dict(out=dr["kT"][:, t0:t0 + N].rearrange("(c p) t -> p c t", p=128), in_=ks_[:, :, :N])),
                  reads=[kk_], writes=[("dram", "kT", t0)])
            P.dma("sp", f"vst{ti % 2}", ("dma_start", dict(out=dr["vtok"][t0:t0 + N, :].rearrange("(b p) c -> p b c", p=128), in_=vs_[:, 0:N // 128, :])),
                  reads=[vk_], writes=[("dram", "vtok", t0)])
        P.barrier()

    def attn_core_phase(self):
        nc, P, dr = self.nc, self.P, self.dram
        ar = self.arena
        ar.reset()
        kc_ = ar.alloc("kctx", [128, 4, 256], BF16)
        vc_ = ar.alloc("vctx", [128, 2, 256], BF16)
        msk = ar.alloc("msk", [128, 2, 2, 128], BF16)
        esk = ar.alloc("esk", [128, 16], F32)
        q4 = [ar.alloc(f"q4{k}", [128, 8, 512], BF16) for k in range(2)]
        kb4 = [ar.alloc(f"kb4{k}", [128, 4, 768], BF16) for k in range(2)]
        vb4 = [ar.alloc(f"vb4{k}", [128, 6, 256], BF16) for k in range(2)]
        ot4 = [ar.alloc(f"ot4{k}", [128, 8, 512], BF16) for k in range(2)]
        pt = [ar.alloc(f"pt{k}", [128, 5, 512], BF16) for k in range(2)]
        rden = [ar.alloc(f"rden{k}", [128, 4, 128], F32) for k in range(2)]
        cd = dr["consts"]
        P.dma("pool", "acst", [("dma_start", dict(out=msk[:, 0, 0, :], in_=cd[:, 256:384])), ("dma_start", dict(out=msk[:, 0, 1, :], in_=cd[:, 256:384])),
                               ("dma_start", dict(out=msk[:, 1, 0, :], in_=cd[:, 384:512])), ("dma_start", dict(out=msk[:, 1, 1, :], in_=cd[:, 384:512]))],
              writes=["msk"])
        P.dma("sp", "acst2", [("dma_start", dict(out=esk[:], in_=dr["attn_sink"].partition_broadcast(128))),
                              ("dma_start", dict(out=kc_[:], in_=dr["kT"][:, 0:256].rearrange("(c p) t -> p c t", p=128))),
                              ("dma_start", dict(out=vc_[:], in_=dr["vtok"][0:256, :].rearrange("(b p) c -> p b c", p=128)))],
              reads=[("dram", "kT", 0), ("dram", "vtok", 0)], writes=["esk", "kctx", "vctx"])
        P.op("act", ("activation", dict(out=esk[:], in_=esk[:], func=AF.Exp)), reads=["esk"], writes=["esk"])
        gi = 0
        for ti, (t0, N, j) in enumerate(tiles_for(True)):
            lat = (j == 0)
            q_, qk_ = q4[ti % 2], ("q4", ti % 2)
            kb_, kbk = kb4[ti % 2], ("kb4", ti % 2)
            vb_, vbk = vb4[ti % 2], ("vb4", ti % 2)
            o_, ok_ = ot4[ti % 2], ("ot4", ti % 2)
            P.dma("sp", f"q4{ti % 2}", ("dma_start", dict(out=q_[:, :, :N], in_=dr["qT"][:, t0:t0 + N].rearrange("(c p) t -> p c t", p=128))),
                  reads=[("dram", "qT", t0)], writes=[qk_])
            if lat:
                lo = max(t0 - 128, TCTX)
                hi = min(t0 + 640, T)
                off = lo - (t0 - 128)
                nb0 = off // 128
                nbl = (hi - lo) // 128
                rk = [("dram", "kT", tt) for tt in (t0 - 512, t0, t0 + 512) if TCTX <= tt < T]
                rv = [("dram", "vtok", tt) for tt in (t0 - 512, t0, t0 + 512) if TCTX <= tt < T]
                P.dma("sp", f"kb4{ti % 2}", ("dma_start", dict(out=kb_[:, :, off:off + (hi - lo)], in_=dr["kT"][:, lo:hi].rearrange("(c p) t -> p c t", p=128))),
                      reads=rk, writes=[kbk])
                P.dma("sp", f"vb4{ti % 2}", ("dma_start", dict(out=vb_[:, nb0:nb0 + nbl, :], in_=dr["vtok"][lo:hi, :].rearrange("(b p) c -> p b c", p=128))),
                      reads=rv, writes=[vbk])
            def kblist(qb):
                kbl = [("c", 0, None), ("c", 1, None)]
                if lat:
                    pos = t0 + qb * 128
                    if pos - 128 >= TCTX:
                        kbl.append(("b", qb, 0))
                    kbl.append(("b", qb + 1, None))
                    if pos + 128 < T:
                        kbl.append(("b", qb + 2, 1))
                return kbl

            def scores(qb, g, gi_):
                kbl = kblist(qb)
                p_, pk_ = pt[gi_ % 2], ("pt", gi_ % 2)
                for bi, (kind, bidx, mi) in enumerate(kbl):
                    rds = [qk_, "msk", "ident_bf"] + (["kctx"] if kind == "c" else [kbk])
                    for e in range(2):
                        ps, pk = self.ps[2 * (bi % 2) + e], ("ps", 2 * (bi % 2) + e)
                        calls = []
                        kT_ap = (kc_[64 * e:64 * e + 64, g, bidx * 128:(bidx + 1) * 128] if kind == "c"
                                 else kb_[64 * e:64 * e + 64, g, bidx * 128:(bidx + 1) * 128])
                        calls.append(("matmul", dict(out=ps[:, 0:256], lhsT=kT_ap,
                                                     rhs=q_[64 * e:64 * e + 64, 2 * g:2 * g + 2, qb * 128:(qb + 1) * 128],
                                                     start=True, stop=(mi is None))))
                        if mi is not None:
                            calls.append(("matmul", dict(out=ps[:, 0:256], lhsT=self.ident_bf[:],
                                                         rhs=msk[:, mi, :, :], start=False, stop=True)))
                        P.op("pe", calls, reads=rds, writes=[pk])
                        P.op("act", ("activation", dict(out=p_[:, bi, 256 * e:256 * e + 256], in_=ps[:, 0:256], func=AF.Exp, scale=0.125)),
                             reads=[pk], writes=[(pk_, bi, e)])

            def finish(qb, g, gi_):
                kbl = kblist(qb)
                nkb = len(kbl)
                p_, pk_ = pt[gi_ % 2], ("pt", gi_ % 2)
                pkeys = [(pk_, bi, e) for bi in range(nkb) for e in range(2)]
                pd, pdk = self.ps[4 + gi_ % 2], ("ps", 4 + gi_ % 2)
                calls = [("matmul", dict(out=pd[:, :], lhsT=self.ones_bf[:], rhs=p_[:, bi, :], start=(bi == 0), stop=(bi == nkb - 1)))
                         for bi in range(nkb)]
                P.op("pe", calls, reads=pkeys + ["ones_bf"], writes=[pdk])
                rd, rdk = rden[gi_ % 2], ("rden", gi_ % 2)
                P.op("dve", [("tensor_tensor", dict(out=rd[:], in0=pd[:, :].rearrange("p (h q) -> p h q", h=4),
                                                   in1=esk[:, 4 * g:4 * g + 4].unsqueeze(2).to_broadcast([128, 4, 128]), op=ALU.add)),
                             ("reciprocal", dict(out=rd[:], in_=rd[:]))], reads=[pdk, "esk"], writes=[rdk])
                po, pok = self.ps[6 + gi_ % 2], ("ps", 6 + gi_ % 2)
                calls = []
                for cc in range(2):
                    for e in range(2):
                        colb = e * 2 + cc
                        for bi, (kind, bidx, mi) in enumerate(kbl):
                            v_ap = (vc_[:, bidx, g * 64:(g + 1) * 64] if kind == "c" else vb_[:, bidx, g * 64:(g + 1) * 64])
                            calls.append(("matmul", dict(out=po[64 * e:64 * e + 64, cc * 128:(cc + 1) * 128], lhsT=v_ap,
                                                         rhs=p_[:, bi, colb * 128:(colb + 1) * 128], start=(bi == 0), stop=(bi == nkb - 1))))
                P.op("pe", calls, reads=pkeys + ["vctx", vbk], writes=[pok])
                calls = []
                for cc in range(2):
                    for e in range(2):
                        colb = e * 2 + cc
                        calls.append(("tensor_tensor", dict(out=o_[64 * e:64 * e + 64, 2 * g + cc, qb * 128:(qb + 1) * 128],
                                                            in0=po[64 * e:64 * e + 64, cc * 128:(cc + 1) * 128],
                                                            in1=rd[64 * e:64 * e + 64, colb, :], op=ALU.mult)))
                P.op("dve", calls, reads=[pok, rdk], writes=[(ok_, g, qb)])

            jobs = [(qb, g) for qb in range(N // 128) for g in range(4)]
            scores(jobs[0][0], jobs[0][1], gi)
            for k_, (qb, g) in enumerate(jobs):
                if k_ + 1 < len(jobs):
                    scores(jobs[k_ + 1][0], jobs[k_ + 1][1], gi + 1)
                finish(qb, g, gi)
                gi += 1
            P.dma("sp", f"ot4{ti % 2}", ("dma_start", dict(out=dr["ymix"][0:1024, t0:t0 + N].rearrange("(c p) t -> p c t", p=128), in_=o_[:, :, :N])),
                  reads=[(ok_, g, qb) for g in range(4) for qb in range(N // 128)], writes=[("dram", "ymix", t0)])
        P.barrier()

    def outproj_phase(self, i, src, dst, w_ap, kin, include_ctx=True):
        nc, P, dr = self.nc, self.P, self.dram
        ar = self.arena
        ar.reset()
        nk = kin // 128
        wo = ar.alloc("wo", [128, nk, D], BF16)
        hb = [ar.alloc(f"h{k}", [128, KC, 512], F32) for k in range(2)]
        yb = [ar.alloc(f"y{k}", [128, nk, 512], BF16) for k in range(2)]
        wsrc = w_ap.rearrange("(k p) m -> p k m", p=128)
        for kk in range(0, nk, 4):
            P.dma("pool", "wo", ("dma_start", dict(out=wo[:, kk:kk + 4, :], in_=wsrc[:, kk:kk + 4, :])), writes=[("wo", kk)])
        wkeys = [("wo", kk) for kk in range(0, nk, 4)]
        ymix = dr["ymix"]
        tl_ = tiles_for(include_ctx)

        def loads(ti):
            t0, N, j = tl_[ti]
            P.dma("sp", f"yload{ti % 2}", ("dma_start", dict(out=yb[ti % 2][:, :, :N], in_=ymix[0:kin, t0:t0 + N].rearrange("(k p) t -> p k t", p=128))),
                  reads=[("dram", "ymix", t0)], writes=[("y", ti % 2)])
            P.dma("sp", f"hload{ti % 2}", ("dma_start", dict(out=hb[ti % 2][:, :, :N], in_=src[:, t0:t0 + N].rearrange("(kc p) t -> p kc t", p=128))),
                  reads=[("dram", src.name, t0)], writes=[("h", ti % 2)])

        loads(0)
        for ti, (t0, N, j) in enumerate(tl_):
            h, hk = hb[ti % 2], ("h", ti % 2)
            y, yk = yb[ti % 2], ("y", ti % 2)
            if ti + 1 < len(tl_):
                loads(ti + 1)
            for mc in range(KC):
                ps, pk = self.ps[mc % 4], ("ps", mc % 4)
                calls = [("matmul", dict(out=ps[:, :N], lhsT=wo[:, kk, mc * 128:(mc + 1) * 128], rhs=y[:, kk, :N],
                                         start=(kk == 0), stop=(kk == nk - 1))) for kk in range(nk)]
                P.op("pe", calls, reads=[yk] + wkeys, writes=[pk])
                P.op("dve", ("scalar_tensor_tensor", dict(out=h[:, mc, :N], in0=ps[:, :N], scalar=self.mod[:, i, 2 * 8 + mc, j:j + 1],
                                                          in1=h[:, mc, :N], op0=ALU.mult, op1=ALU.add)),
                     reads=[pk, hk, "mod"], writes=[hk])
            P.dma("sp", f"hstore{ti % 2}", ("dma_start", dict(out=dst[:, t0:t0 + N].rearrange("(kc p) t -> p kc t", p=128), in_=h[:, :, :N])),
                  reads=[hk], writes=[("dram", dst.name, t0)])
        P.barrier()

    def build(self):
        self.setup()
        self.prologue()
        dr = self.dram
        for ph in self.cfg["phases"]:
            kind = ph[0]
            if kind == "ffn":
                _, i, srcn, final = ph
                self.ffn_phase(i, dr[srcn], dr["h"], final=final)
            elif kind == "ssd_proj":
                self.ssd_proj_phase(ph[1], dr[ph[2]])
            elif kind == "ssd_scan":
                self.ssd_scan_phase(ph[1], ph[2], store_ctx=ph[3])
            elif kind == "ssd_out":
                self.ssd_out_phase(ph[1], dr[ph[2]], dr["h"], include_ctx=ph[3])
            elif kind == "attn_proj":
                self.attn_proj_phase(ph[1], dr[ph[2]])
            elif kind == "attn_core":
                self.attn_core_phase()
            elif kind == "gmlp":
                _, i, srcn = ph
                self.gmlp_phase(i, dr[srcn])
            elif kind == "outproj":
                _, i, srcn, wname, kin, inc_ctx = ph
                w_ap = dr[wname] if wname != "ssd_w_out" else dr[wname][i // 3]
                self.outproj_phase(i, dr[srcn], dr["h"], w_ap, kin, include_ctx=inc_ctx)
        self.P.barrier()
        self.P.emit()
        return self.nc


def layer_phases(i, srcn):
    last = i == DEPTH - 1
    kind = i % 3
    ph = []
    if kind == 0:
        ph.append(("ssd_proj", i, srcn))
        ph.append(("ssd_scan", i, 0, True))
        ph.append(("ssd_scan", i, 1, not last))
        ph.append(("ssd_out", i, srcn, not last))
    if kind == 1:
        ph.append(("attn_proj", i, srcn))
        ph.append(("attn_core",))
        ph.append(("outproj", i, srcn, "attn_w_o", 1024, not last))
    if kind == 2:
        ph.append(("gmlp", i, srcn))
        ph.append(("outproj", i, srcn, "gmlp_w_out", 2048, not last))
    ph.append(("ffn", i, "h", last))
    return ph


FULL_CFG = dict(layers_mod=[0, 1, 2, 3],
                phases=[p for i in range(DEPTH) for p in layer_phases(i, "hin" if i == 0 else "h")],
                debug_outs=())


def host_inputs(inputs, b):
    x = np.asarray(inputs["x"][b], dtype=np.float32)
    ctx = np.asarray(inputs["ctx"][b], dtype=np.float32)
    hin = np.ascontiguousarray(np.concatenate([ctx, x], axis=0).T)
    c = np.asarray(inputs["c"][b], dtype=np.float32).reshape(KC, 128)
    cc = np.asarray(inputs["c_ctx"], dtype=np.float32).reshape(KC, 128)
    csrc = np.ascontiguousarray(np.stack([c, cc], axis=-1).transpose(1, 0, 2).reshape(128, KC * 2))
    cst = make_consts()
    consts = np.ascontiguousarray(np.concatenate([cst[k] for k in CONST_ORDER], axis=1))
    m = {
        "hin": hin, "csrc": csrc, "consts": consts,
        "w_mod": np.asarray(inputs["w_mod"], np.float32),
        "b_mod": np.ascontiguousarray(np.asarray(inputs["b_mod"], np.float32).reshape(DEPTH * 48, 128)),
        "norm_g": np.ascontiguousarray(np.asarray(inputs["norm_g"], np.float32).reshape(DEPTH * 2 * KC, 128)),
        "final_g": np.ascontiguousarray(np.asarray(inputs["final_g"], np.float32).reshape(KC, 128)),
        "ffn_w_in": np.asarray(inputs["ffn_w_in"], np.float32),
        "ffn_w_out": np.asarray(inputs["ffn_w_out"], np.float32),
        "gmlp_w_in": np.asarray(inputs["gmlp_w_in"][0], np.float32),
        "gmlp_ln_g": np.asarray(inputs["gmlp_ln_g"], np.float32).reshape(1, 2048),
        "gmlp_ln_b": np.asarray(inputs["gmlp_ln_b"], np.float32).reshape(1, 2048),
        "gmlp_w_s": np.asarray(inputs["gmlp_w_s"][0], np.float32),
        "gmlp_b_s": np.asarray(inputs["gmlp_b_s"], np.float32).reshape(1, 1024),
        "gmlp_w_out": np.asarray(inputs["gmlp_w_out"][0], np.float32),
        "attn_w_o": np.asarray(inputs["attn_w_o"][0], np.float32),
        "ssd_w_out": np.asarray(inputs["ssd_w_out"], np.float32),
        "ones2k": np.ones((1, 2048), np.float32),
        "ssd_w_in": np.asarray(inputs["ssd_w_in"], np.float32),
        "ssd_conv_w": np.ascontiguousarray(np.asarray(inputs["ssd_conv_w"], np.float32).reshape(240, 128)),
        "ssd_conv_b": np.ascontiguousarray(np.asarray(inputs["ssd_conv_b"], np.float32).reshape(48, 128)),
        "ssd_conv_b_row": np.asarray(inputs["ssd_conv_b"], np.float32),
        "ssd_norm_w": np.ascontiguousarray(np.asarray(inputs["ssd_norm_w"], np.float32).reshape(32, 128)),
        "ssd_a_log": np.ascontiguousarray(np.asarray(inputs["ssd_a_log"], np.float32).reshape(4, 32)),
        "ssd_dt_bias": np.ascontiguousarray(np.asarray(inputs["ssd_dt_bias"], np.float32).reshape(2, 64)),
        "ssd_d": np.asarray(inputs["ssd_d"], np.float32),
        "onehots": make_onehots(),
        "attn_w_qkv": np.asarray(inputs["attn_w_qkv"][0], np.float32),
        "attn_sink": np.ascontiguousarray(np.asarray(inputs["attn_sink"], np.float32).reshape(4, 2, 2).transpose(0, 2, 1).reshape(1, 16)),
        "rope": make_rope(),
    }
    return m


def kernel(**inputs):
    nc = Builder(FULL_CFG).build()
    in_maps = [host_inputs(inputs, b) for b in range(8)]
    res = run_bass_kernel_spmd(nc, in_maps, core_ids=list(range(8)))
    out = np.stack([np.ascontiguousarray(res.results[b]["out"].T) for b in range(8)], axis=0)
    return out.astype(np.float32)
```
